# Optimizing a Trainium2 kernel written in Bass

```python
import jax, jax.numpy as jnp
from jax import lax
import numpy as np

D_MODEL = 1024
BATCH = 2
SEQ = 8192
DEPTH = 2

N_MIXERS = 2
N_POOL_LAYERS = (DEPTH + N_MIXERS - 1) // N_MIXERS
N_MLA_LAYERS = DEPTH // N_MIXERS
EXPAND = 2
POOL_WIDTH = EXPAND * D_MODEL
POOL_WINDOWS = (2, 4, 8, 16)
N_POOL_GROUPS = len(POOL_WINDOWS)
POOL_GROUP = POOL_WIDTH // N_POOL_GROUPS
N_HEADS = 16
QK_NOPE_DIM = 128
QK_ROPE_DIM = 64
QK_HEAD_DIM = QK_NOPE_DIM + QK_ROPE_DIM
V_HEAD_DIM = 128
Q_LORA_RANK = 384
KV_LORA_RANK = 256
MLA_WIDTH = N_HEADS * V_HEAD_DIM
MLA_IN_WIDTH = Q_LORA_RANK + KV_LORA_RANK + QK_ROPE_DIM + MLA_WIDTH
ROPE_THETA = 10000.0
Q_BLOCK = 128
EPS = 1e-6

kernel_name = "interleaved_pool_mla_gated_trunk"


def rmsnorm(x, g):
    xf = x.astype(jnp.float32)
    inv = lax.rsqrt(jnp.mean(xf * xf, axis=-1, keepdims=True) + EPS)
    return (xf * inv).astype(x.dtype) * g


def causal_window_mean(u, w):
    S = u.shape[1]
    cs = jnp.cumsum(u.astype(jnp.float32), axis=1)
    shifted = jnp.pad(cs, ((0, 0), (w, 0), (0, 0)))[:, :S]
    count = jnp.minimum(jnp.arange(1, S + 1), w).astype(jnp.float32)
    return ((cs - shifted) / count[None, :, None]).astype(u.dtype)


def pool_layer(x, norm_g, w_in, w_group, scale, w_out):
    B, S, _ = x.shape
    h = rmsnorm(x, norm_g)
    uz = h @ w_in
    u, z = uz[..., :POOL_WIDTH], uz[..., POOL_WIDTH:]
    ug = u.reshape(B, S, N_POOL_GROUPS, POOL_GROUP)
    pooled = jnp.stack([causal_window_mean(ug[:, :, g], w)
                        for g, w in enumerate(POOL_WINDOWS)], axis=2) - ug
    mixed = jnp.einsum('bsgc,gcd->bsgd', pooled, w_group).reshape(B, S, POOL_WIDTH) * scale
    y = mixed * jax.nn.silu(z)
    return x + y @ w_out


def rope_tables(positions, dtype):
    inv_freq = 1.0 / (ROPE_THETA ** (jnp.arange(0, QK_ROPE_DIM, 2, dtype=jnp.float32) / QK_ROPE_DIM))
    ang = positions.astype(jnp.float32)[..., None] * inv_freq
    return jnp.cos(ang).astype(dtype), jnp.sin(ang).astype(dtype)


def apply_rope(t, cos, sin):
    half = QK_ROPE_DIM // 2
    t1, t2 = t[..., :half], t[..., half:]
    return jnp.concatenate([t1 * cos - t2 * sin, t2 * cos + t1 * sin], axis=-1)


def causal_mla_attention(q_nope, q_rope, k_nope, k_rope, v):
    B, S, H, _ = q_nope.shape
    nb = S // Q_BLOCK
    scale = QK_HEAD_DIM ** -0.5
    key_pos = jnp.arange(S)

    def to_blocks(t):
        return t.reshape(B, nb, Q_BLOCK, *t.shape[2:]).swapaxes(0, 1)

    def one_block(args):
        qn, qr, start = args
        s = (jnp.einsum('bqhd,bkhd->bhqk', qn, k_nope)
             + jnp.einsum('bqhr,bkr->bhqk', qr, k_rope)).astype(jnp.float32) * scale
        q_pos = start + jnp.arange(Q_BLOCK)
        mask = q_pos[:, None] >= key_pos[None, :]
        s = jnp.where(mask, s, jnp.finfo(jnp.float32).min)
        p = jax.nn.softmax(s, axis=-1).astype(v.dtype)
        return jnp.einsum('bhqk,bkhd->bqhd', p, v)

    starts = jnp.arange(nb, dtype=jnp.int32) * Q_BLOCK
    out = lax.map(one_block, (to_blocks(q_nope), to_blocks(q_rope), starts))
    return out.swapaxes(0, 1).reshape(B, S, H, V_HEAD_DIM)


def mla_layer(x, cos, sin, norm_g, w_in, q_norm_g, w_q_b, kv_norm_g, w_kv_b, w_out):
    B, S, _ = x.shape
    h = rmsnorm(x, norm_g)
    proj = h @ w_in
    o1 = Q_LORA_RANK
    o2 = o1 + KV_LORA_RANK
    o3 = o2 + QK_ROPE_DIM
    q_lat, kv_lat, k_rope, z = proj[..., :o1], proj[..., o1:o2], proj[..., o2:o3], proj[..., o3:]
    q = (rmsnorm(q_lat, q_norm_g) @ w_q_b).reshape(B, S, N_HEADS, QK_HEAD_DIM)
    q_nope, q_rope = q[..., :QK_NOPE_DIM], q[..., QK_NOPE_DIM:]
    kv = (rmsnorm(kv_lat, kv_norm_g) @ w_kv_b).reshape(B, S, N_HEADS, QK_NOPE_DIM + V_HEAD_DIM)
    k_nope, v = kv[..., :QK_NOPE_DIM], kv[..., QK_NOPE_DIM:]
    q_rope = apply_rope(q_rope, cos[:, :, None, :], sin[:, :, None, :])
    k_rope = apply_rope(k_rope, cos, sin)
    o = causal_mla_attention(q_nope, q_rope, k_nope, k_rope, v)
    y = o.reshape(B, S, MLA_WIDTH) * jax.nn.silu(z)
    return x + y @ w_out


def setup_inputs(seed: int = 0) -> dict:
    key = jax.random.key(seed)
    ks = jax.random.split(key, 16)
    nrm = jax.random.normal
    Lp, Lm = N_POOL_LAYERS, N_MLA_LAYERS
    x = nrm(ks[0], (BATCH, SEQ, D_MODEL), jnp.float32)
    positions = jnp.broadcast_to(jnp.arange(SEQ, dtype=jnp.int32)[None, :], (BATCH, SEQ))
    return {
        "x": x,
        "positions": positions,
        "pool_norm": 1.0 + 0.02 * nrm(ks[1], (Lp, D_MODEL), jnp.float32),
        "pool_w_in": nrm(ks[2], (Lp, D_MODEL, 2 * POOL_WIDTH), jnp.float32) * D_MODEL ** -0.5,
        "pool_w_group": nrm(ks[3], (Lp, N_POOL_GROUPS, POOL_GROUP, POOL_GROUP), jnp.float32) * POOL_GROUP ** -0.5,
        "pool_scale": 1.0 + 0.02 * nrm(ks[4], (Lp, POOL_WIDTH), jnp.float32),
        "pool_w_out": nrm(ks[5], (Lp, POOL_WIDTH, D_MODEL), jnp.float32) * POOL_WIDTH ** -0.5,
        "mla_norm": 1.0 + 0.02 * nrm(ks[6], (Lm, D_MODEL), jnp.float32),
        "mla_w_in": nrm(ks[7], (Lm, D_MODEL, MLA_IN_WIDTH), jnp.float32) * D_MODEL ** -0.5,
        "mla_q_norm": 1.0 + 0.02 * nrm(ks[8], (Lm, Q_LORA_RANK), jnp.float32),
        "mla_w_q_b": nrm(ks[9], (Lm, Q_LORA_RANK, N_HEADS * QK_HEAD_DIM), jnp.float32) * Q_LORA_RANK ** -0.5,
        "mla_kv_norm": 1.0 + 0.02 * nrm(ks[10], (Lm, KV_LORA_RANK), jnp.float32),
        "mla_w_kv_b": nrm(ks[11], (Lm, KV_LORA_RANK, N_HEADS * (QK_NOPE_DIM + V_HEAD_DIM)), jnp.float32) * KV_LORA_RANK ** -0.5,
        "mla_w_out": nrm(ks[12], (Lm, MLA_WIDTH, D_MODEL), jnp.float32) * MLA_WIDTH ** -0.5,
        "final_norm": 1.0 + 0.02 * nrm(ks[13], (D_MODEL,), jnp.float32),
    }


def reference(x, positions, pool_norm, pool_w_in, pool_w_group, pool_scale, pool_w_out,
              mla_norm, mla_w_in, mla_q_norm, mla_w_q_b, mla_kv_norm, mla_w_kv_b, mla_w_out,
              final_norm):
    cos, sin = rope_tables(positions, x.dtype)
    for i in range(DEPTH):
        j = i // N_MIXERS
        if i % N_MIXERS == 0:
            x = pool_layer(x, pool_norm[j], pool_w_in[j], pool_w_group[j], pool_scale[j], pool_w_out[j])
        else:
            x = mla_layer(x, cos, sin, mla_norm[j], mla_w_in[j], mla_q_norm[j], mla_w_q_b[j],
                          mla_kv_norm[j], mla_w_kv_b[j], mla_w_out[j])
    return rmsnorm(x, final_norm)
```

```python
import numpy as np
from contextlib import ExitStack
import concourse.bass as bass
import concourse.mybir as mybir
from concourse.bass_utils import run_bass_kernel_spmd

F32 = mybir.dt.float32
BF16 = mybir.dt.bfloat16
I32 = mybir.dt.int32
AF = mybir.ActivationFunctionType
ALU = mybir.AluOpType


class _Op:
    __slots__ = ("eng", "emit", "deps", "is_dma", "semkey", "signal", "count", "waits", "final", "inc", "self_inc", "ext")

    def __init__(self):
        self.inc, self.self_inc, self.ext = 16, False, None


class Prog:
    COMPUTE = ("pe", "act", "dve", "pool")
    NSEM = 0
    NPROG = 0

    def __init__(self, nc, es, sem_es=None):
        self.nc, self.es = nc, es
        self.sem_es = sem_es if sem_es is not None else es
        Prog.NPROG += 1
        self.uid = Prog.NPROG
        self.ops = []
        self.last_writer = {}
        self.readers = {}
        self.last_dma_on_sem = {}
        self.nsb = 0

    def sbuf(self, name, shape, dtype):
        return self.es.enter_context(self.nc.sbuf_tensor("sb%d_" % self.uid + name, list(shape), dtype))

    def psum(self, name, shape, dtype):
        return self.es.enter_context(self.nc.psum_tensor("pp%d_" % self.uid + name, list(shape), dtype))

    def _record(self, op, reads, writes):
        idx = len(self.ops)
        deps = {}

        def add(i, kind):
            if i is not None:
                deps[i] = deps.get(i, 0) | kind

        for k in reads:
            add(self.last_writer.get(k), 1)
        for k in writes:
            add(self.last_writer.get(k), 2)
            for i in self.readers.get(k, ()):
                add(i, 4)
        deps.pop(idx, None)
        op.deps = deps
        self.ops.append(op)
        for k in writes:
            self.last_writer[k] = idx
            self.readers[k] = []
        for k in reads:
            lst = self.readers.setdefault(k, [])
            if not op.is_dma:
                lst[:] = [i for i in lst if self.ops[i].is_dma or self.ops[i].eng != op.eng]
            lst.append(idx)
        return idx

    def op(self, eng, emit, reads=(), writes=()):
        o = _Op()
        o.eng, o.emit, o.is_dma, o.semkey, o.signal, o.count, o.final = eng, emit, False, None, False, 0, False
        return self._record(o, list(reads), list(writes))

    def dma(self, queue, out, in_, reads=(), writes=(), sem=None, final=False, **kw):
        o = _Op()
        o.eng, o.is_dma, o.semkey, o.signal, o.count, o.final = queue, True, sem, True, 0, final
        o.emit = lambda e: e.dma_start(out=out, in_=in_, **kw)
        idx = self._record(o, list(reads), list(writes))
        prev = self.last_dma_on_sem.get(sem)
        if prev is not None:
            o.deps[prev] = o.deps.get(prev, 0) | 2
        self.last_dma_on_sem[sem] = idx
        if queue == "pool":
            self.pool_dmas = getattr(self, "pool_dmas", [])
            if len(self.pool_dmas) >= 2:
                o.deps[self.pool_dmas[-2]] = o.deps.get(self.pool_dmas[-2], 0) | 2
            self.pool_dmas.append(idx)
        return idx

    def custom_dma(self, queue, emit, reads=(), writes=(), sem=None, final=False, inc=16, self_inc=False):
        o = _Op()
        o.eng, o.is_dma, o.semkey, o.signal, o.count, o.final = queue, True, sem, True, 0, final
        o.inc, o.self_inc = inc, self_inc
        o.emit = emit
        idx = self._record(o, list(reads), list(writes))
        prev = self.last_dma_on_sem.get(sem)
        if prev is not None:
            o.deps[prev] = o.deps.get(prev, 0) | 2
        self.last_dma_on_sem[sem] = idx
        return idx

    def extern(self, sem, count, writes):
        o = _Op()
        o.eng, o.is_dma, o.semkey, o.signal, o.count, o.final = "sp", True, ("ext", len(self.ops)), True, count, False
        o.ext, o.emit = sem, None
        return self._record(o, [], list(writes))

    def barrier(self, skip=None):
        last = {}
        for i, o in enumerate(self.ops):
            if o.is_dma and skip is not None and isinstance(o.semkey, tuple) and o.semkey[0] == skip:
                continue
            last[("dma", o.semkey) if o.is_dma else ("eng", o.eng)] = i
        for eng in ("pe", "act", "dve", "pool", "sp"):
            o = _Op()
            o.eng, o.is_dma, o.semkey, o.signal, o.count, o.final = eng, False, None, False, 0, False
            o.emit = lambda e: e.nop()
            o.deps = {i: 1 for k, i in last.items() if k != ("eng", eng)}
            self.ops.append(o)

    def emit(self):
        nc, es, ops = self.nc, self.es, self.ops
        for o in ops:
            keep = []
            for i, kind in o.deps.items():
                p = ops[i]
                if not p.is_dma and not o.is_dma and p.eng == o.eng:
                    if o.eng == "pe":
                        continue
                if not p.is_dma and o.is_dma and p.eng == o.eng and not (kind & 3):
                    pass
                keep.append(i)
                p.signal = True
            o.deps = keep
        eng_cnt = {e: 0 for e in ("pe", "act", "dve", "pool", "sp")}
        dma_cnt = {}
        for o in ops:
            if o.ext is not None:
                continue
            if o.is_dma:
                dma_cnt[o.semkey] = dma_cnt.get(o.semkey, 0) + o.inc
                o.count = dma_cnt[o.semkey]
            elif o.signal:
                eng_cnt[o.eng] += 1
                o.count = eng_cnt[o.eng]
        sems = {}
        for e in self.COMPUTE:
            if eng_cnt[e]:
                Prog.NSEM += 1
                sems[("eng", e)] = self.sem_es.enter_context(nc.semaphore("sem%d" % Prog.NSEM))
        for k in dma_cnt:
            Prog.NSEM += 1
            sems[("dma", k)] = self.sem_es.enter_context(nc.semaphore("sem%d" % Prog.NSEM))
        for o in ops:
            if o.ext is not None:
                sems[("dma", o.semkey)] = o.ext
        self.sems, self.dma_cnt = sems, dma_cnt

        def tok(p):
            return (sems[("dma", p.semkey)] if p.is_dma else sems[("eng", p.eng)], p.count,
                    ("dma", p.semkey) if p.is_dma else ("eng", p.eng))

        waited = {e: {} for e in eng_cnt}
        for o in ops:
            w = {}
            for i in o.deps:
                s, c, k = tok(ops[i])
                if c > waited[o.eng].get(k, 0) and c > w.get(k, (None, 0))[1]:
                    w[k] = (s, c)
            for k, (s, c) in w.items():
                waited[o.eng][k] = c
            o.waits = list(w.values())
        finals = [(sems[("dma", k)], dma_cnt[k]) for k in dma_cnt
                  if any(o.final and o.semkey == k for o in ops if o.is_dma)]
        by_eng = {e: [o for o in ops if o.eng == e] for e in eng_cnt}

        def run(ename, e):
            for o in by_eng[ename]:
                if o.ext is not None:
                    continue
                for s, c in o.waits:
                    e.wait_ge(s, c)
                if o.self_inc:
                    o.emit(e, sems[("dma", o.semkey)])
                    continue
                inst = o.emit(e)
                if inst is None:
                    continue
                if o.is_dma:
                    inst.then_inc(sems[("dma", o.semkey)], o.inc)
                elif o.signal:
                    inst.then_inc(sems[("eng", o.eng)], 1)
            if ename == "sp":
                for s, c in finals:
                    e.wait_ge(s, c)

        with nc.Block() as block:
            @block.tensor
            def _(e):
                run("pe", e)

            @block.scalar
            def _(e):
                run("act", e)

            @block.vector
            def _(e):
                run("dve", e)

            @block.gpsimd
            def _(e):
                run("pool", e)

            @block.sync
            def _(e):
                run("sp", e)


D = 1024
S = 8192
NB = 2
TOK = 2048
HALO = 16
NH = 16
HPC = 4
EPS = 1e-6
LATS = 7
LATR = 832
SCALE = 192 ** -0.5

V_G0, V_G1, V_SC, V_GQ, V_GKV, V_GF, NV = 0, 8, 16, 32, 35, 37, 45


class Banks:
    def __init__(self, P, n=8, prefix="ps"):
        self.t = [P.psum("%s%d" % (prefix, i), [128, 512], F32) for i in range(n)]
        self.keys = [(prefix, i) for i in range(n)]
        self.i = 0

    def next(self):
        i = self.i
        self.i = (self.i + 1) % len(self.t)
        return self.t[i], self.keys[i]


def rms_inv(P, banks, ones, src_sq, nk, W, nfeat, rt, inv, keys_sq, key_rt, key_inv):
    ps, pk = banks.next()

    def mm(e):
        for k in range(nk):
            r = e.matmul(ps[:, :W], lhsT=ones[:], rhs=src_sq[:, k, :W], start=(k == 0), stop=(k == nk - 1))
        return r
    P.op("pe", mm, reads=list(keys_sq) + ["ones"], writes=[pk])
    P.op("act", lambda e: e.activation(out=rt[:, :W], in_=ps[:, :W], func=AF.Ln, bias=EPS, scale=1.0 / nfeat),
         reads=[pk], writes=[key_rt])
    P.op("act", lambda e: e.activation(out=inv[:, :W], in_=rt[:, :W], func=AF.Exp, scale=-0.5), reads=[key_rt], writes=[key_inv])


def phase_A(nc, P, io):
    TT, NT = 256, TOK // 256
    xT, x1T, lat = io["xT"], io["x1T"], io["lat"]
    banks = Banks(P)
    ones = P.sbuf("ones", [128, 128], BF16)
    P.op("pool", lambda e: e.memset(ones[:], 1.0), writes=["ones"])
    vec = P.sbuf("vec", [128, NV], F32)
    P.dma("sp", vec[:], io["vec"], writes=["vec"], sem="vec")
    rc = P.sbuf("rc", [128, 64], F32)
    P.dma("sp", rc[:], io["rc"], writes=["rc"], sem="rc")

    w_in = P.sbuf("w_in", [128, 8, 4096], BF16)
    wg = P.sbuf("wg", [128, 4, 4, 512], BF16)
    wo = P.sbuf("wo", [128, 16, 1024], BF16)
    wl = P.sbuf("wl", [128, 8, LATR], BF16)
    for g in range(4):
        P.dma("pool", w_in[:, :, g * 1024:(g + 1) * 1024], io["pool_w_in"][:, g, :, :],
              writes=[("w_in", g)], sem=("w_in", g))
        P.dma("pool", wg[:, g, :, :], io["pool_w_group"][:, g, :, :], writes=[("wg", g)], sem=("wg", g))
    for h2 in range(2):
        P.dma("pool", wo[:, h2 * 8:(h2 + 1) * 8, :].rearrange("p (a b) f -> p a (b f)", b=2),
              io["pool_w_out"][:, h2 * 4:(h2 + 1) * 4, :], writes=[("wo", h2)], sem=("wo", h2))
    P.dma("pool", wl[:], io["wl"], writes=["wl"], sem="wl")

    xt = [P.sbuf("xt%d" % i, [128, 8, TT], F32) for i in range(2)]
    sq = P.sbuf("sq", [128, 8, TT], BF16)
    rt = P.sbuf("rt", [128, TT], F32)
    inv = P.sbuf("inv", [128, TT], F32)
    hb = [P.sbuf("h%d" % i, [128, 8, TT], BF16) for i in range(3)]
    NU = 3
    u = [P.sbuf("u%d" % i, [128, HALO + TT], F32) for i in range(NU)]
    pa = P.sbuf("pa", [128, HALO + TT], F32)
    pb = P.sbuf("pb", [128, HALO + TT], F32)
    tmp16 = P.sbuf("tmp16", [128, 16], F32)
    pooled = [P.sbuf("pooled%d" % i, [128, 4, TT], BF16) for i in range(2)]
    sz = [P.sbuf("sz%d" % i, [128, 4, TT], BF16) for i in range(2)]
    yb = [P.sbuf("y%d" % i, [128, 16, TT], BF16) for i in range(2)]
    uh = [P.sbuf("uh%d" % i, [128, 16, HALO], F32) for i in range(2)]
    ql = P.sbuf("ql", [128, 5, TT], F32)
    sql = P.sbuf("sql", [128, 5, TT], BF16)
    lat_o = [P.sbuf("lat_o%d" % i, [128, LATS, TT], BF16) for i in range(2)]
    for i in range(2):
        P.op("pool", lambda e, i=i: e.memset(lat_o[i][:], 0.0), writes=[("lat_o%d" % i, j) for j in range(LATS)])
    ucount = [0]

    def norm_h(xs, xkey, W, gcol0, hi):
        h = hb[hi]
        P.op("act", lambda e: e.activation(out=sq[:, :, :W], in_=xs[:, :, :W], func=AF.Square),
             reads=[xkey], writes=["sq"])
        rms_inv(P, banks, ones, sq, 8, W, D, rt, inv, ["sq"], "rt", "inv")
        for kc in range(8):
            P.op("dve", lambda e, kc=kc: e.scalar_tensor_tensor(
                out=h[:, kc, :W], in0=xs[:, kc, :W], scalar=vec[:, gcol0 + kc:gcol0 + kc + 1],
                in1=inv[:, :W], op0=ALU.mult, op1=ALU.mult),
                reads=[xkey, "inv", "vec"], writes=[("h", hi, kc)])

    def unit8(wkeys, lhs_of, W, hi):
        ps, pk = banks.next()
        h = hb[hi]

        def mm(e):
            for kc in range(8):
                r = e.matmul(ps[:, :W], lhsT=lhs_of(kc), rhs=h[:, kc, :W], start=(kc == 0), stop=(kc == 7))
            return r
        P.op("pe", mm, reads=[("h", hi, kc) for kc in range(8)] + list(wkeys), writes=[pk])
        return ps, pk

    xh = xt[1]
    P.dma("sp", xh[:, :, :HALO], io["xh"], writes=["xt1"], sem="xt1")
    norm_h(xh, "xt1", HALO, V_G0, 2)

    def halo(g):
        for f in range(4 * g, 4 * g + 4):
            ps, pk = unit8([("w_in", g)], lambda kc, f=f, g=g: w_in[:, kc, g * 1024 + (f % 4) * 128:g * 1024 + (f % 4 + 1) * 128], HALO, 2)
            P.op("act", lambda e, ps=ps, f=f: e.activation(out=uh[0][:, f, :], in_=ps[:, :HALO], func=AF.Copy),
                 reads=[pk], writes=[("uh", 0, f)])

    def F0(t):
        xs, xkey = xt[t % 2], "xt%d" % (t % 2)
        P.dma("sp", xs[:], xT[:, t, :, :], writes=[xkey], sem=xkey)
        norm_h(xs, xkey, TT, V_G0, t % 2)

    def F1(t, g):
        uhc, uhn = uh[t % 2], uh[(t + 1) % 2]
        slot = (t * 4 + g) % 2
        w = 2 << g
        hi = t % 2
        for fc in range(4):
            f = g * 4 + fc
            us = u[ucount[0] % NU]
            ukey = "u%d" % (ucount[0] % NU)
            ucount[0] += 1
            ps, pk = unit8([("w_in", g)], lambda kc, fc=fc, g=g: w_in[:, kc, g * 1024 + fc * 128:g * 1024 + (fc + 1) * 128], TT, hi)
            P.op("act", lambda e, ps=ps, us=us: e.activation(out=us[:, HALO:], in_=ps[:, :TT], func=AF.Copy),
                 reads=[pk], writes=[ukey])
            P.op("act", lambda e, ps=ps, f=f, uhn=uhn: e.activation(out=uhn[:, f, :], in_=ps[:, TT - HALO:TT], func=AF.Copy),
                 reads=[pk], writes=[("uh", (t + 1) % 2, f)])
            P.op("act", lambda e, us=us, f=f, uhc=uhc: e.activation(out=us[:, :HALO], in_=uhc[:, f, :], func=AF.Copy),
                 reads=[("uh", t % 2, f)], writes=[ukey])
            ps, pk = unit8([("w_in", g)], lambda kc, fc=fc, g=g: w_in[:, kc, g * 1024 + 512 + fc * 128:g * 1024 + 512 + (fc + 1) * 128], TT, hi)
            P.op("act", lambda e, ps=ps, slot=slot, fc=fc: e.activation(out=sz[slot][:, fc, :], in_=ps[:, :TT], func=AF.Silu),
                 reads=[pk], writes=[("sz", slot, fc)])
            E = HALO + TT
            src, skey = us, ukey
            bufs = [(pa, "pa"), (pb, "pb")]
            step, lo, bi = 1, 1, 0
            while step < w:
                dst, dkey = bufs[bi]
                P.op("dve", lambda e, dst=dst, src=src, step=step, lo=lo: e.tensor_tensor(
                    out=dst[:, lo:E], in0=src[:, lo:E], in1=src[:, lo - step:E - step], op=ALU.add),
                    reads=[skey], writes=[dkey])
                src, skey = dst, dkey
                step *= 2
                lo = 2 * step - 1
                bi ^= 1
            P.op("dve", lambda e, src=src, us=us, slot=slot, fc=fc, w=w: e.scalar_tensor_tensor(
                out=pooled[slot][:, fc, :], in0=src[:, HALO:E], scalar=1.0 / w, in1=us[:, HALO:E],
                op0=ALU.mult, op1=ALU.subtract),
                reads=[skey, ukey], writes=[("pooled", slot, fc)])
            if t == 0:
                P.op("dve", lambda e, src=src, g=g: e.tensor_tensor(
                    out=tmp16[:], in0=src[:, HALO:HALO + 16], in1=rc[:, g * 16:(g + 1) * 16], op=ALU.mult),
                    reads=[skey, "rc"], writes=["tmp16"])
                P.op("dve", lambda e, us=us, slot=slot, fc=fc: e.tensor_tensor(
                    out=pooled[slot][:, fc, 0:16], in0=tmp16[:], in1=us[:, HALO:HALO + 16], op=ALU.subtract),
                    reads=["tmp16", ukey], writes=[("pooled", slot, fc)])

    def F2(t, g):
        slot = (t * 4 + g) % 2
        y, yk = yb[t % 2], t % 2
        for fo in range(4):
            f = g * 4 + fo
            ps, pk = banks.next()

            def mm(e, ps=ps, g=g, fo=fo, slot=slot):
                for kc in range(4):
                    r = e.matmul(ps[:, :TT], lhsT=wg[:, g, kc, fo * 128:(fo + 1) * 128],
                                 rhs=pooled[slot][:, kc, :], start=(kc == 0), stop=(kc == 3))
                return r
            P.op("pe", mm, reads=[("pooled", slot, kc) for kc in range(4)] + [("wg", g)], writes=[pk])
            P.op("dve", lambda e, ps=ps, f=f, fo=fo, slot=slot, y=y: e.scalar_tensor_tensor(
                out=y[:, f, :], in0=ps[:, :TT], scalar=vec[:, V_SC + f:V_SC + f + 1], in1=sz[slot][:, fo, :],
                op0=ALU.mult, op1=ALU.mult),
                reads=[pk, ("sz", slot, fo), "vec"], writes=[("y", yk, f)])

    def F3(t):
        xs, xkey = xt[t % 2], "xt%d" % (t % 2)
        y, yk = yb[t % 2], t % 2
        for dc in range(8):
            ps, pk = banks.next()

            def mm(e, ps=ps, dc=dc, y=y):
                for kc in range(16):
                    r = e.matmul(ps[:, :TT], lhsT=wo[:, kc, dc * 128:(dc + 1) * 128], rhs=y[:, kc, :],
                                 start=(kc == 0), stop=(kc == 15))
                return r
            P.op("pe", mm, reads=[("y", yk, kc) for kc in range(16)] + [("wo", 0), ("wo", 1)], writes=[pk])
            P.op("dve", lambda e, ps=ps, dc=dc, xs=xs: e.tensor_tensor(
                out=xs[:, dc, :], in0=ps[:, :TT], in1=xs[:, dc, :], op=ALU.add),
                reads=[pk, xkey], writes=[xkey])

    def F4(t):
        xs, xkey = xt[t % 2], "xt%d" % (t % 2)
        P.dma("pool", x1T[:, t, :, :], xs[:], reads=[xkey], writes=[("x1T", t)],
              sem=("x1st", t % 2), final=True)
        norm_h(xs, xkey, TT, V_G1, 2)

    def F5(t):
        lo_t, lkey = lat_o[t % 2], "lat_o%d" % (t % 2)
        h = hb[2]
        for j in range(5):
            ps, pk = unit8(["wl"], lambda kc, j=j: wl[:, kc, j * 128:(j + 1) * 128], TT, 2)
            P.op("act", lambda e, ps=ps, j=j: e.activation(out=ql[:, j, :], in_=ps[:, :TT], func=AF.Copy),
                 reads=[pk], writes=[("ql", j)])
        for j in (5, 6):
            ps, pk = unit8(["wl"], lambda kc, j=j: wl[:, kc, 640 + (j - 5) * 64:768 + (j - 5) * 64], TT, 2)
            P.op("act", lambda e, ps=ps, lo_t=lo_t, j=j: e.activation(out=lo_t[0:64, j, :], in_=ps[0:64, :TT], func=AF.Copy),
                 reads=[pk], writes=[(lkey, j)])
        for (j0, nj, nfeat, gc) in ((0, 3, 384, V_GQ), (3, 2, 256, V_GKV)):
            P.op("act", lambda e, j0=j0, nj=nj: e.activation(out=sql[:, j0:j0 + nj, :], in_=ql[:, j0:j0 + nj, :], func=AF.Square),
                 reads=[("ql", j) for j in range(j0, j0 + nj)], writes=[("sql", j0)])
            ps, pk = banks.next()

            def mm(e, ps=ps, j0=j0, nj=nj):
                for k in range(nj):
                    r = e.matmul(ps[:, :TT], lhsT=ones[:], rhs=sql[:, j0 + k, :], start=(k == 0), stop=(k == nj - 1))
                return r
            P.op("pe", mm, reads=[("sql", j0), "ones"], writes=[pk])
            P.op("act", lambda e, ps=ps, nfeat=nfeat: e.activation(out=rt[:, :TT], in_=ps[:, :TT], func=AF.Ln, bias=EPS, scale=1.0 / nfeat),
                 reads=[pk], writes=["rt"])
            P.op("act", lambda e: e.activation(out=inv[:, :TT], in_=rt[:, :TT], func=AF.Exp, scale=-0.5), reads=["rt"], writes=["inv"])
            for k in range(nj):
                j = j0 + k
                P.op("dve", lambda e, j=j, k=k, gc=gc, lo_t=lo_t: e.scalar_tensor_tensor(
                    out=lo_t[:, j, :], in0=ql[:, j, :], scalar=vec[:, gc + k:gc + k + 1], in1=inv[:, :TT],
                    op0=ALU.mult, op1=ALU.mult),
                    reads=[("ql", j), "inv", "vec"], writes=[(lkey, j)])
        P.dma("pool", lat[t * 128:(t + 1) * 128, :], lo_t[:].rearrange("p j c -> p (j c)"), reads=[(lkey, j) for j in range(LATS)],
              writes=[("lat", t)], sem=("latst", t % 2), final=True)
        if t % 2 == 1 and io.get("ag1") is not None:
            io["ag1"](P, t // 2)

    F0(0)
    for t in range(NT):
        prev = t - 1
        if t == 0:
            halo(0)
        F1(t, 0)
        if prev >= 0:
            F2(prev, 3)
        if t == 0:
            halo(1)
        F1(t, 1)
        if prev >= 0:
            F3(prev)
        F2(t, 0)
        if t == 0:
            halo(2)
        F1(t, 2)
        if prev >= 0:
            F4(prev)
        F2(t, 1)
        if t == 0:
            halo(3)
        F1(t, 3)
        if t + 1 < NT:
            F0(t + 1)
        if prev >= 0:
            F5(prev)
        F2(t, 2)
    F2(NT - 1, 3)
    F3(NT - 1)
    F4(NT - 1)
    F5(NT - 1)


PI = float(np.pi)
C1 = 6.28125
C2 = float(2.0 * np.pi - 6.28125)


def phase_B(nc, P, io):
    QT = 512
    NQ = S // QT
    latA, oT = io["latA"], io["oT"]
    ones = P.sbuf("ones", [128, 128], BF16)
    P.op("pool", lambda e: e.memset(ones[:], 1.0), writes=["ones"])
    tri = P.sbuf("tri", [128, 128], BF16)
    P.dma("pool", tri[:], io["tri"], writes=["tri"], sem="tri")
    rcol = P.sbuf("rcol", [128, 2], F32)
    P.dma("sp", rcol[:], io["rcol"], writes=["rcol"], sem="rcol")
    wq = P.sbuf("wq", [128, 3, HPC * 320], BF16)
    wk = P.sbuf("wk", [128, 2, HPC * 128], BF16)
    wv = P.sbuf("wv", [128, 2, HPC * 128], BF16)
    P.dma("pool", wk[:], io["wk"], writes=["wk"], sem="wk")
    P.dma("pool", wv[:], io["wv"], writes=["wv"], sem="wv")
    P.dma("pool", wq[:], io["wq"], writes=["wq"], sem="wq")

    KT = P.sbuf("KT", [128, HPC, S], BF16)
    V = P.sbuf("V", [128, S // 128, HPC * 128], BF16)
    KR = P.sbuf("KR", [128, S], BF16)
    pst = [P.psum("ps%d" % i, [128, 512], F32) for i in range(8)]
    pskey = [("ps", i) for i in range(8)]
    latA_v = latA.rearrange("(k r t p) (j c) -> p k r t j c", k=4, r=4, t=2, p=128, j=LATS)

    SB = [0, 1, 2, 7]
    OB = [3, 4]
    DB = [5, 6]
    sbi = [0]

    def sbank():
        b = SB[sbi[0] % 4]
        sbi[0] += 1
        return pst[b], pskey[b]

    posi = P.sbuf("posi", [128, QT], I32)
    ang = P.sbuf("ang", [128, QT], F32)
    kf = P.sbuf("kf", [128, QT], F32)
    rr = P.sbuf("rr", [128, QT], F32)
    CS = [P.sbuf("CS%d" % i, [128, QT], F32) for i in range(2)]
    Ct = [c[0:64, :] for c in CS]
    S0 = P.sbuf("S0", [64, QT], F32)
    kk = [P.sbuf("kk%d" % i, [64, 2, 2, QT // 2], BF16) for i in range(2)]

    def v2(ap):
        return ap.rearrange("p (t c) -> p t c", t=2)
    kvn = [P.sbuf("kvn%d" % i, [128, 2, 2, QT // 2], BF16) for i in range(2)]
    qn = [P.sbuf("qn%d" % i, [128, 2, 3, QT // 2], BF16) for i in range(2)]
    t1f = P.sbuf("t1", [128, QT], F32)
    bi = [0]
    ev = [0]

    def evac_copy(dst, src, rkeys, wkeys):
        ev[0] += 1
        if ev[0] % 2:
            P.op("act", lambda e: e.activation(out=dst, in_=src, func=AF.Copy), reads=rkeys, writes=wkeys)
        else:
            P.op("dve", lambda e: e.tensor_copy(out=dst, in_=src), reads=rkeys, writes=wkeys)

    def prep_steps(tt):
        r_ = tt // 4
        sl = tt % 2
        g0 = tt * QT
        ck = ("latA", tt % 4)
        steps = []

        def dve(fn, reads, writes):
            steps.append(lambda: P.op("dve", fn, reads=reads, writes=writes))

        def loads():
            P.dma("sp", posi[:], io["posr"][:, g0:g0 + QT], writes=["posi"], sem="posi")
            P.dma("sp", kk[sl][:], latA_v[0:64, tt % 4, r_, :, 5:7, :], reads=[ck], writes=[("kk", sl)], sem=("kk", sl))
            P.dma("sp", kvn[sl][:], latA_v[:, tt % 4, r_, :, 3:5, :], reads=[ck], writes=[("kvn", sl)], sem=("kvn", sl))
            P.dma("sp", qn[sl][:], latA_v[:, tt % 4, r_, :, 0:3, :], reads=[ck], writes=[("qn", sl)], sem=("qn", sl))
        steps.append(loads)
        for hl in range(HPC):
            def kstep(hl=hl):
                ps, pk = sbank()

                def mm(e, ps=ps, hl=hl, sl=sl):
                    for kc in range(2):
                        r = e.matmul(ps[:], lhsT=wk[:, kc, hl * 128:(hl + 1) * 128], rhs=kvn[sl][:, :, kc, :],
                                     start=(kc == 0), stop=(kc == 1))
                    return r
                P.op("pe", mm, reads=[("kvn", sl), "wk"], writes=[pk])
                evac_copy(KT[:, hl, g0:g0 + QT], ps[:], [pk], [("KT", hl, tt)])
            steps.append(kstep)
        for sub in range(4):
            def vstep(sub=sub):
                ps, pk = sbank()

                def mm(e, ps=ps, sub=sub, sl=sl):
                    for kc in range(2):
                        r = e.matmul(ps[:], lhsT=kvn[sl][:, sub // 2, kc, (sub % 2) * 128:(sub % 2 + 1) * 128], rhs=wv[:, kc, :],
                                     start=(kc == 0), stop=(kc == 1))
                    return r
                P.op("pe", mm, reads=[("kvn", sl), "wv"], writes=[pk])
                evac_copy(V[:, tt * 4 + sub, :], ps[:], [pk], [("V", tt * 4 + sub)])
            steps.append(vstep)
        dve(lambda e: e.tensor_copy(out=ang[:], in_=posi[:]), ["posi"], ["ang"])
        dve(lambda e: e.tensor_scalar(out=ang[:], in0=ang[:], scalar1=rcol[:, 0:1], scalar2=None, op0=ALU.mult), ["ang", "rcol"], ["ang"])
        dve(lambda e: e.tensor_scalar(out=kf[:], in0=ang[:], scalar1=1.0 / (2 * PI), scalar2=0.5, op0=ALU.mult, op1=ALU.add), ["ang"], ["kf"])
        dve(lambda e: e.tensor_copy(out=posi[:], in_=kf[:]), ["kf"], ["posi"])
        dve(lambda e: e.tensor_copy(out=kf[:], in_=posi[:]), ["posi"], ["kf"])
        dve(lambda e: e.scalar_tensor_tensor(out=rr[:], in0=kf[:], scalar=-C1, in1=ang[:], op0=ALU.mult, op1=ALU.add), ["kf", "ang"], ["rr"])
        dve(lambda e: e.scalar_tensor_tensor(out=rr[:], in0=kf[:], scalar=-C2, in1=rr[:], op0=ALU.mult, op1=ALU.add), ["kf", "rr"], ["rr"])

        dve(lambda e: e.tensor_scalar(out=kf[:], in0=rr[:], scalar1=-PI, scalar2=2 * PI, op0=ALU.is_lt, op1=ALU.mult), ["rr"], ["kf"])
        dve(lambda e: e.tensor_tensor(out=rr[:], in0=rr[:], in1=kf[:], op=ALU.add), ["rr", "kf"], ["rr"])
        dve(lambda e: e.tensor_scalar(out=ang[:], in0=rr[:], scalar1=PI / 2, scalar2=None, op0=ALU.add), ["rr"], ["ang"])
        dve(lambda e: e.tensor_scalar(out=kf[:], in0=ang[:], scalar1=PI, scalar2=-2 * PI, op0=ALU.is_gt, op1=ALU.mult), ["ang"], ["kf"])
        dve(lambda e: e.tensor_tensor(out=ang[:], in0=ang[:], in1=kf[:], op=ALU.add), ["ang", "kf"], ["ang"])

        def sins():
            P.op("act", lambda e: e.activation(out=S0[:], in_=rr[0:64, :], func=AF.Sin), reads=["rr"], writes=["S0"])
            P.op("act", lambda e: e.activation(out=CS[sl][64:128, :], in_=rr[64:128, :], func=AF.Sin), reads=["rr"], writes=[("St", sl)])
            P.op("act", lambda e: e.activation(out=CS[sl][0:64, :], in_=ang[0:64, :], func=AF.Sin), reads=["ang"], writes=[("Ct", sl)])
        nA = len(steps)
        steps.append(sins)
        dve(lambda e: e.tensor_scalar(out=S0[:], in0=S0[:], scalar1=rcol[0:64, 1:2], scalar2=None, op0=ALU.mult), ["S0", "rcol"], ["S0"])
        dve(lambda e: e.tensor_scalar(out=CS[sl][64:128, :], in0=CS[sl][64:128, :], scalar1=rcol[64:128, 1:2], scalar2=None, op0=ALU.mult),
            [("St", sl), "rcol"], [("St", sl)])
        dve(lambda e: e.tensor_tensor(out=v2(kf[0:64, :]), in0=kk[sl][:, :, 0, :], in1=v2(Ct[sl]), op=ALU.mult), [("kk", sl), ("Ct", sl)], ["kf"])
        dve(lambda e: e.tensor_tensor(out=v2(rr[0:64, :]), in0=kk[sl][:, :, 1, :], in1=v2(S0[:]), op=ALU.mult), [("kk", sl), "S0"], ["rr"])
        dve(lambda e: e.tensor_tensor(out=KR[0:64, g0:g0 + QT], in0=kf[0:64, :], in1=rr[0:64, :], op=ALU.add), ["kf", "rr"], [("KR", tt)])
        steps.append(lambda: P.dma("sp", KR[64:128, g0:g0 + QT], KR[0:64, g0:g0 + QT], reads=[("KR", tt)], writes=[("KR2", tt)], sem=("krd", sl)))
        return steps[:nA], steps[nA:nA + 1], steps[nA + 1:]

    pendA, pendB, pendC = [], [], []

    def drain(lst, n):
        for _ in range(min(n, len(lst))):
            lst.pop(0)()

    Qn = [P.sbuf("Qn%d" % i, [128, QT], BF16) for i in range(2)]
    Qr = [P.sbuf("Qr%d" % i, [128, QT], BF16) for i in range(2)]
    NP = 4
    pT = [P.sbuf("pT%d" % i, [128, QT], BF16) for i in range(NP)]
    rden = t1f
    ot1 = P.sbuf("ot", [128, HPC, QT], BF16)
    ot = [ot1, ot1]
    psm = [P.sbuf("psm%d" % i, [128, QT], BF16) for i in range(4)]
    psi = [0]
    pti = [0]
    WQH = 320

    def qproj(it):
        qi, hl = it // HPC, it % HPC
        qs, s2 = qi % 2, it % 2
        ps, pk = sbank()

        def mm(e, ps=ps, hl=hl, qs=qs):
            for kc in range(3):
                r = e.matmul(ps[:], lhsT=wq[:, kc, hl * WQH:hl * WQH + 128], rhs=qn[qs][:, :, kc, :],
                             start=(kc == 0), stop=(kc == 2))
            return r
        P.op("pe", mm, reads=[("qn", qs), "wq"], writes=[pk])
        P.op("act", lambda e, ps=ps, s2=s2: e.activation(out=Qn[s2][:], in_=ps[:], func=AF.Copy),
             reads=[pk], writes=[("Qn", s2)])
        ps, pk = sbank()
        off = hl * WQH + 128

        def mm(e, ps=ps, off=off, qs=qs):
            for kc in range(3):
                r = e.matmul(ps[:], lhsT=wq[:, kc, off:off + 128], rhs=qn[qs][:, :, kc, :],
                             start=(kc == 0), stop=(kc == 2))
            return r
        P.op("pe", mm, reads=[("qn", qs), "wq"], writes=[pk])
        P.op("dve", lambda e, ps=ps, qs=qs, s2=s2: e.tensor_tensor(out=Qr[s2][:], in0=ps[:], in1=CS[qs][:], op=ALU.mult),
             reads=[pk, ("Ct", qs), ("St", qs)], writes=[("Qr", s2)])

    def attn(it):
        qi, hl = it // HPC, it % HPC
        qs, s2 = qi % 2, it % 2
        nk = 4 * (qi + 1)
        ob, db = OB[s2], DB[s2]
        LA = 2
        stiles = {}
        den_at = {}

        def emit_S(kj):
            m = kj - 4 * qi
            lo = 128 * m if m > 0 else 0
            ps, pk = sbank()

            def mm(e, ps=ps, kj=kj, lo=lo, hl=hl, s2=s2):
                e.matmul(ps[:, lo:QT], lhsT=KT[:, hl, kj * 128:(kj + 1) * 128], rhs=Qn[s2][:, lo:QT],
                         start=True, stop=False)
                return e.matmul(ps[:, lo:QT], lhsT=KR[:, kj * 128:(kj + 1) * 128], rhs=Qr[s2][:, lo:QT],
                                start=False, stop=True)
            P.op("pe", mm, reads=[("KT", hl, kj // 4), ("KR", kj // 4), ("KR2", kj // 4), ("Qn", s2), ("Qr", s2)], writes=[pk])
            pi_ = pti[0] % NP
            pti[0] += 1
            P.op("act", lambda e, ps=ps, lo=lo, pi_=pi_: e.activation(out=pT[pi_][:, lo:QT], in_=ps[:, lo:QT], func=AF.Exp, scale=SCALE),
                 reads=[pk], writes=[("pT", pi_)])
            if m >= 0:
                P.op("dve", lambda e, lo=lo, pi_=pi_: e.tensor_tensor(out=pT[pi_][:, lo:lo + 128], in0=pT[pi_][:, lo:lo + 128], in1=tri[:], op=ALU.mult),
                     reads=[("pT", pi_), "tri"], writes=[("pT", pi_)])
            sm = None
            if m < 0 and kj % 2 == 1:
                nfull = 4 * qi
                sp_ = (kj // 2) % 4
                pj = stiles[kj - 1][0]
                P.op("dve", lambda e, sp_=sp_, pj=pj, pi_=pi_: e.tensor_tensor(out=psm[sp_][:], in0=pT[pj][:], in1=pT[pi_][:], op=ALU.add),
                     reads=[("pT", pj), ("pT", pi_)], writes=[("psm", sp_)])
                if kj % 4 == 3:
                    P.op("dve", lambda e, sp_=sp_: e.tensor_tensor(out=psm[sp_][:], in0=psm[sp_ - 1][:], in1=psm[sp_][:], op=ALU.add),
                         reads=[("psm", sp_ - 1), ("psm", sp_)], writes=[("psm", sp_)])
                    if kj % 8 == 7:
                        P.op("dve", lambda e: e.tensor_tensor(out=psm[3][:], in0=psm[1][:], in1=psm[3][:], op=ALU.add),
                             reads=[("psm", 1), ("psm", 3)], writes=[("psm", 3)])
                        den_at[min(kj + 2, nfull - 1)] = (3, kj == 7)
                    elif kj == nfull - 1:
                        den_at[nfull - 1] = (1, kj == 3)
            stiles[kj] = (pi_, lo, sm)

        def emit_PV(kj):
            pi_, lo, sm = stiles.pop(kj)
            m = kj - 4 * qi
            sm, first = den_at.pop(kj, (None, False))

            def mm(e, kj=kj, lo=lo, pi_=pi_, hl=hl, ob=ob, db=db, nk=nk, sm=sm, m=m, first=first):
                r = e.matmul(pst[ob][:, lo:QT], lhsT=V[:, kj, hl * 128:(hl + 1) * 128], rhs=pT[pi_][:, lo:QT],
                             start=(kj == 0), stop=(kj == nk - 1))
                if sm is not None:
                    r = e.matmul(pst[db][:], lhsT=ones[:], rhs=psm[sm][:], start=first, stop=False)
                if m >= 0:
                    r = e.matmul(pst[db][:, lo:QT], lhsT=ones[:], rhs=pT[pi_][:, lo:QT],
                                 start=(kj == 0), stop=(kj == nk - 1))
                return r
            rd = [("pT", pi_), ("V", kj), "ones"] + ([("psm", sm)] if sm is not None else [])
            P.op("pe", mm, reads=rd, writes=[pskey[ob], pskey[db]])

        if hl == 1:
            drain(pendA, len(pendA))
        if hl == 2:
            drain(pendB, len(pendB))
        if hl == 3:
            drain(pendC, len(pendC))
        perA = -(-len(pendA) // nk) if hl == 0 else 0
        perC = -(-len(pendC) // nk) if hl == 2 else 0
        for kj in range(nk + LA):
            if kj < nk:
                emit_S(kj)
                drain(pendA, perA)
                drain(pendC, perC)
            if kj - LA >= 0:
                emit_PV(kj - LA)
            if kj == 1 and it + 1 < NQ * HPC:
                qproj(it + 1)
        P.op("act", lambda e, db=db: e.activation(out=rden[:], in_=pst[db][:], func=AF.Ln), reads=[pskey[db]], writes=["t1"])
        P.op("act", lambda e: e.activation(out=rden[:], in_=rden[:], func=AF.Exp, scale=-1.0), reads=["t1"], writes=["t1"])
        P.op("dve", lambda e, ob=ob, qs=qs, hl=hl: e.tensor_tensor(out=ot[qs][:, hl, :], in0=pst[ob][:], in1=rden[:], op=ALU.mult),
             reads=[pskey[ob], "t1"], writes=[("ot", hl)])

    for lst in prep_steps(0):
        for st in lst:
            st()
    qproj(0)
    for qi in range(NQ):
        if qi + 1 < NQ:
            a_, b_, c_ = prep_steps(qi + 1)
            pendA.extend(a_), pendB.extend(b_), pendC.extend(c_)
        for hl in range(HPC):
            attn(qi * HPC + hl)
        qs = qi % 2
        P.dma("pool", oT[qi * 128:(qi + 1) * 128, :], ot[qs][:].rearrange("p h c -> p (h c)"),
              reads=[("ot", hl) for hl in range(HPC)], writes=[("oT", qi)], sem="ost", final=True)
        if qi % 2 == 1 and io.get("ag2") is not None:
            io["ag2"](P, qi // 2)
        if io.get("prefetch") is not None:
            io["prefetch"](P, qi)


def phase_C(nc, P, io):
    TT, NT = 512, TOK // 512
    x1T, oTo, outT = io["x1T"], io["oTo"], io["outT"]
    banks = Banks(P)
    ones = P.sbuf("ones", [128, 128], BF16)
    P.op("pool", lambda e: e.memset(ones[:], 1.0), writes=["ones"])
    vec = P.sbuf("vec", [128, NV], F32)
    P.dma("sp", vec[:], io["vec"], writes=["vec"], sem="vec")
    wz = P.sbuf("wz", [128, 8, 2048], BF16)
    wo = P.sbuf("wo", [128, 16, 1024], BF16)
    wq_, wsrc, wosrc = ("act", io["wz_bf"], io["wo_bf"]) if io.get("wz_bf") is not None else ("pool", io["wz"], io["mla_w_out"])

    def wload(first):
        for q4 in ([0] if first else [1, 2, 3]):
            P.dma(wq_, wz[:, :, q4 * 512:(q4 + 1) * 512], wsrc[:, :, q4 * 512:(q4 + 1) * 512],
                  reads=([] if first else ["xt0"]), writes=[("wz", q4)], sem=("wz", q4))
        if not first:
            for h2 in range(2):
                P.dma(wq_, wo[:, h2 * 8:(h2 + 1) * 8, :].rearrange("p (a b) f -> p a (b f)", b=2),
                      wosrc[:, h2 * 4:(h2 + 1) * 4, :], reads=["xt0"], writes=[("wo", h2)], sem=("wo", h2))
    wload(True)
    xt = [P.sbuf("xt%d" % i, [128, 2, 8, 256], F32) for i in range(3)]
    og = [P.sbuf("og%d" % i, [128, 16, TT], BF16) for i in range(2)]
    sq = P.sbuf("sq", [128, 2, 8, 256], BF16)
    rt = P.sbuf("rt", [128, TT], F32)
    inv = rt
    hb = [P.sbuf("h%d" % i, [128, 8, TT], BF16) for i in range(2)]
    szt = [P.sbuf("szt%d" % i, [128, TT], F32) for i in range(2)]
    yb = [P.sbuf("y%d" % i, [128, 16, TT], BF16) for i in range(2)]

    def v2(ap):
        return ap.rearrange("p (a c) -> p a c", a=2)

    def stats(xs, xkey):
        P.op("act", lambda e, xs=xs: e.activation(out=sq[:], in_=xs[:], func=AF.Square), reads=[xkey], writes=["sq"])
        ps, pk = banks.next()

        def mm(e, ps=ps):
            for k in range(8):
                r = e.matmul(ps[:], lhsT=ones[:], rhs=sq[:, :, k, :], start=(k == 0), stop=(k == 7))
            return r
        P.op("pe", mm, reads=["sq", "ones"], writes=[pk])
        P.op("act", lambda e, ps=ps: e.activation(out=rt[:], in_=ps[:], func=AF.Ln, bias=EPS, scale=1.0 / D),
             reads=[pk], writes=["rt"])
        P.op("act", lambda e: e.activation(out=inv[:], in_=rt[:], func=AF.Exp, scale=-0.5), reads=["rt"], writes=["rt"])

    def C0(t):
        xs, xkey = xt[t % 3], "xt%d" % (t % 3)
        os_, okey = og[t % 2], "og%d" % (t % 2)
        h = hb[t % 2]
        P.dma("sp", xs[:], x1T[:, 2 * t:2 * t + 2, :, :], reads=["x1T"], writes=[xkey], sem=xkey)
        for rr in range(4):
            io["oTo_dma"](P, t, rr, os_[:, rr * 4:(rr + 1) * 4, :].rearrange("p h c -> p (h c)"), (okey, rr))
        stats(xs, xkey)
        for kc in range(8):
            P.op("dve", lambda e, kc=kc, xs=xs, h=h: e.scalar_tensor_tensor(
                out=v2(h[:, kc, :]), in0=xs[:, :, kc, :], scalar=vec[:, V_G1 + kc:V_G1 + kc + 1], in1=v2(inv[:]),
                op0=ALU.mult, op1=ALU.mult), reads=[xkey, "rt", "vec"], writes=[("h", t % 2, kc)])

    def C1(t):
        os_, okey = og[t % 2], "og%d" % (t % 2)
        h, y = hb[t % 2], yb[t % 2]
        for f in range(16):
            ps, pk = banks.next()

            def mm(e, ps=ps, f=f, h=h):
                for kc in range(8):
                    r = e.matmul(ps[:], lhsT=wz[:, kc, f * 128:(f + 1) * 128], rhs=h[:, kc, :], start=(kc == 0), stop=(kc == 7))
                return r
            P.op("pe", mm, reads=[("h", t % 2, kc) for kc in range(8)] + [("wz", f // 4)], writes=[pk])
            zs = f % 2
            P.op("act", lambda e, ps=ps, zs=zs: e.activation(out=szt[zs][:], in_=ps[:], func=AF.Silu), reads=[pk], writes=[("szt", zs)])
            P.op("dve", lambda e, zs=zs, f=f, os_=os_, y=y: e.tensor_tensor(out=y[:, f, :], in0=szt[zs][:], in1=os_[:, f, :], op=ALU.mult),
                 reads=[("szt", zs), (okey, f // 4)], writes=[("y", t % 2, f)])

    def C2(t):
        xs, xkey = xt[t % 3], "xt%d" % (t % 3)
        y = yb[t % 2]
        for dc in range(8):
            ps, pk = banks.next()

            def mm(e, ps=ps, dc=dc, y=y):
                for kc in range(16):
                    r = e.matmul(ps[:], lhsT=wo[:, kc, dc * 128:(dc + 1) * 128], rhs=y[:, kc, :], start=(kc == 0), stop=(kc == 15))
                return r
            P.op("pe", mm, reads=[("y", t % 2, kc) for kc in range(16)] + [("wo", 0), ("wo", 1)], writes=[pk])
            P.op("dve", lambda e, ps=ps, dc=dc, xs=xs: e.tensor_tensor(out=xs[:, :, dc, :], in0=v2(ps[:]), in1=xs[:, :, dc, :], op=ALU.add),
                 reads=[pk, xkey], writes=[xkey])

    def C3(t):
        xs, xkey = xt[t % 3], "xt%d" % (t % 3)
        stats(xs, xkey)
        for kc in range(8):
            P.op("dve", lambda e, kc=kc, xs=xs: e.scalar_tensor_tensor(
                out=xs[:, :, kc, :], in0=xs[:, :, kc, :], scalar=vec[:, V_GF + kc:V_GF + kc + 1], in1=v2(inv[:]),
                op0=ALU.mult, op1=ALU.mult), reads=[xkey, "rt", "vec"], writes=[xkey])
        P.dma("pool", outT[:, 2 * t:2 * t + 2, :, :], xs[:], reads=[xkey],
              writes=[("outT", t)], sem="outst", final=True)

    C0(0)
    wload(False)
    C1(0)
    for t in range(NT):
        if t + 1 < NT:
            C0(t + 1)
        C2(t)
        if t + 1 < NT:
            C1(t + 1)
        C3(t)


def _dram(nc, name, shape, dt, kind):
    return nc.dram_tensor(name, list(shape), dt, kind=kind).ap()


def build_A():
    nc = bass.Bass("TRN2", target_bir_lowering=False)
    io = {
        "xT": _dram(nc, "xT", [128, 8, 8, 256], F32, "ExternalInput"),
        "xh": _dram(nc, "xh", [128, 8, HALO], F32, "ExternalInput"),
        "vec": _dram(nc, "vec", [128, NV], F32, "ExternalInput"),
        "rc": _dram(nc, "rc", [128, 64], F32, "ExternalInput"),
        "pool_w_in": _dram(nc, "pool_w_in", [128, 4, 8, 1024], F32, "ExternalInput"),
        "pool_w_group": _dram(nc, "pool_w_group", [128, 4, 4, 512], F32, "ExternalInput"),
        "pool_w_out": _dram(nc, "pool_w_out", [128, 8, 2048], F32, "ExternalInput"),
        "wl": _dram(nc, "wl", [128, 8, LATR], F32, "ExternalInput"),
        "x1T": _dram(nc, "x1T", [128, 8, 8, 256], F32, "ExternalOutput"),
        "lat": _dram(nc, "lat", [8 * 128, LATS * 256], BF16, "ExternalOutput"),
    }
    with ExitStack() as es:
        P = Prog(nc, es)
        phase_A(nc, P, io)
        P.emit()
    return nc


def build_B():
    nc = bass.Bass("TRN2", target_bir_lowering=False)
    io = {
        "latA": _dram(nc, "latA", [4 * 8 * 128, LATS * 256], BF16, "ExternalInput"),
        "posr": _dram(nc, "posr", [128, S], I32, "ExternalInput"),
        "rcol": _dram(nc, "rcol", [128, 2], F32, "ExternalInput"),
        "tri": _dram(nc, "tri", [128, 128], F32, "ExternalInput"),
        "wq": _dram(nc, "wq", [128, 3, HPC * 320], F32, "ExternalInput"),
        "wk": _dram(nc, "wk", [128, 2, HPC * 128], F32, "ExternalInput"),
        "wv": _dram(nc, "wv", [128, 2, HPC * 128], F32, "ExternalInput"),
        "oT": _dram(nc, "oT", [16 * 128, HPC * 512], BF16, "ExternalOutput"),
        "cs": nc.dram_tensor("cs", [16, 64, 2 * 512], F32).ap(),
    }
    with ExitStack() as es:
        P = Prog(nc, es)
        phase_B(nc, P, io)
        P.emit()
    return nc


def build_C():
    nc = bass.Bass("TRN2", target_bir_lowering=False)
    oTo = _dram(nc, "oTo", [4 * 4 * 128, 2048], BF16, "ExternalInput")
    io = {
        "x1T": _dram(nc, "x1T", [128, 8, 8, 256], F32, "ExternalInput"),
        "oTo": oTo,
        "oTo_dma": lambda P, t, rr, dst, key: P.dma("sp", dst, oTo[(t * 4 + rr) * 128:(t * 4 + rr + 1) * 128, :],
                                                  reads=["oTo"], writes=[key], sem=key),
        "vec": _dram(nc, "vec", [128, NV], F32, "ExternalInput"),
        "wz": _dram(nc, "wz", [128, 8, 2048], F32, "ExternalInput"),
        "mla_w_out": _dram(nc, "mla_w_out", [128, 8, 2048], F32, "ExternalInput"),
        "outT": _dram(nc, "outT", [128, 8, 8, 256], F32, "ExternalOutput"),
    }
    with ExitStack() as es:
        P = Prog(nc, es)
        phase_C(nc, P, io)
        P.emit()
    return nc


def build_fused():
    nc = bass.Bass("TRN2", target_bir_lowering=False)
    ein = lambda name, shape, dt: _dram(nc, name, shape, dt, "ExternalInput")
    x1s = nc.dram_tensor("x1s", [128, 8, 8, 256], F32).ap()
    lat_own = nc.dram_tensor("lat_own", [8 * 128, LATS * 256], BF16).ap()
    latA = nc.dram_tensor("latA", [4 * 8 * 128, LATS * 256], BF16).ap()
    o_own = nc.dram_tensor("o_own", [16 * 128, HPC * 512], BF16).ap()
    oA = nc.dram_tensor("oA", [4 * 16 * 128, HPC * 512], BF16).ap()
    vec = ein("vec", [128, NV], F32)
    groups = [[0, 1, 2, 3], [4, 5, 6, 7]]
    ioA = {
        "xT": ein("xT", [128, 8, 8, 256], F32), "xh": ein("xh", [128, 8, HALO], F32), "vec": vec,
        "rc": ein("rc", [128, 64], F32), "pool_w_in": ein("pool_w_in", [128, 4, 8, 1024], F32),
        "pool_w_group": ein("pool_w_group", [128, 4, 4, 512], F32), "pool_w_out": ein("pool_w_out", [128, 8, 2048], F32),
        "wl": ein("wl", [128, 8, LATR], F32), "x1T": x1s, "lat": lat_own,
    }
    ioB = {
        "latA": latA, "posr": ein("posr", [128, S], I32), "rcol": ein("rcol", [128, 2], F32), "tri": ein("tri", [128, 128], F32),
        "wq": ein("wq", [128, 3, HPC * 320], F32), "wk": ein("wk", [128, 2, HPC * 128], F32),
        "wv": ein("wv", [128, 2, HPC * 128], F32), "oT": o_own, "cs": nc.dram_tensor("cs", [16, 64, 2 * 512], F32).ap(),
    }

    def ag1(P, k):
        P.custom_dma("pool", lambda e: e.collective_compute(
            "AllGather", ALU.bypass, replica_groups=groups,
            ins=[lat_own[k * 256:(k + 1) * 256, :]], outs=[latA[k * 1024:(k + 1) * 1024, :]]),
            reads=[("lat", 2 * k), ("lat", 2 * k + 1)], writes=[("latA", k)], sem=("ag1", k), inc=1)

    def ag2(P, m):
        P.custom_dma("pool", lambda e: e.collective_compute(
            "AllGather", ALU.bypass, replica_groups=groups,
            ins=[o_own[m * 256:(m + 1) * 256, :]], outs=[oA[m * 1024:(m + 1) * 1024, :]]),
            reads=[("oT", 2 * m), ("oT", 2 * m + 1)], writes=[("oTo", m)], sem=("ag2", m), inc=1)

    ioA["ag1"] = ag1
    ioB["ag2"] = ag2
    wz_bf = nc.dram_tensor("wz_bf", [128, 8, 2048], BF16).ap()
    wo_bf = nc.dram_tensor("wo_bf", [128, 8, 2048], BF16).ap()

    def prefetch(P, i):
        name, dst = (("wz", wz_bf), ("mla_w_out", wo_bf))[i // 8]
        k = i % 8
        P.dma("pool", dst[:, k, :], wsrc[name][:, k, :], writes=[("pf", i)], sem=("pf", i % 2))
    ioB["prefetch"] = prefetch

    ag2_tok = {}

    def oTo_dma(P, t, rr, dst, key):
        def emit(e, sem):
            core = e.partition_id()
            for k in range(8):
                m = (k % 4) * 2 + t // 2
                row = m * 1024 + rr * 256 + (t % 2) * 128
                with e.If(core == k):
                    e.wait_ge(ag2_tok[m][0], ag2_tok[m][1])
                    e.dma_start(out=dst, in_=oA[row:row + 128, :]).then_inc(sem, 16)
        P.custom_dma("sp", emit, writes=[key], sem=key, self_inc=True)

    wsrc = {"wz": ein("wz", [128, 8, 2048], F32), "mla_w_out": ein("mla_w_out", [128, 8, 2048], F32)}
    ioC = {
        "x1T": x1s, "oTo": oA, "oTo_dma": oTo_dma, "vec": vec, "wz": wsrc["wz"],
        "mla_w_out": wsrc["mla_w_out"], "wz_bf": wz_bf, "wo_bf": wo_bf,
        "outT": _dram(nc, "outT", [128, 8, 8, 256], F32, "ExternalOutput"),
    }
    with ExitStack() as sem_es:
        with ExitStack() as es:
            PA = Prog(nc, es, sem_es)
            phase_A(nc, PA, ioA)
            PA.barrier(skip="ag1")
            PA.emit()
        with ExitStack() as es:
            PB = Prog(nc, es, sem_es)
            for k in range(4):
                PB.extern(PA.sems[("dma", ("ag1", k))], PA.dma_cnt[("ag1", k)], [("latA", k)])
            phase_B(nc, PB, ioB)
            PB.barrier(skip="ag2")
            PB.emit()
        with ExitStack() as es:
            PC = Prog(nc, es, sem_es)
            for m in range(8):
                ag2_tok[m] = (PB.sems[("dma", ("ag2", m))], PB.dma_cnt[("ag2", m)])
                PC.extern(ag2_tok[m][0], ag2_tok[m][1], [("oTo", m)])
            phase_C(nc, PC, ioC)
            PC.barrier()
            PC.emit()
    return nc


def _cols(v):
    return np.ascontiguousarray(np.asarray(v, np.float32).reshape(-1, 128).T)


def _pm(w):
    nk = w.shape[0] // 128
    return np.ascontiguousarray(w.reshape(nk, 128, w.shape[1]).transpose(1, 0, 2))


def host_inputs(inp):
    f = lambda k: np.asarray(inp[k], np.float32)
    x = f("x")
    vec = np.concatenate([_cols(f("pool_norm")[0]), _cols(f("mla_norm")[0]), _cols(f("pool_scale")[0]),
                          _cols(f("mla_q_norm")[0]), _cols(f("mla_kv_norm")[0]), _cols(f("final_norm"))], axis=1)
    assert vec.shape == (128, NV)
    perm = np.concatenate([np.concatenate([np.arange(g * 512, (g + 1) * 512), 2048 + np.arange(g * 512, (g + 1) * 512)])
                           for g in range(4)])
    w_in_p = _pm(f("pool_w_in")[0][:, perm]).reshape(128, 8, 4, 1024).transpose(0, 2, 1, 3)
    w_in_p = np.ascontiguousarray(w_in_p)
    wg_p = np.ascontiguousarray(f("pool_w_group")[0].reshape(4, 4, 128, 512).transpose(2, 0, 1, 3))
    wo0_p = _pm(f("pool_w_out")[0]).reshape(128, 8, 2048)
    wo1_p = _pm(f("mla_w_out")[0]).reshape(128, 8, 2048)
    mw = f("mla_w_in")[0]
    wl_p = _pm(np.concatenate([mw[:, :704], mw[:, 672:704], mw[:, 640:672], mw[:, 640:704]], axis=1))
    wz_p = _pm(mw[:, 704:])
    wqb, wkvb = f("mla_w_q_b")[0], f("mla_w_kv_b")[0]
    invf = (np.float32(1.0) / (np.float32(10000.0) ** (np.arange(0, 64, 2, dtype=np.float32) / np.float32(64)))).astype(np.float32)
    rcol = np.stack([np.tile(invf, 4), np.tile(np.concatenate([-np.ones(32, np.float32), np.ones(32, np.float32)]), 2)], axis=1)
    tri = (np.arange(128)[None, :] >= np.arange(128)[:, None]).astype(np.float32)
    A, Bm, C = [], [], []
    for c in range(8):
        b, r = c // 4, c % 4
        t0 = r * TOK
        xT = np.ascontiguousarray(x[b, t0:t0 + TOK].reshape(8, 256, 8, 128).transpose(3, 0, 2, 1))
        xh = np.zeros((HALO, D), np.float32)
        if t0 > 0:
            xh[:] = x[b, t0 - HALO:t0]
        xh = np.ascontiguousarray(xh.reshape(HALO, 8, 128).transpose(2, 1, 0))
        rc = np.zeros((128, 64), np.float32)
        for g, w in enumerate((2, 4, 8, 16)):
            rc[:, g * 16:(g + 1) * 16] = 1.0 / np.minimum(t0 + np.arange(16) + 1, w).astype(np.float32)
        A.append({"xT": xT, "xh": xh, "vec": vec, "rc": rc, "pool_w_in": w_in_p,
                  "pool_w_group": wg_p, "pool_w_out": wo0_p, "wl": wl_p})
        hs = [4 * r + hl for hl in range(4)]
        wq = np.concatenate([np.concatenate([wqb[:, h * 192:h * 192 + 192], wqb[:, h * 192 + 160:h * 192 + 192],
                                             wqb[:, h * 192 + 128:h * 192 + 160], wqb[:, h * 192 + 128:h * 192 + 192]], axis=1)
                             for h in hs], axis=1)
        wk = np.concatenate([wkvb[:, h * 256:h * 256 + 128] for h in hs], axis=1)
        wv = np.concatenate([wkvb[:, h * 256 + 128:h * 256 + 256] for h in hs], axis=1)
        posr = np.ascontiguousarray(np.broadcast_to(np.asarray(inp["positions"])[b].astype(np.int32)[None, :], (128, S)))
        Bm.append({"posr": posr, "rcol": rcol, "tri": tri, "wq": _pm(wq), "wk": _pm(wk), "wv": _pm(wv)})
        C.append({"vec": vec, "wz": wz_p, "mla_w_out": wo1_p})
    return A, Bm, C


def _assemble(res):
    out = np.empty((NB, S, D), np.float32)
    for c in range(8):
        b, r = c // 4, c % 4
        o = res[c]["outT"]
        out[b, r * TOK:(r + 1) * TOK, :] = o.transpose(1, 3, 2, 0).reshape(TOK, D)
    return out


_NC = {}


def _get(name, fn):
    if name not in _NC:
        _NC[name] = fn()
    return _NC[name]


FUSED = True


def kernel(**inputs):
    A, Bm, C = host_inputs(inputs)
    cores = list(range(8))
    if FUSED:
        maps = []
        for c in cores:
            m = {}
            m.update(A[c]); m.update(Bm[c]); m.update(C[c])
            maps.append(m)
        res = run_bass_kernel_spmd(_get("F", build_fused), maps, core_ids=cores).results
        return _assemble(res)
    ra = run_bass_kernel_spmd(_get("A", build_A), A, core_ids=cores).results
    for c in cores:
        b = c // 4
        Bm[c]["latA"] = np.concatenate([ra[b * 4 + r]["lat"][k * 256:(k + 1) * 256] for k in range(4) for r in range(4)], axis=0)
    rb = run_bass_kernel_spmd(_get("B", build_B), Bm, core_ids=cores).results
    for c in cores:
        b, r = c // 4, c % 4
        C[c]["x1T"] = ra[c]["x1T"]
        C[c]["oTo"] = np.ascontiguousarray(np.concatenate(
            [rb[b * 4 + rr]["oT"][(r * 4 + t) * 128:(r * 4 + t + 1) * 128, :] for t in range(4) for rr in range(4)], axis=0))
    rc = run_bass_kernel_spmd(_get("C", build_C), C, core_ids=cores).results
    return _assemble(rc)
```

```python
import numpy as np
from contextlib import ExitStack
import concourse.bass as bass
import concourse.mybir as mybir
from concourse.bass_utils import run_bass_kernel_spmd

F32 = mybir.dt.float32
BF16 = mybir.dt.bfloat16
I32 = mybir.dt.int32
AF = mybir.ActivationFunctionType
ALU = mybir.AluOpType


class _Op:
    __slots__ = ("eng", "emit", "deps", "is_dma", "semkey", "signal", "count", "waits", "final", "inc", "self_inc", "ext")

    def __init__(self):
        self.inc, self.self_inc, self.ext = 16, False, None


class Prog:
    COMPUTE = ("pe", "act", "dve", "pool")
    NSEM = 0
    NPROG = 0

    def __init__(self, nc, es, sem_es=None):
        self.nc, self.es = nc, es
        self.sem_es = sem_es if sem_es is not None else es
        Prog.NPROG += 1
        self.uid = Prog.NPROG
        self.ops = []
        self.last_writer = {}
        self.readers = {}
        self.last_dma_on_sem = {}
        self.nsb = 0

    def sbuf(self, name, shape, dtype):
        return self.es.enter_context(self.nc.sbuf_tensor("sb%d_" % self.uid + name, list(shape), dtype))

    def psum(self, name, shape, dtype):
        return self.es.enter_context(self.nc.psum_tensor("pp%d_" % self.uid + name, list(shape), dtype))

    def _record(self, op, reads, writes):
        idx = len(self.ops)
        deps = {}

        def add(i, kind):
            if i is not None:
                deps[i] = deps.get(i, 0) | kind

        for k in reads:
            add(self.last_writer.get(k), 1)
        for k in writes:
            add(self.last_writer.get(k), 2)
            for i in self.readers.get(k, ()):
                add(i, 4)
        deps.pop(idx, None)
        op.deps = deps
        self.ops.append(op)
        for k in writes:
            self.last_writer[k] = idx
            self.readers[k] = []
        for k in reads:
            lst = self.readers.setdefault(k, [])
            if not op.is_dma:
                lst[:] = [i for i in lst if self.ops[i].is_dma or self.ops[i].eng != op.eng]
            lst.append(idx)
        return idx

    def op(self, eng, emit, reads=(), writes=()):
        o = _Op()
        o.eng, o.emit, o.is_dma, o.semkey, o.signal, o.count, o.final = eng, emit, False, None, False, 0, False
        return self._record(o, list(reads), list(writes))

    def dma(self, queue, out, in_, reads=(), writes=(), sem=None, final=False, **kw):
        o = _Op()
        o.eng, o.is_dma, o.semkey, o.signal, o.count, o.final = queue, True, sem, True, 0, final
        o.emit = lambda e: e.dma_start(out=out, in_=in_, **kw)
        idx = self._record(o, list(reads), list(writes))
        prev = self.last_dma_on_sem.get(sem)
        if prev is not None:
            o.deps[prev] = o.deps.get(prev, 0) | 2
        self.last_dma_on_sem[sem] = idx
        if queue == "pool":
            self.pool_dmas = getattr(self, "pool_dmas", [])
            if len(self.pool_dmas) >= 2:
                o.deps[self.pool_dmas[-2]] = o.deps.get(self.pool_dmas[-2], 0) | 2
            self.pool_dmas.append(idx)
        return idx

    def custom_dma(self, queue, emit, reads=(), writes=(), sem=None, final=False, inc=16, self_inc=False):
        o = _Op()
        o.eng, o.is_dma, o.semkey, o.signal, o.count, o.final = queue, True, sem, True, 0, final
        o.inc, o.self_inc = inc, self_inc
        o.emit = emit
        idx = self._record(o, list(reads), list(writes))
        prev = self.last_dma_on_sem.get(sem)
        if prev is not None:
            o.deps[prev] = o.deps.get(prev, 0) | 2
        self.last_dma_on_sem[sem] = idx
        return idx

    def extern(self, sem, count, writes):
        o = _Op()
        o.eng, o.is_dma, o.semkey, o.signal, o.count, o.final = "sp", True, ("ext", len(self.ops)), True, count, False
        o.ext, o.emit = sem, None
        return self._record(o, [], list(writes))

    def barrier(self, skip=None):
        last = {}
        for i, o in enumerate(self.ops):
            if o.is_dma and skip is not None and isinstance(o.semkey, tuple) and o.semkey[0] == skip:
                continue
            last[("dma", o.semkey) if o.is_dma else ("eng", o.eng)] = i
        for eng in ("pe", "act", "dve", "pool", "sp"):
            o = _Op()
            o.eng, o.is_dma, o.semkey, o.signal, o.count, o.final = eng, False, None, False, 0, False
            o.emit = lambda e: e.nop()
            o.deps = {i: 1 for k, i in last.items() if k != ("eng", eng)}
            self.ops.append(o)

    def emit(self):
        nc, es, ops = self.nc, self.es, self.ops
        for o in ops:
            keep = []
            for i, kind in o.deps.items():
                p = ops[i]
                if not p.is_dma and not o.is_dma and p.eng == o.eng:
                    if o.eng == "pe":
                        continue
                if not p.is_dma and o.is_dma and p.eng == o.eng and not (kind & 3):
                    pass
                keep.append(i)
                p.signal = True
            o.deps = keep
        eng_cnt = {e: 0 for e in ("pe", "act", "dve", "pool", "sp")}
        dma_cnt = {}
        for o in ops:
            if o.ext is not None:
                continue
            if o.is_dma:
                dma_cnt[o.semkey] = dma_cnt.get(o.semkey, 0) + o.inc
                o.count = dma_cnt[o.semkey]
            elif o.signal:
                eng_cnt[o.eng] += 1
                o.count = eng_cnt[o.eng]
        sems = {}
        for e in self.COMPUTE:
            if eng_cnt[e]:
                Prog.NSEM += 1
                sems[("eng", e)] = self.sem_es.enter_context(nc.semaphore("sem%d" % Prog.NSEM))
        for k in dma_cnt:
            Prog.NSEM += 1
            sems[("dma", k)] = self.sem_es.enter_context(nc.semaphore("sem%d" % Prog.NSEM))
        for o in ops:
            if o.ext is not None:
                sems[("dma", o.semkey)] = o.ext
        self.sems, self.dma_cnt = sems, dma_cnt

        def tok(p):
            return (sems[("dma", p.semkey)] if p.is_dma else sems[("eng", p.eng)], p.count,
                    ("dma", p.semkey) if p.is_dma else ("eng", p.eng))

        waited = {e: {} for e in eng_cnt}
        for o in ops:
            w = {}
            for i in o.deps:
                s, c, k = tok(ops[i])
                if c > waited[o.eng].get(k, 0) and c > w.get(k, (None, 0))[1]:
                    w[k] = (s, c)
            for k, (s, c) in w.items():
                waited[o.eng][k] = c
            o.waits = list(w.values())
        finals = [(sems[("dma", k)], dma_cnt[k]) for k in dma_cnt
                  if any(o.final and o.semkey == k for o in ops if o.is_dma)]
        by_eng = {e: [o for o in ops if o.eng == e] for e in eng_cnt}

        def run(ename, e):
            for o in by_eng[ename]:
                if o.ext is not None:
                    continue
                for s, c in o.waits:
                    e.wait_ge(s, c)
                if o.self_inc:
                    o.emit(e, sems[("dma", o.semkey)])
                    continue
                inst = o.emit(e)
                if inst is None:
                    continue
                if o.is_dma:
                    inst.then_inc(sems[("dma", o.semkey)], o.inc)
                elif o.signal:
                    inst.then_inc(sems[("eng", o.eng)], 1)
            if ename == "sp":
                for s, c in finals:
                    e.wait_ge(s, c)

        with nc.Block() as block:
            @block.tensor
            def _(e):
                run("pe", e)

            @block.scalar
            def _(e):
                run("act", e)

            @block.vector
            def _(e):
                run("dve", e)

            @block.gpsimd
            def _(e):
                run("pool", e)

            @block.sync
            def _(e):
                run("sp", e)


D = 1024
S = 8192
NB = 2
TOK = 2048
HALO = 16
NH = 16
HPC = 4
EPS = 1e-6
LATS = 7
LATR = 832
SCALE = 192 ** -0.5

V_G0, V_G1, V_SC, V_GQ, V_GKV, V_GF, NV = 0, 8, 16, 32, 35, 37, 45


class Banks:
    def __init__(self, P, n=8, prefix="ps"):
        self.t = [P.psum("%s%d" % (prefix, i), [128, 512], F32) for i in range(n)]
        self.keys = [(prefix, i) for i in range(n)]
        self.i = 0

    def next(self):
        i = self.i
        self.i = (self.i + 1) % len(self.t)
        return self.t[i], self.keys[i]


def rms_inv(P, banks, ones, src_sq, nk, W, nfeat, rt, inv, keys_sq, key_rt, key_inv):
    ps, pk = banks.next()

    def mm(e):
        for k in range(nk):
            r = e.matmul(ps[:, :W], lhsT=ones[:], rhs=src_sq[:, k, :W], start=(k == 0), stop=(k == nk - 1))
        return r
    P.op("pe", mm, reads=list(keys_sq) + ["ones"], writes=[pk])
    P.op("act", lambda e: e.activation(out=rt[:, :W], in_=ps[:, :W], func=AF.Ln, bias=EPS, scale=1.0 / nfeat),
         reads=[pk], writes=[key_rt])
    P.op("act", lambda e: e.activation(out=inv[:, :W], in_=rt[:, :W], func=AF.Exp, scale=-0.5), reads=[key_rt], writes=[key_inv])


def phase_A(nc, P, io):
    TT, NT = 256, TOK // 256
    xT, x1T, lat = io["xT"], io["x1T"], io["lat"]
    banks = Banks(P)
    ones = P.sbuf("ones", [128, 128], BF16)
    P.op("pool", lambda e: e.memset(ones[:], 1.0), writes=["ones"])
    vec = P.sbuf("vec", [128, NV], F32)
    P.dma("sp", vec[:], io["vec"], writes=["vec"], sem="vec")
    rc = P.sbuf("rc", [128, 64], F32)
    P.dma("sp", rc[:], io["rc"], writes=["rc"], sem="rc")

    w_in = P.sbuf("w_in", [128, 8, 4096], BF16)
    wg = P.sbuf("wg", [128, 4, 4, 512], BF16)
    wo = P.sbuf("wo", [128, 16, 1024], BF16)
    wl = P.sbuf("wl", [128, 8, LATR], BF16)
    for g in range(4):
        P.dma("pool", w_in[:, :, g * 1024:(g + 1) * 1024], io["pool_w_in"][:, g, :, :],
              writes=[("w_in", g)], sem=("w_in", g))
        P.dma("pool", wg[:, g, :, :], io["pool_w_group"][:, g, :, :], writes=[("wg", g)], sem=("wg", g))
    for h2 in range(2):
        P.dma("pool", wo[:, h2 * 8:(h2 + 1) * 8, :].rearrange("p (a b) f -> p a (b f)", b=2),
              io["pool_w_out"][:, h2 * 4:(h2 + 1) * 4, :], writes=[("wo", h2)], sem=("wo", h2))
    P.dma("pool", wl[:], io["wl"], writes=["wl"], sem="wl")

    xt = [P.sbuf("xt%d" % i, [128, 8, TT], F32) for i in range(2)]
    sq = P.sbuf("sq", [128, 8, TT], BF16)
    rt = P.sbuf("rt", [128, TT], F32)
    inv = P.sbuf("inv", [128, TT], F32)
    hb = [P.sbuf("h%d" % i, [128, 8, TT], BF16) for i in range(3)]
    NU = 3
    u = [P.sbuf("u%d" % i, [128, HALO + TT], F32) for i in range(NU)]
    pa = P.sbuf("pa", [128, HALO + TT], F32)
    pb = P.sbuf("pb", [128, HALO + TT], F32)
    tmp16 = P.sbuf("tmp16", [128, 16], F32)
    pooled = [P.sbuf("pooled%d" % i, [128, 4, TT], BF16) for i in range(2)]
    sz = [P.sbuf("sz%d" % i, [128, 4, TT], BF16) for i in range(2)]
    yb = [P.sbuf("y%d" % i, [128, 16, TT], BF16) for i in range(2)]
    uh = [P.sbuf("uh%d" % i, [128, 16, HALO], F32) for i in range(2)]
    ql = P.sbuf("ql", [128, 5, TT], F32)
    sql = P.sbuf("sql", [128, 5, TT], BF16)
    lat_o = [P.sbuf("lat_o%d" % i, [128, LATS, TT], BF16) for i in range(2)]
    for i in range(2):
        P.op("pool", lambda e, i=i: e.memset(lat_o[i][:], 0.0), writes=[("lat_o%d" % i, j) for j in range(LATS)])
    ucount = [0]

    def norm_h(xs, xkey, W, gcol0, hi):
        h = hb[hi]
        P.op("act", lambda e: e.activation(out=sq[:, :, :W], in_=xs[:, :, :W], func=AF.Square),
             reads=[xkey], writes=["sq"])
        rms_inv(P, banks, ones, sq, 8, W, D, rt, inv, ["sq"], "rt", "inv")
        for kc in range(8):
            P.op("dve", lambda e, kc=kc: e.scalar_tensor_tensor(
                out=h[:, kc, :W], in0=xs[:, kc, :W], scalar=vec[:, gcol0 + kc:gcol0 + kc + 1],
                in1=inv[:, :W], op0=ALU.mult, op1=ALU.mult),
                reads=[xkey, "inv", "vec"], writes=[("h", hi, kc)])

    def unit8(wkeys, lhs_of, W, hi):
        ps, pk = banks.next()
        h = hb[hi]

        def mm(e):
            for kc in range(8):
                r = e.matmul(ps[:, :W], lhsT=lhs_of(kc), rhs=h[:, kc, :W], start=(kc == 0), stop=(kc == 7))
            return r
        P.op("pe", mm, reads=[("h", hi, kc) for kc in range(8)] + list(wkeys), writes=[pk])
        return ps, pk

    xh = xt[1]
    P.dma("sp", xh[:, :, :HALO], io["xh"], writes=["xt1"], sem="xt1")
    norm_h(xh, "xt1", HALO, V_G0, 2)

    def halo(g):
        for f in range(4 * g, 4 * g + 4):
            ps, pk = unit8([("w_in", g)], lambda kc, f=f, g=g: w_in[:, kc, g * 1024 + (f % 4) * 128:g * 1024 + (f % 4 + 1) * 128], HALO, 2)
            P.op("act", lambda e, ps=ps, f=f: e.activation(out=uh[0][:, f, :], in_=ps[:, :HALO], func=AF.Copy),
                 reads=[pk], writes=[("uh", 0, f)])

    def F0(t):
        xs, xkey = xt[t % 2], "xt%d" % (t % 2)
        P.dma("sp", xs[:], xT[:, t, :, :], writes=[xkey], sem=xkey)
        norm_h(xs, xkey, TT, V_G0, t % 2)

    def F1(t, g):
        uhc, uhn = uh[t % 2], uh[(t + 1) % 2]
        slot = (t * 4 + g) % 2
        w = 2 << g
        hi = t % 2
        for fc in range(4):
            f = g * 4 + fc
            us = u[ucount[0] % NU]
            ukey = "u%d" % (ucount[0] % NU)
            ucount[0] += 1
            ps, pk = unit8([("w_in", g)], lambda kc, fc=fc, g=g: w_in[:, kc, g * 1024 + fc * 128:g * 1024 + (fc + 1) * 128], TT, hi)
            P.op("act", lambda e, ps=ps, us=us: e.activation(out=us[:, HALO:], in_=ps[:, :TT], func=AF.Copy),
                 reads=[pk], writes=[ukey])
            P.op("act", lambda e, ps=ps, f=f, uhn=uhn: e.activation(out=uhn[:, f, :], in_=ps[:, TT - HALO:TT], func=AF.Copy),
                 reads=[pk], writes=[("uh", (t + 1) % 2, f)])
            P.op("act", lambda e, us=us, f=f, uhc=uhc: e.activation(out=us[:, :HALO], in_=uhc[:, f, :], func=AF.Copy),
                 reads=[("uh", t % 2, f)], writes=[ukey])
            ps, pk = unit8([("w_in", g)], lambda kc, fc=fc, g=g: w_in[:, kc, g * 1024 + 512 + fc * 128:g * 1024 + 512 + (fc + 1) * 128], TT, hi)
            P.op("act", lambda e, ps=ps, slot=slot, fc=fc: e.activation(out=sz[slot][:, fc, :], in_=ps[:, :TT], func=AF.Silu),
                 reads=[pk], writes=[("sz", slot, fc)])
            E = HALO + TT
            src, skey = us, ukey
            bufs = [(pa, "pa"), (pb, "pb")]
            step, lo, bi = 1, 1, 0
            while step < w:
                dst, dkey = bufs[bi]
                P.op("dve", lambda e, dst=dst, src=src, step=step, lo=lo: e.tensor_tensor(
                    out=dst[:, lo:E], in0=src[:, lo:E], in1=src[:, lo - step:E - step], op=ALU.add),
                    reads=[skey], writes=[dkey])
                src, skey = dst, dkey
                step *= 2
                lo = 2 * step - 1
                bi ^= 1
            P.op("dve", lambda e, src=src, us=us, slot=slot, fc=fc, w=w: e.scalar_tensor_tensor(
                out=pooled[slot][:, fc, :], in0=src[:, HALO:E], scalar=1.0 / w, in1=us[:, HALO:E],
                op0=ALU.mult, op1=ALU.subtract),
                reads=[skey, ukey], writes=[("pooled", slot, fc)])
            if t == 0:
                P.op("dve", lambda e, src=src, g=g: e.tensor_tensor(
                    out=tmp16[:], in0=src[:, HALO:HALO + 16], in1=rc[:, g * 16:(g + 1) * 16], op=ALU.mult),
                    reads=[skey, "rc"], writes=["tmp16"])
                P.op("dve", lambda e, us=us, slot=slot, fc=fc: e.tensor_tensor(
                    out=pooled[slot][:, fc, 0:16], in0=tmp16[:], in1=us[:, HALO:HALO + 16], op=ALU.subtract),
                    reads=["tmp16", ukey], writes=[("pooled", slot, fc)])

    def F2(t, g):
        slot = (t * 4 + g) % 2
        y, yk = yb[t % 2], t % 2
        for fo in range(4):
            f = g * 4 + fo
            ps, pk = banks.next()

            def mm(e, ps=ps, g=g, fo=fo, slot=slot):
                for kc in range(4):
                    r = e.matmul(ps[:, :TT], lhsT=wg[:, g, kc, fo * 128:(fo + 1) * 128],
                                 rhs=pooled[slot][:, kc, :], start=(kc == 0), stop=(kc == 3))
                return r
            P.op("pe", mm, reads=[("pooled", slot, kc) for kc in range(4)] + [("wg", g)], writes=[pk])
            P.op("dve", lambda e, ps=ps, f=f, fo=fo, slot=slot, y=y: e.scalar_tensor_tensor(
                out=y[:, f, :], in0=ps[:, :TT], scalar=vec[:, V_SC + f:V_SC + f + 1], in1=sz[slot][:, fo, :],
                op0=ALU.mult, op1=ALU.mult),
                reads=[pk, ("sz", slot, fo), "vec"], writes=[("y", yk, f)])

    def F3(t):
        xs, xkey = xt[t % 2], "xt%d" % (t % 2)
        y, yk = yb[t % 2], t % 2
        for dc in range(8):
            ps, pk = banks.next()

            def mm(e, ps=ps, dc=dc, y=y):
                for kc in range(16):
                    r = e.matmul(ps[:, :TT], lhsT=wo[:, kc, dc * 128:(dc + 1) * 128], rhs=y[:, kc, :],
                                 start=(kc == 0), stop=(kc == 15))
                return r
            P.op("pe", mm, reads=[("y", yk, kc) for kc in range(16)] + [("wo", 0), ("wo", 1)], writes=[pk])
            P.op("dve", lambda e, ps=ps, dc=dc, xs=xs: e.tensor_tensor(
                out=xs[:, dc, :], in0=ps[:, :TT], in1=xs[:, dc, :], op=ALU.add),
                reads=[pk, xkey], writes=[xkey])

    def F4(t):
        xs, xkey = xt[t % 2], "xt%d" % (t % 2)
        P.dma("pool", x1T[:, t, :, :], xs[:], reads=[xkey], writes=[("x1T", t)],
              sem=("x1st", t % 2), final=True)
        norm_h(xs, xkey, TT, V_G1, 2)

    def F5(t):
        lo_t, lkey = lat_o[t % 2], "lat_o%d" % (t % 2)
        h = hb[2]
        for j in range(5):
            ps, pk = unit8(["wl"], lambda kc, j=j: wl[:, kc, j * 128:(j + 1) * 128], TT, 2)
            P.op("act", lambda e, ps=ps, j=j: e.activation(out=ql[:, j, :], in_=ps[:, :TT], func=AF.Copy),
                 reads=[pk], writes=[("ql", j)])
        for j in (5, 6):
            ps, pk = unit8(["wl"], lambda kc, j=j: wl[:, kc, 640 + (j - 5) * 64:768 + (j - 5) * 64], TT, 2)
            P.op("act", lambda e, ps=ps, lo_t=lo_t, j=j: e.activation(out=lo_t[0:64, j, :], in_=ps[0:64, :TT], func=AF.Copy),
                 reads=[pk], writes=[(lkey, j)])
        for (j0, nj, nfeat, gc) in ((0, 3, 384, V_GQ), (3, 2, 256, V_GKV)):
            P.op("act", lambda e, j0=j0, nj=nj: e.activation(out=sql[:, j0:j0 + nj, :], in_=ql[:, j0:j0 + nj, :], func=AF.Square),
                 reads=[("ql", j) for j in range(j0, j0 + nj)], writes=[("sql", j0)])
            ps, pk = banks.next()

            def mm(e, ps=ps, j0=j0, nj=nj):
                for k in range(nj):
                    r = e.matmul(ps[:, :TT], lhsT=ones[:], rhs=sql[:, j0 + k, :], start=(k == 0), stop=(k == nj - 1))
                return r
            P.op("pe", mm, reads=[("sql", j0), "ones"], writes=[pk])
            P.op("act", lambda e, ps=ps, nfeat=nfeat: e.activation(out=rt[:, :TT], in_=ps[:, :TT], func=AF.Ln, bias=EPS, scale=1.0 / nfeat),
                 reads=[pk], writes=["rt"])
            P.op("act", lambda e: e.activation(out=inv[:, :TT], in_=rt[:, :TT], func=AF.Exp, scale=-0.5), reads=["rt"], writes=["inv"])
            for k in range(nj):
                j = j0 + k
                P.op("dve", lambda e, j=j, k=k, gc=gc, lo_t=lo_t: e.scalar_tensor_tensor(
                    out=lo_t[:, j, :], in0=ql[:, j, :], scalar=vec[:, gc + k:gc + k + 1], in1=inv[:, :TT],
                    op0=ALU.mult, op1=ALU.mult),
                    reads=[("ql", j), "inv", "vec"], writes=[(lkey, j)])
        P.dma("pool", lat[t * 128:(t + 1) * 128, :], lo_t[:].rearrange("p j c -> p (j c)"), reads=[(lkey, j) for j in range(LATS)],
              writes=[("lat", t)], sem=("latst", t % 2), final=True)
        if t % 2 == 1 and io.get("ag1") is not None:
            io["ag1"](P, t // 2)

    F0(0)
    for t in range(NT):
        prev = t - 1
        if t == 0:
            halo(0)
        F1(t, 0)
        if prev >= 0:
            F2(prev, 3)
        if t == 0:
            halo(1)
        F1(t, 1)
        if prev >= 0:
            F3(prev)
        F2(t, 0)
        if t == 0:
            halo(2)
        F1(t, 2)
        if prev >= 0:
            F4(prev)
        F2(t, 1)
        if t == 0:
            halo(3)
        F1(t, 3)
        if t + 1 < NT:
            F0(t + 1)
        if prev >= 0:
            F5(prev)
        F2(t, 2)
    F2(NT - 1, 3)
    F3(NT - 1)
    F4(NT - 1)
    F5(NT - 1)


PI = float(np.pi)
C1 = 6.28125
C2 = float(2.0 * np.pi - 6.28125)


def phase_B(nc, P, io):
    QT = 512
    NQ = S // QT
    latA, oT = io["latA"], io["oT"]
    ones = P.sbuf("ones", [128, 128], BF16)
    P.op("pool", lambda e: e.memset(ones[:], 1.0), writes=["ones"])
    tri = P.sbuf("tri", [128, 128], BF16)
    P.dma("pool", tri[:], io["tri"], writes=["tri"], sem="tri")
    rcol = P.sbuf("rcol", [128, 2], F32)
    P.dma("sp", rcol[:], io["rcol"], writes=["rcol"], sem="rcol")
    wq = P.sbuf("wq", [128, 3, HPC * 320], BF16)
    wk = P.sbuf("wk", [128, 2, HPC * 128], BF16)
    wv = P.sbuf("wv", [128, 2, HPC * 128], BF16)
    P.dma("pool", wk[:], io["wk"], writes=["wk"], sem="wk")
    P.dma("pool", wv[:], io["wv"], writes=["wv"], sem="wv")
    P.dma("pool", wq[:], io["wq"], writes=["wq"], sem="wq")

    KT = P.sbuf("KT", [128, HPC, S], BF16)
    V = P.sbuf("V", [128, S // 128, HPC * 128], BF16)
    KR = P.sbuf("KR", [128, S], BF16)
    pst = [P.psum("ps%d" % i, [128, 512], F32) for i in range(8)]
    pskey = [("ps", i) for i in range(8)]
    latA_v = latA.rearrange("(k r t p) (j c) -> p k r t j c", k=4, r=4, t=2, p=128, j=LATS)

    SB = [0, 1, 2, 7]
    OB = [3, 4]
    DB = [5, 6]
    sbi = [0]

    def sbank():
        b = SB[sbi[0] % 4]
        sbi[0] += 1
        return pst[b], pskey[b]

    posi = P.sbuf("posi", [128, QT], I32)
    ang = P.sbuf("ang", [128, QT], F32)
    kf = P.sbuf("kf", [128, QT], F32)
    rr = P.sbuf("rr", [128, QT], F32)
    CS = [P.sbuf("CS%d" % i, [128, QT], F32) for i in range(2)]
    Ct = [c[0:64, :] for c in CS]
    S0 = P.sbuf("S0", [64, QT], F32)
    kk = [P.sbuf("kk%d" % i, [64, 2, 2, QT // 2], BF16) for i in range(2)]

    def v2(ap):
        return ap.rearrange("p (t c) -> p t c", t=2)
    kvn = [P.sbuf("kvn%d" % i, [128, 2, 2, QT // 2], BF16) for i in range(2)]
    qn = [P.sbuf("qn%d" % i, [128, 2, 3, QT // 2], BF16) for i in range(2)]
    t1f = P.sbuf("t1", [128, QT], F32)
    bi = [0]
    ev = [0]

    def evac_copy(dst, src, rkeys, wkeys):
        ev[0] += 1
        if ev[0] % 2:
            P.op("act", lambda e: e.activation(out=dst, in_=src, func=AF.Copy), reads=rkeys, writes=wkeys)
        else:
            P.op("dve", lambda e: e.tensor_copy(out=dst, in_=src), reads=rkeys, writes=wkeys)

    def prep_steps(tt):
        r_ = tt // 4
        sl = tt % 2
        g0 = tt * QT
        ck = ("latA", tt % 4)
        steps = []

        def dve(fn, reads, writes):
            steps.append(lambda: P.op("dve", fn, reads=reads, writes=writes))

        def loads():
            P.dma("sp", posi[:], io["posr"][:, g0:g0 + QT], writes=["posi"], sem="posi")
            P.dma("sp", kk[sl][:], latA_v[0:64, tt % 4, r_, :, 5:7, :], reads=[ck], writes=[("kk", sl)], sem=("kk", sl))
            P.dma("sp", kvn[sl][:], latA_v[:, tt % 4, r_, :, 3:5, :], reads=[ck], writes=[("kvn", sl)], sem=("kvn", sl))
            P.dma("sp", qn[sl][:], latA_v[:, tt % 4, r_, :, 0:3, :], reads=[ck], writes=[("qn", sl)], sem=("qn", sl))
        steps.append(loads)
        for hl in range(HPC):
            def kstep(hl=hl):
                ps, pk = sbank()

                def mm(e, ps=ps, hl=hl, sl=sl):
                    for kc in range(2):
                        r = e.matmul(ps[:], lhsT=wk[:, kc, hl * 128:(hl + 1) * 128], rhs=kvn[sl][:, :, kc, :],
                                     start=(kc == 0), stop=(kc == 1))
                    return r
                P.op("pe", mm, reads=[("kvn", sl), "wk"], writes=[pk])
                evac_copy(KT[:, hl, g0:g0 + QT], ps[:], [pk], [("KT", hl, tt)])
            steps.append(kstep)
        for sub in range(4):
            def vstep(sub=sub):
                ps, pk = sbank()

                def mm(e, ps=ps, sub=sub, sl=sl):
                    for kc in range(2):
                        r = e.matmul(ps[:], lhsT=kvn[sl][:, sub // 2, kc, (sub % 2) * 128:(sub % 2 + 1) * 128], rhs=wv[:, kc, :],
                                     start=(kc == 0), stop=(kc == 1))
                    return r
                P.op("pe", mm, reads=[("kvn", sl), "wv"], writes=[pk])
                evac_copy(V[:, tt * 4 + sub, :], ps[:], [pk], [("V", tt * 4 + sub)])
            steps.append(vstep)
        dve(lambda e: e.tensor_scalar(out=ang[:], in0=posi[:], scalar1=rcol[:, 0:1], scalar2=None, op0=ALU.mult), ["posi", "rcol"], ["ang"])
        dve(lambda e: e.tensor_scalar(out=posi[:], in0=ang[:], scalar1=1.0 / (2 * PI), scalar2=0.5, op0=ALU.mult, op1=ALU.add), ["ang"], ["posi"])
        dve(lambda e: e.scalar_tensor_tensor(out=rr[:], in0=posi[:], scalar=-C1, in1=ang[:], op0=ALU.mult, op1=ALU.add), ["posi", "ang"], ["rr"])
        dve(lambda e: e.scalar_tensor_tensor(out=rr[:], in0=posi[:], scalar=-C2, in1=rr[:], op0=ALU.mult, op1=ALU.add), ["posi", "rr"], ["rr"])
        dve(lambda e: e.tensor_scalar(out=kf[:], in0=rr[:], scalar1=-PI, scalar2=2 * PI, op0=ALU.is_lt, op1=ALU.mult), ["rr"], ["kf"])
        dve(lambda e: e.tensor_tensor(out=rr[:], in0=rr[:], in1=kf[:], op=ALU.add), ["rr", "kf"], ["rr"])
        dve(lambda e: e.tensor_scalar(out=ang[:], in0=rr[:], scalar1=PI / 2, scalar2=None, op0=ALU.add), ["rr"], ["ang"])
        dve(lambda e: e.tensor_scalar(out=kf[:], in0=ang[:], scalar1=PI, scalar2=-2 * PI, op0=ALU.is_gt, op1=ALU.mult), ["ang"], ["kf"])
        dve(lambda e: e.tensor_tensor(out=ang[:], in0=ang[:], in1=kf[:], op=ALU.add), ["ang", "kf"], ["ang"])

        def sins():
            P.op("act", lambda e: e.activation(out=S0[:], in_=rr[0:64, :], func=AF.Sin), reads=["rr"], writes=["S0"])
            P.op("act", lambda e: e.activation(out=CS[sl][64:128, :], in_=rr[64:128, :], func=AF.Sin), reads=["rr"], writes=[("St", sl)])
            P.op("act", lambda e: e.activation(out=CS[sl][0:64, :], in_=ang[0:64, :], func=AF.Sin), reads=["ang"], writes=[("Ct", sl)])
        nA = len(steps)
        steps.append(sins)
        dve(lambda e: e.tensor_scalar(out=S0[:], in0=S0[:], scalar1=rcol[0:64, 1:2], scalar2=None, op0=ALU.mult), ["S0", "rcol"], ["S0"])
        dve(lambda e: e.tensor_scalar(out=CS[sl][64:128, :], in0=CS[sl][64:128, :], scalar1=rcol[64:128, 1:2], scalar2=None, op0=ALU.mult),
            [("St", sl), "rcol"], [("St", sl)])
        dve(lambda e: e.tensor_tensor(out=v2(kf[0:64, :]), in0=kk[sl][:, :, 0, :], in1=v2(Ct[sl]), op=ALU.mult), [("kk", sl), ("Ct", sl)], ["kf"])
        dve(lambda e: e.tensor_tensor(out=v2(rr[0:64, :]), in0=kk[sl][:, :, 1, :], in1=v2(S0[:]), op=ALU.mult), [("kk", sl), "S0"], ["rr"])
        dve(lambda e: e.tensor_tensor(out=KR[0:64, g0:g0 + QT], in0=kf[0:64, :], in1=rr[0:64, :], op=ALU.add), ["kf", "rr"], [("KR", tt)])
        steps.append(lambda: P.dma("sp", KR[64:128, g0:g0 + QT], KR[0:64, g0:g0 + QT], reads=[("KR", tt)], writes=[("KR2", tt)], sem=("krd", sl)))
        return steps[:nA], steps[nA:nA + 1], steps[nA + 1:]

    pendA, pendB, pendC = [], [], []

    def drain(lst, n):
        for _ in range(min(n, len(lst))):
            lst.pop(0)()

    Qn = [P.sbuf("Qn%d" % i, [128, QT], BF16) for i in range(2)]
    Qr = [P.sbuf("Qr%d" % i, [128, QT], BF16) for i in range(2)]
    NP = 4
    pT = [P.sbuf("pT%d" % i, [128, QT], BF16) for i in range(NP)]
    rden = t1f
    ot1 = P.sbuf("ot", [128, HPC, QT], BF16)
    ot = [ot1, ot1]
    psm = [P.sbuf("psm%d" % i, [128, QT], BF16) for i in range(4)]
    psi = [0]
    pti = [0]
    WQH = 320

    def qproj(it):
        qi, hl = it // HPC, it % HPC
        qs, s2 = qi % 2, it % 2
        ps, pk = sbank()

        def mm(e, ps=ps, hl=hl, qs=qs):
            for kc in range(3):
                r = e.matmul(ps[:], lhsT=wq[:, kc, hl * WQH:hl * WQH + 128], rhs=qn[qs][:, :, kc, :],
                             start=(kc == 0), stop=(kc == 2))
            return r
        P.op("pe", mm, reads=[("qn", qs), "wq"], writes=[pk])
        P.op("act", lambda e, ps=ps, s2=s2: e.activation(out=Qn[s2][:], in_=ps[:], func=AF.Copy),
             reads=[pk], writes=[("Qn", s2)])
        ps, pk = sbank()
        off = hl * WQH + 128

        def mm(e, ps=ps, off=off, qs=qs):
            for kc in range(3):
                r = e.matmul(ps[:], lhsT=wq[:, kc, off:off + 128], rhs=qn[qs][:, :, kc, :],
                             start=(kc == 0), stop=(kc == 2))
            return r
        P.op("pe", mm, reads=[("qn", qs), "wq"], writes=[pk])
        P.op("dve", lambda e, ps=ps, qs=qs, s2=s2: e.tensor_tensor(out=Qr[s2][:], in0=ps[:], in1=CS[qs][:], op=ALU.mult),
             reads=[pk, ("Ct", qs), ("St", qs)], writes=[("Qr", s2)])

    def attn(it):
        qi, hl = it // HPC, it % HPC
        qs, s2 = qi % 2, it % 2
        nk = 4 * (qi + 1)
        ob, db = OB[s2], DB[s2]
        LA = 2
        stiles = {}
        den_at = {}

        def emit_S(kj):
            m = kj - 4 * qi
            lo = 128 * m if m > 0 else 0
            ps, pk = sbank()

            def mm(e, ps=ps, kj=kj, lo=lo, hl=hl, s2=s2):
                e.matmul(ps[:, lo:QT], lhsT=KT[:, hl, kj * 128:(kj + 1) * 128], rhs=Qn[s2][:, lo:QT],
                         start=True, stop=False)
                return e.matmul(ps[:, lo:QT], lhsT=KR[:, kj * 128:(kj + 1) * 128], rhs=Qr[s2][:, lo:QT],
                                start=False, stop=True)
            P.op("pe", mm, reads=[("KT", hl, kj // 4), ("KR", kj // 4), ("KR2", kj // 4), ("Qn", s2), ("Qr", s2)], writes=[pk])
            pi_ = pti[0] % NP
            pti[0] += 1
            P.op("act", lambda e, ps=ps, lo=lo, pi_=pi_: e.activation(out=pT[pi_][:, lo:QT], in_=ps[:, lo:QT], func=AF.Exp, scale=SCALE),
                 reads=[pk], writes=[("pT", pi_)])
            if m >= 0:
                P.op("dve", lambda e, lo=lo, pi_=pi_: e.tensor_tensor(out=pT[pi_][:, lo:lo + 128], in0=pT[pi_][:, lo:lo + 128], in1=tri[:], op=ALU.mult),
                     reads=[("pT", pi_), "tri"], writes=[("pT", pi_)])
            sm = None
            if m < 0 and kj % 2 == 1:
                nfull = 4 * qi
                sp_ = (kj // 2) % 4
                pj = stiles[kj - 1][0]
                P.op("dve", lambda e, sp_=sp_, pj=pj, pi_=pi_: e.tensor_tensor(out=psm[sp_][:], in0=pT[pj][:], in1=pT[pi_][:], op=ALU.add),
                     reads=[("pT", pj), ("pT", pi_)], writes=[("psm", sp_)])
                if kj % 4 == 3:
                    P.op("dve", lambda e, sp_=sp_: e.tensor_tensor(out=psm[sp_][:], in0=psm[sp_ - 1][:], in1=psm[sp_][:], op=ALU.add),
                         reads=[("psm", sp_ - 1), ("psm", sp_)], writes=[("psm", sp_)])
                    if kj % 8 == 7:
                        P.op("dve", lambda e: e.tensor_tensor(out=psm[3][:], in0=psm[1][:], in1=psm[3][:], op=ALU.add),
                             reads=[("psm", 1), ("psm", 3)], writes=[("psm", 3)])
                        den_at[min(kj + 2, nfull - 1)] = (3, kj == 7)
                    elif kj == nfull - 1:
                        den_at[nfull - 1] = (1, kj == 3)
            stiles[kj] = (pi_, lo, sm)

        def emit_PV(kj):
            pi_, lo, sm = stiles.pop(kj)
            m = kj - 4 * qi
            sm, first = den_at.pop(kj, (None, False))

            def mm(e, kj=kj, lo=lo, pi_=pi_, hl=hl, ob=ob, db=db, nk=nk, sm=sm, m=m, first=first):
                r = e.matmul(pst[ob][:, lo:QT], lhsT=V[:, kj, hl * 128:(hl + 1) * 128], rhs=pT[pi_][:, lo:QT],
                             start=(kj == 0), stop=(kj == nk - 1))
                if sm is not None:
                    r = e.matmul(pst[db][:], lhsT=ones[:], rhs=psm[sm][:], start=first, stop=False)
                if m >= 0:
                    r = e.matmul(pst[db][:, lo:QT], lhsT=ones[:], rhs=pT[pi_][:, lo:QT],
                                 start=(kj == 0), stop=(kj == nk - 1))
                return r
            rd = [("pT", pi_), ("V", kj), "ones"] + ([("psm", sm)] if sm is not None else [])
            P.op("pe", mm, reads=rd, writes=[pskey[ob], pskey[db]])

        if hl == 1:
            drain(pendA, len(pendA))
        if hl == 2:
            drain(pendB, len(pendB))
        if hl == 3:
            drain(pendC, len(pendC))
        perA = -(-len(pendA) // nk) if hl == 0 else 0
        perC = -(-len(pendC) // nk) if hl == 2 else 0
        for kj in range(nk + LA):
            if kj < nk:
                emit_S(kj)
                drain(pendA, perA)
                drain(pendC, perC)
            if kj - LA >= 0:
                emit_PV(kj - LA)
            if kj == 1 and it + 1 < NQ * HPC:
                qproj(it + 1)
        P.op("act", lambda e, db=db: e.activation(out=rden[:], in_=pst[db][:], func=AF.Ln), reads=[pskey[db]], writes=["t1"])
        P.op("act", lambda e: e.activation(out=rden[:], in_=rden[:], func=AF.Exp, scale=-1.0), reads=["t1"], writes=["t1"])
        P.op("dve", lambda e, ob=ob, qs=qs, hl=hl: e.tensor_tensor(out=ot[qs][:, hl, :], in0=pst[ob][:], in1=rden[:], op=ALU.mult),
             reads=[pskey[ob], "t1"], writes=[("ot", hl)])

    for lst in prep_steps(0):
        for st in lst:
            st()
    qproj(0)
    for qi in range(NQ):
        if qi + 1 < NQ:
            a_, b_, c_ = prep_steps(qi + 1)
            pendA.extend(a_), pendB.extend(b_), pendC.extend(c_)
        for hl in range(HPC):
            attn(qi * HPC + hl)
        qs = qi % 2
        P.dma("pool", oT[qi * 128:(qi + 1) * 128, :], ot[qs][:].rearrange("p h c -> p (h c)"),
              reads=[("ot", hl) for hl in range(HPC)], writes=[("oT", qi)], sem="ost", final=True)
        if qi % 2 == 1 and io.get("ag2") is not None:
            io["ag2"](P, qi // 2)
        if io.get("prefetch") is not None:
            io["prefetch"](P, qi)


def phase_C(nc, P, io):
    TT, NT = 512, TOK // 512
    x1T, oTo, outT = io["x1T"], io["oTo"], io["outT"]
    banks = Banks(P)
    ones = P.sbuf("ones", [128, 128], BF16)
    P.op("pool", lambda e: e.memset(ones[:], 1.0), writes=["ones"])
    vec = P.sbuf("vec", [128, NV], F32)
    P.dma("sp", vec[:], io["vec"], writes=["vec"], sem="vec")
    wz = P.sbuf("wz", [128, 8, 2048], BF16)
    wo = P.sbuf("wo", [128, 16, 1024], BF16)
    wq_, wsrc, wosrc = ("act", io["wz_bf"], io["wo_bf"]) if io.get("wz_bf") is not None else ("pool", io["wz"], io["mla_w_out"])

    def wload(first):
        for q4 in ([0] if first else [1, 2, 3]):
            P.dma(wq_, wz[:, :, q4 * 512:(q4 + 1) * 512], wsrc[:, :, q4 * 512:(q4 + 1) * 512],
                  reads=([] if first else ["xt0"]), writes=[("wz", q4)], sem=("wz", q4))
        if not first:
            for h2 in range(2):
                P.dma(wq_, wo[:, h2 * 8:(h2 + 1) * 8, :].rearrange("p (a b) f -> p a (b f)", b=2),
                      wosrc[:, h2 * 4:(h2 + 1) * 4, :], reads=["xt0"], writes=[("wo", h2)], sem=("wo", h2))
    wload(True)
    xt = [P.sbuf("xt%d" % i, [128, 2, 8, 256], F32) for i in range(3)]
    og = [P.sbuf("og%d" % i, [128, 16, TT], BF16) for i in range(2)]
    sq = P.sbuf("sq", [128, 2, 8, 256], BF16)
    rt = P.sbuf("rt", [128, TT], F32)
    inv = rt
    hb = [P.sbuf("h%d" % i, [128, 8, TT], BF16) for i in range(2)]
    szt = [P.sbuf("szt%d" % i, [128, TT], F32) for i in range(2)]
    yb = [P.sbuf("y%d" % i, [128, 16, TT], BF16) for i in range(2)]

    def v2(ap):
        return ap.rearrange("p (a c) -> p a c", a=2)

    def stats(xs, xkey):
        P.op("act", lambda e, xs=xs: e.activation(out=sq[:], in_=xs[:], func=AF.Square), reads=[xkey], writes=["sq"])
        ps, pk = banks.next()

        def mm(e, ps=ps):
            for k in range(8):
                r = e.matmul(ps[:], lhsT=ones[:], rhs=sq[:, :, k, :], start=(k == 0), stop=(k == 7))
            return r
        P.op("pe", mm, reads=["sq", "ones"], writes=[pk])
        P.op("act", lambda e, ps=ps: e.activation(out=rt[:], in_=ps[:], func=AF.Ln, bias=EPS, scale=1.0 / D),
             reads=[pk], writes=["rt"])
        P.op("act", lambda e: e.activation(out=inv[:], in_=rt[:], func=AF.Exp, scale=-0.5), reads=["rt"], writes=["rt"])

    def C0(t):
        xs, xkey = xt[t % 3], "xt%d" % (t % 3)
        os_, okey = og[t % 2], "og%d" % (t % 2)
        h = hb[t % 2]
        P.dma("sp", xs[:], x1T[:, 2 * t:2 * t + 2, :, :], reads=["x1T"], writes=[xkey], sem=xkey)
        for rr in range(4):
            io["oTo_dma"](P, t, rr, os_[:, rr * 4:(rr + 1) * 4, :].rearrange("p h c -> p (h c)"), (okey, rr))
        stats(xs, xkey)
        for kc in range(8):
            P.op("dve", lambda e, kc=kc, xs=xs, h=h: e.scalar_tensor_tensor(
                out=v2(h[:, kc, :]), in0=xs[:, :, kc, :], scalar=vec[:, V_G1 + kc:V_G1 + kc + 1], in1=v2(inv[:]),
                op0=ALU.mult, op1=ALU.mult), reads=[xkey, "rt", "vec"], writes=[("h", t % 2, kc)])

    def C1(t):
        os_, okey = og[t % 2], "og%d" % (t % 2)
        h, y = hb[t % 2], yb[t % 2]
        for f in range(16):
            ps, pk = banks.next()

            def mm(e, ps=ps, f=f, h=h):
                for kc in range(8):
                    r = e.matmul(ps[:], lhsT=wz[:, kc, f * 128:(f + 1) * 128], rhs=h[:, kc, :], start=(kc == 0), stop=(kc == 7))
                return r
            P.op("pe", mm, reads=[("h", t % 2, kc) for kc in range(8)] + [("wz", f // 4)], writes=[pk])
            zs = f % 2
            P.op("act", lambda e, ps=ps, zs=zs: e.activation(out=szt[zs][:], in_=ps[:], func=AF.Silu), reads=[pk], writes=[("szt", zs)])
            P.op("dve", lambda e, zs=zs, f=f, os_=os_, y=y: e.tensor_tensor(out=y[:, f, :], in0=szt[zs][:], in1=os_[:, f, :], op=ALU.mult),
                 reads=[("szt", zs), (okey, f // 4)], writes=[("y", t % 2, f)])

    def C2(t):
        xs, xkey = xt[t % 3], "xt%d" % (t % 3)
        y = yb[t % 2]
        for dc in range(8):
            ps, pk = banks.next()

            def mm(e, ps=ps, dc=dc, y=y):
                for kc in range(16):
                    r = e.matmul(ps[:], lhsT=wo[:, kc, dc * 128:(dc + 1) * 128], rhs=y[:, kc, :], start=(kc == 0), stop=(kc == 15))
                return r
            P.op("pe", mm, reads=[("y", t % 2, kc) for kc in range(16)] + [("wo", 0), ("wo", 1)], writes=[pk])
            P.op("dve", lambda e, ps=ps, dc=dc, xs=xs: e.tensor_tensor(out=xs[:, :, dc, :], in0=v2(ps[:]), in1=xs[:, :, dc, :], op=ALU.add),
                 reads=[pk, xkey], writes=[xkey])

    def C3(t):
        xs, xkey = xt[t % 3], "xt%d" % (t % 3)
        stats(xs, xkey)
        for kc in range(8):
            P.op("dve", lambda e, kc=kc, xs=xs: e.scalar_tensor_tensor(
                out=xs[:, :, kc, :], in0=xs[:, :, kc, :], scalar=vec[:, V_GF + kc:V_GF + kc + 1], in1=v2(inv[:]),
                op0=ALU.mult, op1=ALU.mult), reads=[xkey, "rt", "vec"], writes=[xkey])
        P.dma("pool", outT[:, 2 * t:2 * t + 2, :, :], xs[:], reads=[xkey],
              writes=[("outT", t)], sem="outst", final=True)

    C0(0)
    wload(False)
    C1(0)
    for t in range(NT):
        if t + 1 < NT:
            C0(t + 1)
        C2(t)
        if t + 1 < NT:
            C1(t + 1)
        C3(t)


def _dram(nc, name, shape, dt, kind):
    return nc.dram_tensor(name, list(shape), dt, kind=kind).ap()


def build_A():
    nc = bass.Bass("TRN2", target_bir_lowering=False)
    io = {
        "xT": _dram(nc, "xT", [128, 8, 8, 256], F32, "ExternalInput"),
        "xh": _dram(nc, "xh", [128, 8, HALO], F32, "ExternalInput"),
        "vec": _dram(nc, "vec", [128, NV], F32, "ExternalInput"),
        "rc": _dram(nc, "rc", [128, 64], F32, "ExternalInput"),
        "pool_w_in": _dram(nc, "pool_w_in", [128, 4, 8, 1024], F32, "ExternalInput"),
        "pool_w_group": _dram(nc, "pool_w_group", [128, 4, 4, 512], F32, "ExternalInput"),
        "pool_w_out": _dram(nc, "pool_w_out", [128, 8, 2048], F32, "ExternalInput"),
        "wl": _dram(nc, "wl", [128, 8, LATR], F32, "ExternalInput"),
        "x1T": _dram(nc, "x1T", [128, 8, 8, 256], F32, "ExternalOutput"),
        "lat": _dram(nc, "lat", [8 * 128, LATS * 256], BF16, "ExternalOutput"),
    }
    with ExitStack() as es:
        P = Prog(nc, es)
        phase_A(nc, P, io)
        P.emit()
    return nc


def build_B():
    nc = bass.Bass("TRN2", target_bir_lowering=False)
    io = {
        "latA": _dram(nc, "latA", [4 * 8 * 128, LATS * 256], BF16, "ExternalInput"),
        "posr": _dram(nc, "posr", [128, S], I32, "ExternalInput"),
        "rcol": _dram(nc, "rcol", [128, 2], F32, "ExternalInput"),
        "tri": _dram(nc, "tri", [128, 128], F32, "ExternalInput"),
        "wq": _dram(nc, "wq", [128, 3, HPC * 320], F32, "ExternalInput"),
        "wk": _dram(nc, "wk", [128, 2, HPC * 128], F32, "ExternalInput"),
        "wv": _dram(nc, "wv", [128, 2, HPC * 128], F32, "ExternalInput"),
        "oT": _dram(nc, "oT", [16 * 128, HPC * 512], BF16, "ExternalOutput"),
        "cs": nc.dram_tensor("cs", [16, 64, 2 * 512], F32).ap(),
    }
    with ExitStack() as es:
        P = Prog(nc, es)
        phase_B(nc, P, io)
        P.emit()
    return nc


def build_C():
    nc = bass.Bass("TRN2", target_bir_lowering=False)
    oTo = _dram(nc, "oTo", [4 * 4 * 128, 2048], BF16, "ExternalInput")
    io = {
        "x1T": _dram(nc, "x1T", [128, 8, 8, 256], F32, "ExternalInput"),
        "oTo": oTo,
        "oTo_dma": lambda P, t, rr, dst, key: P.dma("sp", dst, oTo[(t * 4 + rr) * 128:(t * 4 + rr + 1) * 128, :],
                                                  reads=["oTo"], writes=[key], sem=key),
        "vec": _dram(nc, "vec", [128, NV], F32, "ExternalInput"),
        "wz": _dram(nc, "wz", [128, 8, 2048], F32, "ExternalInput"),
        "mla_w_out": _dram(nc, "mla_w_out", [128, 8, 2048], F32, "ExternalInput"),
        "outT": _dram(nc, "outT", [128, 8, 8, 256], F32, "ExternalOutput"),
    }
    with ExitStack() as es:
        P = Prog(nc, es)
        phase_C(nc, P, io)
        P.emit()
    return nc


def build_fused():
    nc = bass.Bass("TRN2", target_bir_lowering=False)
    ein = lambda name, shape, dt: _dram(nc, name, shape, dt, "ExternalInput")
    x1s = nc.dram_tensor("x1s", [128, 8, 8, 256], F32).ap()
    lat_own = nc.dram_tensor("lat_own", [8 * 128, LATS * 256], BF16).ap()
    latA = nc.dram_tensor("latA", [4 * 8 * 128, LATS * 256], BF16).ap()
    o_own = nc.dram_tensor("o_own", [16 * 128, HPC * 512], BF16).ap()
    oA = nc.dram_tensor("oA", [4 * 16 * 128, HPC * 512], BF16).ap()
    vec = ein("vec", [128, NV], F32)
    groups = [[0, 1, 2, 3], [4, 5, 6, 7]]
    ioA = {
        "xT": ein("xT", [128, 8, 8, 256], F32), "xh": ein("xh", [128, 8, HALO], F32), "vec": vec,
        "rc": ein("rc", [128, 64], F32), "pool_w_in": ein("pool_w_in", [128, 4, 8, 1024], F32),
        "pool_w_group": ein("pool_w_group", [128, 4, 4, 512], F32), "pool_w_out": ein("pool_w_out", [128, 8, 2048], F32),
        "wl": ein("wl", [128, 8, LATR], F32), "x1T": x1s, "lat": lat_own,
    }
    ioB = {
        "latA": latA, "posr": ein("posr", [128, S], I32), "rcol": ein("rcol", [128, 2], F32), "tri": ein("tri", [128, 128], F32),
        "wq": ein("wq", [128, 3, HPC * 320], F32), "wk": ein("wk", [128, 2, HPC * 128], F32),
        "wv": ein("wv", [128, 2, HPC * 128], F32), "oT": o_own, "cs": nc.dram_tensor("cs", [16, 64, 2 * 512], F32).ap(),
    }

    def ag1(P, k):
        P.custom_dma("pool", lambda e: e.collective_compute(
            "AllGather", ALU.bypass, replica_groups=groups,
            ins=[lat_own[k * 256:(k + 1) * 256, :]], outs=[latA[k * 1024:(k + 1) * 1024, :]]),
            reads=[("lat", 2 * k), ("lat", 2 * k + 1)], writes=[("latA", k)], sem=("ag1", k), inc=1)

    def ag2(P, m):
        P.custom_dma("pool", lambda e: e.collective_compute(
            "AllGather", ALU.bypass, replica_groups=groups,
            ins=[o_own[m * 256:(m + 1) * 256, :]], outs=[oA[m * 1024:(m + 1) * 1024, :]]),
            reads=[("oT", 2 * m), ("oT", 2 * m + 1)], writes=[("oTo", m)], sem=("ag2", m), inc=1)

    ioA["ag1"] = ag1
    ioB["ag2"] = ag2
    wz_bf = nc.dram_tensor("wz_bf", [128, 8, 2048], BF16).ap()
    wo_bf = nc.dram_tensor("wo_bf", [128, 8, 2048], BF16).ap()

    def prefetch(P, i):
        name, dst = (("wz", wz_bf), ("mla_w_out", wo_bf))[i // 8]
        k = i % 8
        P.dma("pool", dst[:, k, :], wsrc[name][:, k, :], writes=[("pf", i)], sem=("pf", i % 2))
    ioB["prefetch"] = prefetch

    ag2_tok = {}

    def oTo_dma(P, t, rr, dst, key):
        def emit(e, sem):
            core = e.partition_id()
            for k in range(8):
                m = (k % 4) * 2 + t // 2
                row = m * 1024 + rr * 256 + (t % 2) * 128
                with e.If(core == k):
                    e.wait_ge(ag2_tok[m][0], ag2_tok[m][1])
                    e.dma_start(out=dst, in_=oA[row:row + 128, :]).then_inc(sem, 16)
        P.custom_dma("sp", emit, writes=[key], sem=key, self_inc=True)

    wsrc = {"wz": ein("wz", [128, 8, 2048], F32), "mla_w_out": ein("mla_w_out", [128, 8, 2048], F32)}
    ioC = {
        "x1T": x1s, "oTo": oA, "oTo_dma": oTo_dma, "vec": vec, "wz": wsrc["wz"],
        "mla_w_out": wsrc["mla_w_out"], "wz_bf": wz_bf, "wo_bf": wo_bf,
        "outT": _dram(nc, "outT", [128, 8, 8, 256], F32, "ExternalOutput"),
    }
    with ExitStack() as sem_es:
        with ExitStack() as es:
            PA = Prog(nc, es, sem_es)
            phase_A(nc, PA, ioA)
            PA.barrier(skip="ag1")
            PA.emit()
        with ExitStack() as es:
            PB = Prog(nc, es, sem_es)
            for k in range(4):
                PB.extern(PA.sems[("dma", ("ag1", k))], PA.dma_cnt[("ag1", k)], [("latA", k)])
            phase_B(nc, PB, ioB)
            PB.barrier(skip="ag2")
            PB.emit()
        with ExitStack() as es:
            PC = Prog(nc, es, sem_es)
            for m in range(8):
                ag2_tok[m] = (PB.sems[("dma", ("ag2", m))], PB.dma_cnt[("ag2", m)])
                PC.extern(ag2_tok[m][0], ag2_tok[m][1], [("oTo", m)])
            phase_C(nc, PC, ioC)
            PC.barrier()
            PC.emit()
    return nc


def _cols(v):
    return np.ascontiguousarray(np.asarray(v, np.float32).reshape(-1, 128).T)


def _pm(w):
    nk = w.shape[0] // 128
    return np.ascontiguousarray(w.reshape(nk, 128, w.shape[1]).transpose(1, 0, 2))


def host_inputs(inp):
    f = lambda k: np.asarray(inp[k], np.float32)
    x = f("x")
    vec = np.concatenate([_cols(f("pool_norm")[0]), _cols(f("mla_norm")[0]), _cols(f("pool_scale")[0]),
                          _cols(f("mla_q_norm")[0]), _cols(f("mla_kv_norm")[0]), _cols(f("final_norm"))], axis=1)
    assert vec.shape == (128, NV)
    perm = np.concatenate([np.concatenate([np.arange(g * 512, (g + 1) * 512), 2048 + np.arange(g * 512, (g + 1) * 512)])
                           for g in range(4)])
    w_in_p = _pm(f("pool_w_in")[0][:, perm]).reshape(128, 8, 4, 1024).transpose(0, 2, 1, 3)
    w_in_p = np.ascontiguousarray(w_in_p)
    wg_p = np.ascontiguousarray(f("pool_w_group")[0].reshape(4, 4, 128, 512).transpose(2, 0, 1, 3))
    wo0_p = _pm(f("pool_w_out")[0]).reshape(128, 8, 2048)
    wo1_p = _pm(f("mla_w_out")[0]).reshape(128, 8, 2048)
    mw = f("mla_w_in")[0]
    wl_p = _pm(np.concatenate([mw[:, :704], mw[:, 672:704], mw[:, 640:672], mw[:, 640:704]], axis=1))
    wz_p = _pm(mw[:, 704:])
    wqb, wkvb = f("mla_w_q_b")[0], f("mla_w_kv_b")[0]
    invf = (np.float32(1.0) / (np.float32(10000.0) ** (np.arange(0, 64, 2, dtype=np.float32) / np.float32(64)))).astype(np.float32)
    rcol = np.stack([np.tile(invf, 4), np.tile(np.concatenate([-np.ones(32, np.float32), np.ones(32, np.float32)]), 2)], axis=1)
    tri = (np.arange(128)[None, :] >= np.arange(128)[:, None]).astype(np.float32)
    A, Bm, C = [], [], []
    for c in range(8):
        b, r = c // 4, c % 4
        t0 = r * TOK
        xT = np.ascontiguousarray(x[b, t0:t0 + TOK].reshape(8, 256, 8, 128).transpose(3, 0, 2, 1))
        xh = np.zeros((HALO, D), np.float32)
        if t0 > 0:
            xh[:] = x[b, t0 - HALO:t0]
        xh = np.ascontiguousarray(xh.reshape(HALO, 8, 128).transpose(2, 1, 0))
        rc = np.zeros((128, 64), np.float32)
        for g, w in enumerate((2, 4, 8, 16)):
            rc[:, g * 16:(g + 1) * 16] = 1.0 / np.minimum(t0 + np.arange(16) + 1, w).astype(np.float32)
        A.append({"xT": xT, "xh": xh, "vec": vec, "rc": rc, "pool_w_in": w_in_p,
                  "pool_w_group": wg_p, "pool_w_out": wo0_p, "wl": wl_p})
        hs = [4 * r + hl for hl in range(4)]
        wq = np.concatenate([np.concatenate([wqb[:, h * 192:h * 192 + 192], wqb[:, h * 192 + 160:h * 192 + 192],
                                             wqb[:, h * 192 + 128:h * 192 + 160], wqb[:, h * 192 + 128:h * 192 + 192]], axis=1)
                             for h in hs], axis=1)
        wk = np.concatenate([wkvb[:, h * 256:h * 256 + 128] for h in hs], axis=1)
        wv = np.concatenate([wkvb[:, h * 256 + 128:h * 256 + 256] for h in hs], axis=1)
        posr = np.ascontiguousarray(np.broadcast_to(np.asarray(inp["positions"])[b].astype(np.int32)[None, :], (128, S)))
        Bm.append({"posr": posr, "rcol": rcol, "tri": tri, "wq": _pm(wq), "wk": _pm(wk), "wv": _pm(wv)})
        C.append({"vec": vec, "wz": wz_p, "mla_w_out": wo1_p})
    return A, Bm, C


def _assemble(res):
    out = np.empty((NB, S, D), np.float32)
    for c in range(8):
        b, r = c // 4, c % 4
        o = res[c]["outT"]
        out[b, r * TOK:(r + 1) * TOK, :] = o.transpose(1, 3, 2, 0).reshape(TOK, D)
    return out


_NC = {}


def _get(name, fn):
    if name not in _NC:
        _NC[name] = fn()
    return _NC[name]


FUSED = True


def kernel(**inputs):
    A, Bm, C = host_inputs(inputs)
    cores = list(range(8))
    if FUSED:
        maps = []
        for c in cores:
            m = {}
            m.update(A[c]); m.update(Bm[c]); m.update(C[c])
            maps.append(m)
        res = run_bass_kernel_spmd(_get("F", build_fused), maps, core_ids=cores).results
        return _assemble(res)
    ra = run_bass_kernel_spmd(_get("A", build_A), A, core_ids=cores).results
    for c in cores:
        b = c // 4
        Bm[c]["latA"] = np.concatenate([ra[b * 4 + r]["lat"][k * 256:(k + 1) * 256] for k in range(4) for r in range(4)], axis=0)
    rb = run_bass_kernel_spmd(_get("B", build_B), Bm, core_ids=cores).results
    for c in cores:
        b, r = c // 4, c % 4
        C[c]["x1T"] = ra[c]["x1T"]
        C[c]["oTo"] = np.ascontiguousarray(np.concatenate(
            [rb[b * 4 + rr]["oT"][(r * 4 + t) * 128:(r * 4 + t + 1) * 128, :] for t in range(4) for rr in range(4)], axis=0))
    rc = run_bass_kernel_spmd(_get("C", build_C), C, core_ids=cores).results
    return _assemble(rc)
```

```python
import numpy as np
from contextlib import ExitStack
import concourse.bass as bass
import concourse.mybir as mybir
from concourse.bass_utils import run_bass_kernel_spmd

F32 = mybir.dt.float32
BF16 = mybir.dt.bfloat16
I32 = mybir.dt.int32
AF = mybir.ActivationFunctionType
ALU = mybir.AluOpType


class _Op:
    __slots__ = ("eng", "emit", "deps", "is_dma", "semkey", "signal", "count", "waits", "final", "inc", "self_inc", "ext")

    def __init__(self):
        self.inc, self.self_inc, self.ext = 16, False, None


class Prog:
    COMPUTE = ("pe", "act", "dve", "pool")
    NSEM = 0
    NPROG = 0

    def __init__(self, nc, es, sem_es=None):
        self.nc, self.es = nc, es
        self.sem_es = sem_es if sem_es is not None else es
        Prog.NPROG += 1
        self.uid = Prog.NPROG
        self.ops = []
        self.last_writer = {}
        self.readers = {}
        self.last_dma_on_sem = {}
        self.nsb = 0

    def sbuf(self, name, shape, dtype):
        return self.es.enter_context(self.nc.sbuf_tensor("sb%d_" % self.uid + name, list(shape), dtype))

    def psum(self, name, shape, dtype):
        return self.es.enter_context(self.nc.psum_tensor("pp%d_" % self.uid + name, list(shape), dtype))

    def _record(self, op, reads, writes):
        idx = len(self.ops)
        deps = {}

        def add(i, kind):
            if i is not None:
                deps[i] = deps.get(i, 0) | kind

        for k in reads:
            add(self.last_writer.get(k), 1)
        for k in writes:
            add(self.last_writer.get(k), 2)
            for i in self.readers.get(k, ()):
                add(i, 4)
        deps.pop(idx, None)
        op.deps = deps
        self.ops.append(op)
        for k in writes:
            self.last_writer[k] = idx
            self.readers[k] = []
        for k in reads:
            lst = self.readers.setdefault(k, [])
            if not op.is_dma:
                lst[:] = [i for i in lst if self.ops[i].is_dma or self.ops[i].eng != op.eng]
            lst.append(idx)
        return idx

    def op(self, eng, emit, reads=(), writes=()):
        o = _Op()
        o.eng, o.emit, o.is_dma, o.semkey, o.signal, o.count, o.final = eng, emit, False, None, False, 0, False
        return self._record(o, list(reads), list(writes))

    def dma(self, queue, out, in_, reads=(), writes=(), sem=None, final=False, **kw):
        o = _Op()
        o.eng, o.is_dma, o.semkey, o.signal, o.count, o.final = queue, True, sem, True, 0, final
        o.emit = lambda e: e.dma_start(out=out, in_=in_, **kw)
        idx = self._record(o, list(reads), list(writes))
        prev = self.last_dma_on_sem.get(sem)
        if prev is not None:
            o.deps[prev] = o.deps.get(prev, 0) | 2
        self.last_dma_on_sem[sem] = idx
        if queue == "pool":
            self.pool_dmas = getattr(self, "pool_dmas", [])
            if len(self.pool_dmas) >= 2:
                o.deps[self.pool_dmas[-2]] = o.deps.get(self.pool_dmas[-2], 0) | 2
            self.pool_dmas.append(idx)
        return idx

    def custom_dma(self, queue, emit, reads=(), writes=(), sem=None, final=False, inc=16, self_inc=False):
        o = _Op()
        o.eng, o.is_dma, o.semkey, o.signal, o.count, o.final = queue, True, sem, True, 0, final
        o.inc, o.self_inc = inc, self_inc
        o.emit = emit
        idx = self._record(o, list(reads), list(writes))
        prev = self.last_dma_on_sem.get(sem)
        if prev is not None:
            o.deps[prev] = o.deps.get(prev, 0) | 2
        self.last_dma_on_sem[sem] = idx
        return idx

    def extern(self, sem, count, writes):
        o = _Op()
        o.eng, o.is_dma, o.semkey, o.signal, o.count, o.final = "sp", True, ("ext", len(self.ops)), True, count, False
        o.ext, o.emit = sem, None
        return self._record(o, [], list(writes))

    def barrier(self, skip=None):
        last = {}
        for i, o in enumerate(self.ops):
            if o.is_dma and skip is not None and isinstance(o.semkey, tuple) and o.semkey[0] == skip:
                continue
            last[("dma", o.semkey) if o.is_dma else ("eng", o.eng)] = i
        for eng in ("pe", "act", "dve", "pool", "sp"):
            o = _Op()
            o.eng, o.is_dma, o.semkey, o.signal, o.count, o.final = eng, False, None, False, 0, False
            o.emit = lambda e: e.nop()
            o.deps = {i: 1 for k, i in last.items() if k != ("eng", eng)}
            self.ops.append(o)

    def emit(self):
        nc, es, ops = self.nc, self.es, self.ops
        for o in ops:
            keep = []
            for i, kind in o.deps.items():
                p = ops[i]
                if not p.is_dma and not o.is_dma and p.eng == o.eng:
                    if o.eng == "pe":
                        continue
                if not p.is_dma and o.is_dma and p.eng == o.eng and not (kind & 3):
                    pass
                keep.append(i)
                p.signal = True
            o.deps = keep
        eng_cnt = {e: 0 for e in ("pe", "act", "dve", "pool", "sp")}
        dma_cnt = {}
        for o in ops:
            if o.ext is not None:
                continue
            if o.is_dma:
                dma_cnt[o.semkey] = dma_cnt.get(o.semkey, 0) + o.inc
                o.count = dma_cnt[o.semkey]
            elif o.signal:
                eng_cnt[o.eng] += 1
                o.count = eng_cnt[o.eng]
        sems = {}
        for e in self.COMPUTE:
            if eng_cnt[e]:
                Prog.NSEM += 1
                sems[("eng", e)] = self.sem_es.enter_context(nc.semaphore("sem%d" % Prog.NSEM))
        for k in dma_cnt:
            Prog.NSEM += 1
            sems[("dma", k)] = self.sem_es.enter_context(nc.semaphore("sem%d" % Prog.NSEM))
        for o in ops:
            if o.ext is not None:
                sems[("dma", o.semkey)] = o.ext
        self.sems, self.dma_cnt = sems, dma_cnt

        def tok(p):
            return (sems[("dma", p.semkey)] if p.is_dma else sems[("eng", p.eng)], p.count,
                    ("dma", p.semkey) if p.is_dma else ("eng", p.eng))

        waited = {e: {} for e in eng_cnt}
        for o in ops:
            w = {}
            for i in o.deps:
                s, c, k = tok(ops[i])
                if c > waited[o.eng].get(k, 0) and c > w.get(k, (None, 0))[1]:
                    w[k] = (s, c)
            for k, (s, c) in w.items():
                waited[o.eng][k] = c
            o.waits = list(w.values())
        finals = [(sems[("dma", k)], dma_cnt[k]) for k in dma_cnt
                  if any(o.final and o.semkey == k for o in ops if o.is_dma)]
        by_eng = {e: [o for o in ops if o.eng == e] for e in eng_cnt}

        def run(ename, e):
            for o in by_eng[ename]:
                if o.ext is not None:
                    continue
                for s, c in o.waits:
                    e.wait_ge(s, c)
                if o.self_inc:
                    o.emit(e, sems[("dma", o.semkey)])
                    continue
                inst = o.emit(e)
                if inst is None:
                    continue
                if o.is_dma:
                    inst.then_inc(sems[("dma", o.semkey)], o.inc)
                elif o.signal:
                    inst.then_inc(sems[("eng", o.eng)], 1)
            if ename == "sp":
                for s, c in finals:
                    e.wait_ge(s, c)

        with nc.Block() as block:
            @block.tensor
            def _(e):
                run("pe", e)

            @block.scalar
            def _(e):
                run("act", e)

            @block.vector
            def _(e):
                run("dve", e)

            @block.gpsimd
            def _(e):
                run("pool", e)

            @block.sync
            def _(e):
                run("sp", e)


D = 1024
S = 8192
NB = 2
TOK = 2048
HALO = 16
NH = 16
HPC = 4
EPS = 1e-6
LATS = 7
LATR = 832
SCALE = 192 ** -0.5

V_G0, V_G1, V_SC, V_GQ, V_GKV, V_GF, NV = 0, 8, 16, 32, 35, 37, 45


class Banks:
    def __init__(self, P, n=8, prefix="ps"):
        self.t = [P.psum("%s%d" % (prefix, i), [128, 512], F32) for i in range(n)]
        self.keys = [(prefix, i) for i in range(n)]
        self.i = 0

    def next(self):
        i = self.i
        self.i = (self.i + 1) % len(self.t)
        return self.t[i], self.keys[i]


def rms_inv(P, banks, ones, src_sq, nk, W, nfeat, rt, inv, keys_sq, key_rt, key_inv):
    ps, pk = banks.next()

    def mm(e):
        for k in range(nk):
            r = e.matmul(ps[:, :W], lhsT=ones[:], rhs=src_sq[:, k, :W], start=(k == 0), stop=(k == nk - 1))
        return r
    P.op("pe", mm, reads=list(keys_sq) + ["ones"], writes=[pk])
    P.op("act", lambda e: e.activation(out=rt[:, :W], in_=ps[:, :W], func=AF.Ln, bias=EPS, scale=1.0 / nfeat),
         reads=[pk], writes=[key_rt])
    P.op("act", lambda e: e.activation(out=inv[:, :W], in_=rt[:, :W], func=AF.Exp, scale=-0.5), reads=[key_rt], writes=[key_inv])


def phase_A(nc, P, io):
    TT, NT = 256, TOK // 256
    xT, x1T, lat = io["xT"], io["x1T"], io["lat"]
    banks = Banks(P)
    ones = P.sbuf("ones", [128, 128], BF16)
    P.op("pool", lambda e: e.memset(ones[:], 1.0), writes=["ones"])
    vec = P.sbuf("vec", [128, NV], F32)
    P.dma("sp", vec[:], io["vec"], writes=["vec"], sem="vec")
    rc = P.sbuf("rc", [128, 64], F32)
    P.dma("sp", rc[:], io["rc"], writes=["rc"], sem="rc")

    w_in = P.sbuf("w_in", [128, 8, 4096], BF16)
    wg = P.sbuf("wg", [128, 4, 4, 512], BF16)
    wo = P.sbuf("wo", [128, 16, 1024], BF16)
    wl = P.sbuf("wl", [128, 8, LATR], BF16)
    for g in range(4):
        P.dma("pool", w_in[:, :, g * 1024:(g + 1) * 1024], io["pool_w_in"][:, g, :, :],
              writes=[("w_in", g)], sem=("w_in", g))
        P.dma("pool", wg[:, g, :, :], io["pool_w_group"][:, g, :, :], writes=[("wg", g)], sem=("wg", g))
    for h2 in range(2):
        P.dma("pool", wo[:, h2 * 8:(h2 + 1) * 8, :].rearrange("p (a b) f -> p a (b f)", b=2),
              io["pool_w_out"][:, h2 * 4:(h2 + 1) * 4, :], writes=[("wo", h2)], sem=("wo", h2))
    P.dma("pool", wl[:], io["wl"], writes=["wl"], sem="wl")

    xt = [P.sbuf("xt%d" % i, [128, 8, TT], F32) for i in range(2)]
    sq = P.sbuf("sq", [128, 8, TT], BF16)
    rt = P.sbuf("rt", [128, TT], F32)
    inv = P.sbuf("inv", [128, TT], F32)
    hb = [P.sbuf("h%d" % i, [128, 8, TT], BF16) for i in range(3)]
    NU = 3
    u = [P.sbuf("u%d" % i, [128, HALO + TT], F32) for i in range(NU)]
    pa = P.sbuf("pa", [128, HALO + TT], F32)
    pb = P.sbuf("pb", [128, HALO + TT], F32)
    tmp16 = P.sbuf("tmp16", [128, 16], F32)
    pooled = [P.sbuf("pooled%d" % i, [128, 4, TT], BF16) for i in range(2)]
    sz = [P.sbuf("sz%d" % i, [128, 4, TT], BF16) for i in range(2)]
    yb = [P.sbuf("y%d" % i, [128, 16, TT], BF16) for i in range(2)]
    uh = [P.sbuf("uh%d" % i, [128, 16, HALO], F32) for i in range(2)]
    ql = P.sbuf("ql", [128, 5, TT], F32)
    sql = P.sbuf("sql", [128, 5, TT], BF16)
    lat_o = [P.sbuf("lat_o%d" % i, [128, LATS, TT], BF16) for i in range(2)]
    for i in range(2):
        P.op("pool", lambda e, i=i: e.memset(lat_o[i][:], 0.0), writes=[("lat_o%d" % i, j) for j in range(LATS)])
    ucount = [0]

    def norm_h(xs, xkey, W, gcol0, hi):
        h = hb[hi]
        P.op("act", lambda e: e.activation(out=sq[:, :, :W], in_=xs[:, :, :W], func=AF.Square),
             reads=[xkey], writes=["sq"])
        rms_inv(P, banks, ones, sq, 8, W, D, rt, inv, ["sq"], "rt", "inv")
        for kc in range(8):
            P.op("dve", lambda e, kc=kc: e.scalar_tensor_tensor(
                out=h[:, kc, :W], in0=xs[:, kc, :W], scalar=vec[:, gcol0 + kc:gcol0 + kc + 1],
                in1=inv[:, :W], op0=ALU.mult, op1=ALU.mult),
                reads=[xkey, "inv", "vec"], writes=[("h", hi, kc)])

    def unit8(wkeys, lhs_of, W, hi):
        ps, pk = banks.next()
        h = hb[hi]

        def mm(e):
            for kc in range(8):
                r = e.matmul(ps[:, :W], lhsT=lhs_of(kc), rhs=h[:, kc, :W], start=(kc == 0), stop=(kc == 7))
            return r
        P.op("pe", mm, reads=[("h", hi, kc) for kc in range(8)] + list(wkeys), writes=[pk])
        return ps, pk

    xh = xt[1]
    P.dma("sp", xh[:, :, :HALO], io["xh"], writes=["xt1"], sem="xt1")
    norm_h(xh, "xt1", HALO, V_G0, 2)

    def halo(g):
        for f in range(4 * g, 4 * g + 4):
            ps, pk = unit8([("w_in", g)], lambda kc, f=f, g=g: w_in[:, kc, g * 1024 + (f % 4) * 128:g * 1024 + (f % 4 + 1) * 128], HALO, 2)
            P.op("act", lambda e, ps=ps, f=f: e.activation(out=uh[0][:, f, :], in_=ps[:, :HALO], func=AF.Copy),
                 reads=[pk], writes=[("uh", 0, f)])

    def F0(t):
        xs, xkey = xt[t % 2], "xt%d" % (t % 2)
        P.dma("sp", xs[:], xT[:, t, :, :], writes=[xkey], sem=xkey)
        norm_h(xs, xkey, TT, V_G0, t % 2)

    def F1(t, g):
        uhc, uhn = uh[t % 2], uh[(t + 1) % 2]
        slot = (t * 4 + g) % 2
        w = 2 << g
        hi = t % 2
        for fc in range(4):
            f = g * 4 + fc
            us = u[ucount[0] % NU]
            ukey = "u%d" % (ucount[0] % NU)
            ucount[0] += 1
            ps, pk = unit8([("w_in", g)], lambda kc, fc=fc, g=g: w_in[:, kc, g * 1024 + fc * 128:g * 1024 + (fc + 1) * 128], TT, hi)
            P.op("act", lambda e, ps=ps, us=us: e.activation(out=us[:, HALO:], in_=ps[:, :TT], func=AF.Copy),
                 reads=[pk], writes=[ukey])
            P.op("act", lambda e, ps=ps, f=f, uhn=uhn: e.activation(out=uhn[:, f, :], in_=ps[:, TT - HALO:TT], func=AF.Copy),
                 reads=[pk], writes=[("uh", (t + 1) % 2, f)])
            P.op("act", lambda e, us=us, f=f, uhc=uhc: e.activation(out=us[:, :HALO], in_=uhc[:, f, :], func=AF.Copy),
                 reads=[("uh", t % 2, f)], writes=[ukey])
            ps, pk = unit8([("w_in", g)], lambda kc, fc=fc, g=g: w_in[:, kc, g * 1024 + 512 + fc * 128:g * 1024 + 512 + (fc + 1) * 128], TT, hi)
            P.op("act", lambda e, ps=ps, slot=slot, fc=fc: e.activation(out=sz[slot][:, fc, :], in_=ps[:, :TT], func=AF.Silu),
                 reads=[pk], writes=[("sz", slot, fc)])
            E = HALO + TT
            src, skey = us, ukey
            bufs = [(pa, "pa"), (pb, "pb")]
            step, lo, bi = 1, 1, 0
            while step < w:
                dst, dkey = bufs[bi]
                P.op("dve", lambda e, dst=dst, src=src, step=step, lo=lo: e.tensor_tensor(
                    out=dst[:, lo:E], in0=src[:, lo:E], in1=src[:, lo - step:E - step], op=ALU.add),
                    reads=[skey], writes=[dkey])
                src, skey = dst, dkey
                step *= 2
                lo = 2 * step - 1
                bi ^= 1
            P.op("dve", lambda e, src=src, us=us, slot=slot, fc=fc, w=w: e.scalar_tensor_tensor(
                out=pooled[slot][:, fc, :], in0=src[:, HALO:E], scalar=1.0 / w, in1=us[:, HALO:E],
                op0=ALU.mult, op1=ALU.subtract),
                reads=[skey, ukey], writes=[("pooled", slot, fc)])
            if t == 0:
                P.op("dve", lambda e, src=src, g=g: e.tensor_tensor(
                    out=tmp16[:], in0=src[:, HALO:HALO + 16], in1=rc[:, g * 16:(g + 1) * 16], op=ALU.mult),
                    reads=[skey, "rc"], writes=["tmp16"])
                P.op("dve", lambda e, us=us, slot=slot, fc=fc: e.tensor_tensor(
                    out=pooled[slot][:, fc, 0:16], in0=tmp16[:], in1=us[:, HALO:HALO + 16], op=ALU.subtract),
                    reads=["tmp16", ukey], writes=[("pooled", slot, fc)])

    def F2(t, g):
        slot = (t * 4 + g) % 2
        y, yk = yb[t % 2], t % 2
        for fo in range(4):
            f = g * 4 + fo
            ps, pk = banks.next()

            def mm(e, ps=ps, g=g, fo=fo, slot=slot):
                for kc in range(4):
                    r = e.matmul(ps[:, :TT], lhsT=wg[:, g, kc, fo * 128:(fo + 1) * 128],
                                 rhs=pooled[slot][:, kc, :], start=(kc == 0), stop=(kc == 3))
                return r
            P.op("pe", mm, reads=[("pooled", slot, kc) for kc in range(4)] + [("wg", g)], writes=[pk])
            P.op("dve", lambda e, ps=ps, f=f, fo=fo, slot=slot, y=y: e.scalar_tensor_tensor(
                out=y[:, f, :], in0=ps[:, :TT], scalar=vec[:, V_SC + f:V_SC + f + 1], in1=sz[slot][:, fo, :],
                op0=ALU.mult, op1=ALU.mult),
                reads=[pk, ("sz", slot, fo), "vec"], writes=[("y", yk, f)])

    def F3(t):
        xs, xkey = xt[t % 2], "xt%d" % (t % 2)
        y, yk = yb[t % 2], t % 2
        for dc in range(8):
            ps, pk = banks.next()

            def mm(e, ps=ps, dc=dc, y=y):
                for kc in range(16):
                    r = e.matmul(ps[:, :TT], lhsT=wo[:, kc, dc * 128:(dc + 1) * 128], rhs=y[:, kc, :],
                                 start=(kc == 0), stop=(kc == 15))
                return r
            P.op("pe", mm, reads=[("y", yk, kc) for kc in range(16)] + [("wo", 0), ("wo", 1)], writes=[pk])
            P.op("dve", lambda e, ps=ps, dc=dc, xs=xs: e.tensor_tensor(
                out=xs[:, dc, :], in0=ps[:, :TT], in1=xs[:, dc, :], op=ALU.add),
                reads=[pk, xkey], writes=[xkey])

    def F4(t):
        xs, xkey = xt[t % 2], "xt%d" % (t % 2)
        P.dma("pool", x1T[:, t, :, :], xs[:], reads=[xkey], writes=[("x1T", t)],
              sem=("x1st", t % 2), final=True)
        norm_h(xs, xkey, TT, V_G1, 2)

    def F5(t):
        lo_t, lkey = lat_o[t % 2], "lat_o%d" % (t % 2)
        h = hb[2]
        for j in range(5):
            ps, pk = unit8(["wl"], lambda kc, j=j: wl[:, kc, j * 128:(j + 1) * 128], TT, 2)
            P.op("act", lambda e, ps=ps, j=j: e.activation(out=ql[:, j, :], in_=ps[:, :TT], func=AF.Copy),
                 reads=[pk], writes=[("ql", j)])
        for j in (5, 6):
            ps, pk = unit8(["wl"], lambda kc, j=j: wl[:, kc, 640 + (j - 5) * 64:768 + (j - 5) * 64], TT, 2)
            P.op("act", lambda e, ps=ps, lo_t=lo_t, j=j: e.activation(out=lo_t[0:64, j, :], in_=ps[0:64, :TT], func=AF.Copy),
                 reads=[pk], writes=[(lkey, j)])
        for (j0, nj, nfeat, gc) in ((0, 3, 384, V_GQ), (3, 2, 256, V_GKV)):
            P.op("act", lambda e, j0=j0, nj=nj: e.activation(out=sql[:, j0:j0 + nj, :], in_=ql[:, j0:j0 + nj, :], func=AF.Square),
                 reads=[("ql", j) for j in range(j0, j0 + nj)], writes=[("sql", j0)])
            ps, pk = banks.next()

            def mm(e, ps=ps, j0=j0, nj=nj):
                for k in range(nj):
                    r = e.matmul(ps[:, :TT], lhsT=ones[:], rhs=sql[:, j0 + k, :], start=(k == 0), stop=(k == nj - 1))
                return r
            P.op("pe", mm, reads=[("sql", j0), "ones"], writes=[pk])
            P.op("act", lambda e, ps=ps, nfeat=nfeat: e.activation(out=rt[:, :TT], in_=ps[:, :TT], func=AF.Ln, bias=EPS, scale=1.0 / nfeat),
                 reads=[pk], writes=["rt"])
            P.op("act", lambda e: e.activation(out=inv[:, :TT], in_=rt[:, :TT], func=AF.Exp, scale=-0.5), reads=["rt"], writes=["inv"])
            for k in range(nj):
                j = j0 + k
                P.op("dve", lambda e, j=j, k=k, gc=gc, lo_t=lo_t: e.scalar_tensor_tensor(
                    out=lo_t[:, j, :], in0=ql[:, j, :], scalar=vec[:, gc + k:gc + k + 1], in1=inv[:, :TT],
                    op0=ALU.mult, op1=ALU.mult),
                    reads=[("ql", j), "inv", "vec"], writes=[(lkey, j)])
        P.dma("pool", lat[t * 128:(t + 1) * 128, :], lo_t[:].rearrange("p j c -> p (j c)"), reads=[(lkey, j) for j in range(LATS)],
              writes=[("lat", t)], sem=("latst", t % 2), final=True)
        if t % 2 == 1 and io.get("ag1") is not None:
            io["ag1"](P, t // 2)

    F0(0)
    for t in range(NT):
        prev = t - 1
        if t == 0:
            halo(0)
        F1(t, 0)
        if prev >= 0:
            F2(prev, 3)
        if t == 0:
            halo(1)
        F1(t, 1)
        if prev >= 0:
            F3(prev)
        F2(t, 0)
        if t == 0:
            halo(2)
        F1(t, 2)
        if prev >= 0:
            F4(prev)
        F2(t, 1)
        if t == 0:
            halo(3)
        F1(t, 3)
        if t + 1 < NT:
            F0(t + 1)
        if prev >= 0:
            F5(prev)
        F2(t, 2)
    F2(NT - 1, 3)
    F3(NT - 1)
    F4(NT - 1)
    F5(NT - 1)


PI = float(np.pi)
C1 = 6.28125
C2 = float(2.0 * np.pi - 6.28125)


def phase_B(nc, P, io):
    QT = 512
    NQ = S // QT
    latA, oT = io["latA"], io["oT"]
    ones = P.sbuf("ones", [128, 128], BF16)
    P.op("pool", lambda e: e.memset(ones[:], 1.0), writes=["ones"])
    tri = P.sbuf("tri", [128, 128], BF16)
    P.dma("pool", tri[:], io["tri"], writes=["tri"], sem="tri")
    rcol = P.sbuf("rcol", [128, 2], F32)
    P.dma("sp", rcol[:], io["rcol"], writes=["rcol"], sem="rcol")
    wq = P.sbuf("wq", [128, 3, HPC * 320], BF16)
    wk = P.sbuf("wk", [128, 2, HPC * 128], BF16)
    wv = P.sbuf("wv", [128, 2, HPC * 128], BF16)
    P.dma("pool", wk[:], io["wk"], writes=["wk"], sem="wk")
    P.dma("pool", wv[:], io["wv"], writes=["wv"], sem="wv")
    P.dma("pool", wq[:], io["wq"], writes=["wq"], sem="wq")

    KT = P.sbuf("KT", [128, HPC, S], BF16)
    V = P.sbuf("V", [128, S // 128, HPC * 128], BF16)
    KR = P.sbuf("KR", [128, S], BF16)
    pst = [P.psum("ps%d" % i, [128, 512], F32) for i in range(8)]
    pskey = [("ps", i) for i in range(8)]
    latA_v = latA.rearrange("(k r t p) (j c) -> p k r t j c", k=4, r=4, t=2, p=128, j=LATS)

    SB = [0, 1, 2, 7]
    OB = [3, 4]
    DB = [5, 6]
    sbi = [0]

    def sbank():
        b = SB[sbi[0] % 4]
        sbi[0] += 1
        return pst[b], pskey[b]

    posi = P.sbuf("posi", [128, QT], I32)
    ang = P.sbuf("ang", [128, QT], F32)
    kf = P.sbuf("kf", [128, QT], F32)
    rr = P.sbuf("rr", [128, QT], F32)
    CS = [P.sbuf("CS%d" % i, [128, QT], F32) for i in range(2)]
    Ct = [c[0:64, :] for c in CS]
    S0 = P.sbuf("S0", [64, QT], F32)
    kk = [P.sbuf("kk%d" % i, [64, 2, 2, QT // 2], BF16) for i in range(2)]

    def v2(ap):
        return ap.rearrange("p (t c) -> p t c", t=2)
    kvn = [P.sbuf("kvn%d" % i, [128, 2, 2, QT // 2], BF16) for i in range(2)]
    qn = [P.sbuf("qn%d" % i, [128, 2, 3, QT // 2], BF16) for i in range(2)]
    t1f = P.sbuf("t1", [128, QT], F32)
    bi = [0]
    ev = [0]

    def evac_copy(dst, src, rkeys, wkeys):
        ev[0] += 1
        if ev[0] % 2:
            P.op("act", lambda e: e.activation(out=dst, in_=src, func=AF.Copy), reads=rkeys, writes=wkeys)
        else:
            P.op("dve", lambda e: e.tensor_copy(out=dst, in_=src), reads=rkeys, writes=wkeys)

    def prep_steps(tt):
        r_ = tt // 4
        sl = tt % 2
        g0 = tt * QT
        ck = ("latA", tt % 4)
        steps = []

        def dve(fn, reads, writes):
            steps.append(lambda: P.op("dve", fn, reads=reads, writes=writes))

        def loads():
            P.dma("sp", posi[:], io["posr"][:, g0:g0 + QT], writes=["posi"], sem="posi")
            P.dma("sp", kk[sl][:], latA_v[0:64, tt % 4, r_, :, 5:7, :], reads=[ck], writes=[("kk", sl)], sem=("kk", sl))
            P.dma("sp", kvn[sl][:], latA_v[:, tt % 4, r_, :, 3:5, :], reads=[ck], writes=[("kvn", sl)], sem=("kvn", sl))
            P.dma("sp", qn[sl][:], latA_v[:, tt % 4, r_, :, 0:3, :], reads=[ck], writes=[("qn", sl)], sem=("qn", sl))
        steps.append(loads)
        for hl in range(HPC):
            def kstep(hl=hl):
                ps, pk = sbank()

                def mm(e, ps=ps, hl=hl, sl=sl):
                    for kc in range(2):
                        r = e.matmul(ps[:], lhsT=wk[:, kc, hl * 128:(hl + 1) * 128], rhs=kvn[sl][:, :, kc, :],
                                     start=(kc == 0), stop=(kc == 1))
                    return r
                P.op("pe", mm, reads=[("kvn", sl), "wk"], writes=[pk])
                evac_copy(KT[:, hl, g0:g0 + QT], ps[:], [pk], [("KT", hl, tt)])
            steps.append(kstep)
        for sub in range(4):
            def vstep(sub=sub):
                ps, pk = sbank()

                def mm(e, ps=ps, sub=sub, sl=sl):
                    for kc in range(2):
                        r = e.matmul(ps[:], lhsT=kvn[sl][:, sub // 2, kc, (sub % 2) * 128:(sub % 2 + 1) * 128], rhs=wv[:, kc, :],
                                     start=(kc == 0), stop=(kc == 1))
                    return r
                P.op("pe", mm, reads=[("kvn", sl), "wv"], writes=[pk])
                evac_copy(V[:, tt * 4 + sub, :], ps[:], [pk], [("V", tt * 4 + sub)])
            steps.append(vstep)
        dve(lambda e: e.tensor_scalar(out=ang[:], in0=posi[:], scalar1=rcol[:, 0:1], scalar2=None, op0=ALU.mult), ["posi", "rcol"], ["ang"])
        dve(lambda e: e.tensor_scalar(out=posi[:], in0=ang[:], scalar1=1.0 / (2 * PI), scalar2=0.5, op0=ALU.mult, op1=ALU.add), ["ang"], ["posi"])
        dve(lambda e: e.scalar_tensor_tensor(out=rr[:], in0=posi[:], scalar=-C1, in1=ang[:], op0=ALU.mult, op1=ALU.add), ["posi", "ang"], ["rr"])
        dve(lambda e: e.scalar_tensor_tensor(out=rr[:], in0=posi[:], scalar=-C2, in1=rr[:], op0=ALU.mult, op1=ALU.add), ["posi", "rr"], ["rr"])
        dve(lambda e: e.tensor_scalar(out=kf[:], in0=rr[:], scalar1=-PI, scalar2=2 * PI, op0=ALU.is_lt, op1=ALU.mult), ["rr"], ["kf"])
        dve(lambda e: e.tensor_tensor(out=rr[:], in0=rr[:], in1=kf[:], op=ALU.add), ["rr", "kf"], ["rr"])
        dve(lambda e: e.tensor_scalar(out=ang[:], in0=rr[:], scalar1=PI / 2, scalar2=None, op0=ALU.add), ["rr"], ["ang"])
        dve(lambda e: e.tensor_scalar(out=kf[:], in0=ang[:], scalar1=PI, scalar2=-2 * PI, op0=ALU.is_gt, op1=ALU.mult), ["ang"], ["kf"])
        dve(lambda e: e.tensor_tensor(out=ang[:], in0=ang[:], in1=kf[:], op=ALU.add), ["ang", "kf"], ["ang"])

        def sins():
            P.op("act", lambda e: e.activation(out=S0[:], in_=rr[0:64, :], func=AF.Sin, scale=rcol[0:64, 1:2]),
                 reads=["rr", "rcol"], writes=["S0"])
            P.op("act", lambda e: e.activation(out=CS[sl][64:128, :], in_=rr[64:128, :], func=AF.Sin, scale=rcol[64:128, 1:2]),
                 reads=["rr", "rcol"], writes=[("St", sl)])
            P.op("act", lambda e: e.activation(out=CS[sl][0:64, :], in_=ang[0:64, :], func=AF.Sin), reads=["ang"], writes=[("Ct", sl)])
        nA = len(steps)
        steps.append(sins)
        dve(lambda e: e.tensor_tensor(out=v2(kf[0:64, :]), in0=kk[sl][:, :, 0, :], in1=v2(Ct[sl]), op=ALU.mult), [("kk", sl), ("Ct", sl)], ["kf"])
        dve(lambda e: e.tensor_tensor(out=v2(rr[0:64, :]), in0=kk[sl][:, :, 1, :], in1=v2(S0[:]), op=ALU.mult), [("kk", sl), "S0"], ["rr"])
        dve(lambda e: e.tensor_tensor(out=KR[0:64, g0:g0 + QT], in0=kf[0:64, :], in1=rr[0:64, :], op=ALU.add), ["kf", "rr"], [("KR", tt)])
        steps.append(lambda: P.dma("sp", KR[64:128, g0:g0 + QT], KR[0:64, g0:g0 + QT], reads=[("KR", tt)], writes=[("KR2", tt)], sem=("krd", sl)))
        return steps[:nA], steps[nA:nA + 1], steps[nA + 1:]

    pendA, pendB, pendC = [], [], []

    def drain(lst, n):
        for _ in range(min(n, len(lst))):
            lst.pop(0)()

    Qn = [P.sbuf("Qn%d" % i, [128, QT], BF16) for i in range(2)]
    Qr = [P.sbuf("Qr%d" % i, [128, QT], BF16) for i in range(2)]
    NP = 4
    pT = [P.sbuf("pT%d" % i, [128, QT], BF16) for i in range(NP)]
    rden = t1f
    ot1 = P.sbuf("ot", [128, HPC, QT], BF16)
    ot = [ot1, ot1]
    psm = [P.sbuf("psm%d" % i, [128, QT], BF16) for i in range(4)]
    psi = [0]
    pti = [0]
    WQH = 320

    def qproj(it):
        qi, hl = it // HPC, it % HPC
        qs, s2 = qi % 2, it % 2
        ps, pk = sbank()

        def mm(e, ps=ps, hl=hl, qs=qs):
            for kc in range(3):
                r = e.matmul(ps[:], lhsT=wq[:, kc, hl * WQH:hl * WQH + 128], rhs=qn[qs][:, :, kc, :],
                             start=(kc == 0), stop=(kc == 2))
            return r
        P.op("pe", mm, reads=[("qn", qs), "wq"], writes=[pk])
        P.op("act", lambda e, ps=ps, s2=s2: e.activation(out=Qn[s2][:], in_=ps[:], func=AF.Copy),
             reads=[pk], writes=[("Qn", s2)])
        ps, pk = sbank()
        off = hl * WQH + 128

        def mm(e, ps=ps, off=off, qs=qs):
            for kc in range(3):
                r = e.matmul(ps[:], lhsT=wq[:, kc, off:off + 128], rhs=qn[qs][:, :, kc, :],
                             start=(kc == 0), stop=(kc == 2))
            return r
        P.op("pe", mm, reads=[("qn", qs), "wq"], writes=[pk])
        P.op("dve", lambda e, ps=ps, qs=qs, s2=s2: e.tensor_tensor(out=Qr[s2][:], in0=ps[:], in1=CS[qs][:], op=ALU.mult),
             reads=[pk, ("Ct", qs), ("St", qs)], writes=[("Qr", s2)])

    def attn(it):
        qi, hl = it // HPC, it % HPC
        qs, s2 = qi % 2, it % 2
        nk = 4 * (qi + 1)
        ob, db = OB[s2], DB[s2]
        LA = 2
        stiles = {}
        den_at = {}

        def emit_S(kj):
            m = kj - 4 * qi
            lo = 128 * m if m > 0 else 0
            ps, pk = sbank()

            def mm(e, ps=ps, kj=kj, lo=lo, hl=hl, s2=s2):
                e.matmul(ps[:, lo:QT], lhsT=KT[:, hl, kj * 128:(kj + 1) * 128], rhs=Qn[s2][:, lo:QT],
                         start=True, stop=False)
                return e.matmul(ps[:, lo:QT], lhsT=KR[:, kj * 128:(kj + 1) * 128], rhs=Qr[s2][:, lo:QT],
                                start=False, stop=True)
            P.op("pe", mm, reads=[("KT", hl, kj // 4), ("KR", kj // 4), ("KR2", kj // 4), ("Qn", s2), ("Qr", s2)], writes=[pk])
            pi_ = pti[0] % NP
            pti[0] += 1
            P.op("act", lambda e, ps=ps, lo=lo, pi_=pi_: e.activation(out=pT[pi_][:, lo:QT], in_=ps[:, lo:QT], func=AF.Exp, scale=SCALE),
                 reads=[pk], writes=[("pT", pi_)])
            if m >= 0:
                P.op("dve", lambda e, lo=lo, pi_=pi_: e.tensor_tensor(out=pT[pi_][:, lo:lo + 128], in0=pT[pi_][:, lo:lo + 128], in1=tri[:], op=ALU.mult),
                     reads=[("pT", pi_), "tri"], writes=[("pT", pi_)])
            sm = None
            if m < 0 and kj % 2 == 1:
                nfull = 4 * qi
                sp_ = (kj // 2) % 4
                pj = stiles[kj - 1][0]
                P.op("dve", lambda e, sp_=sp_, pj=pj, pi_=pi_: e.tensor_tensor(out=psm[sp_][:], in0=pT[pj][:], in1=pT[pi_][:], op=ALU.add),
                     reads=[("pT", pj), ("pT", pi_)], writes=[("psm", sp_)])
                if kj % 4 == 3:
                    P.op("dve", lambda e, sp_=sp_: e.tensor_tensor(out=psm[sp_][:], in0=psm[sp_ - 1][:], in1=psm[sp_][:], op=ALU.add),
                         reads=[("psm", sp_ - 1), ("psm", sp_)], writes=[("psm", sp_)])
                    if kj % 8 == 7:
                        P.op("dve", lambda e: e.tensor_tensor(out=psm[3][:], in0=psm[1][:], in1=psm[3][:], op=ALU.add),
                             reads=[("psm", 1), ("psm", 3)], writes=[("psm", 3)])
                        den_at[min(kj + 2, nfull - 1)] = (3, kj == 7)
                    elif kj == nfull - 1:
                        den_at[nfull - 1] = (1, kj == 3)
            stiles[kj] = (pi_, lo, sm)

        def emit_PV(kj):
            pi_, lo, sm = stiles.pop(kj)
            m = kj - 4 * qi
            sm, first = den_at.pop(kj, (None, False))

            def mm(e, kj=kj, lo=lo, pi_=pi_, hl=hl, ob=ob, db=db, nk=nk, sm=sm, m=m, first=first):
                r = e.matmul(pst[ob][:, lo:QT], lhsT=V[:, kj, hl * 128:(hl + 1) * 128], rhs=pT[pi_][:, lo:QT],
                             start=(kj == 0), stop=(kj == nk - 1))
                if sm is not None:
                    r = e.matmul(pst[db][:], lhsT=ones[:], rhs=psm[sm][:], start=first, stop=False)
                if m >= 0:
                    r = e.matmul(pst[db][:, lo:QT], lhsT=ones[:], rhs=pT[pi_][:, lo:QT],
                                 start=(kj == 0), stop=(kj == nk - 1))
                return r
            rd = [("pT", pi_), ("V", kj), "ones"] + ([("psm", sm)] if sm is not None else [])
            P.op("pe", mm, reads=rd, writes=[pskey[ob], pskey[db]])

        if hl == 1:
            drain(pendA, len(pendA))
        if hl == 2:
            drain(pendB, len(pendB))
        if hl == 3:
            drain(pendC, len(pendC))
        perA = -(-len(pendA) // nk) if hl == 0 else 0
        perC = -(-len(pendC) // nk) if hl == 2 else 0
        for kj in range(nk + LA):
            if kj < nk:
                emit_S(kj)
                drain(pendA, perA)
                drain(pendC, perC)
            if kj - LA >= 0:
                emit_PV(kj - LA)
            if kj == 1 and it + 1 < NQ * HPC:
                qproj(it + 1)
        P.op("act", lambda e, db=db: e.activation(out=rden[:], in_=pst[db][:], func=AF.Ln), reads=[pskey[db]], writes=["t1"])
        P.op("act", lambda e: e.activation(out=rden[:], in_=rden[:], func=AF.Exp, scale=-1.0), reads=["t1"], writes=["t1"])
        P.op("dve", lambda e, ob=ob, qs=qs, hl=hl: e.tensor_tensor(out=ot[qs][:, hl, :], in0=pst[ob][:], in1=rden[:], op=ALU.mult),
             reads=[pskey[ob], "t1"], writes=[("ot", hl)])

    for lst in prep_steps(0):
        for st in lst:
            st()
    qproj(0)
    for qi in range(NQ):
        if qi + 1 < NQ:
            a_, b_, c_ = prep_steps(qi + 1)
            pendA.extend(a_), pendB.extend(b_), pendC.extend(c_)
        for hl in range(HPC):
            attn(qi * HPC + hl)
        qs = qi % 2
        P.dma("pool", oT[qi * 128:(qi + 1) * 128, :], ot[qs][:].rearrange("p h c -> p (h c)"),
              reads=[("ot", hl) for hl in range(HPC)], writes=[("oT", qi)], sem="ost", final=True)
        if qi % 2 == 1 and io.get("ag2") is not None:
            io["ag2"](P, qi // 2)
        if io.get("prefetch") is not None:
            io["prefetch"](P, qi)


def phase_C(nc, P, io):
    TT, NT = 512, TOK // 512
    x1T, oTo, outT = io["x1T"], io["oTo"], io["outT"]
    banks = Banks(P)
    ones = P.sbuf("ones", [128, 128], BF16)
    P.op("pool", lambda e: e.memset(ones[:], 1.0), writes=["ones"])
    vec = P.sbuf("vec", [128, NV], F32)
    P.dma("sp", vec[:], io["vec"], writes=["vec"], sem="vec")
    wz = P.sbuf("wz", [128, 8, 2048], BF16)
    wo = P.sbuf("wo", [128, 16, 1024], BF16)
    wq_, wsrc, wosrc = ("act", io["wz_bf"], io["wo_bf"]) if io.get("wz_bf") is not None else ("pool", io["wz"], io["mla_w_out"])

    def wload(first):
        for q4 in ([0] if first else [1, 2, 3]):
            P.dma(wq_, wz[:, :, q4 * 512:(q4 + 1) * 512], wsrc[:, :, q4 * 512:(q4 + 1) * 512],
                  reads=([] if first else ["xt0"]), writes=[("wz", q4)], sem=("wz", q4))
        if not first:
            for h2 in range(2):
                P.dma(wq_, wo[:, h2 * 8:(h2 + 1) * 8, :].rearrange("p (a b) f -> p a (b f)", b=2),
                      wosrc[:, h2 * 4:(h2 + 1) * 4, :], reads=["xt0"], writes=[("wo", h2)], sem=("wo", h2))
    wload(True)
    xt = [P.sbuf("xt%d" % i, [128, 2, 8, 256], F32) for i in range(3)]
    og = [P.sbuf("og%d" % i, [128, 16, TT], BF16) for i in range(2)]
    sq = P.sbuf("sq", [128, 2, 8, 256], BF16)
    rt = P.sbuf("rt", [128, TT], F32)
    inv = rt
    hb = [P.sbuf("h%d" % i, [128, 8, TT], BF16) for i in range(2)]
    szt = [P.sbuf("szt%d" % i, [128, TT], F32) for i in range(2)]
    yb = [P.sbuf("y%d" % i, [128, 16, TT], BF16) for i in range(2)]

    def v2(ap):
        return ap.rearrange("p (a c) -> p a c", a=2)

    def stats(xs, xkey):
        P.op("act", lambda e, xs=xs: e.activation(out=sq[:], in_=xs[:], func=AF.Square), reads=[xkey], writes=["sq"])
        ps, pk = banks.next()

        def mm(e, ps=ps):
            for k in range(8):
                r = e.matmul(ps[:], lhsT=ones[:], rhs=sq[:, :, k, :], start=(k == 0), stop=(k == 7))
            return r
        P.op("pe", mm, reads=["sq", "ones"], writes=[pk])
        P.op("act", lambda e, ps=ps: e.activation(out=rt[:], in_=ps[:], func=AF.Ln, bias=EPS, scale=1.0 / D),
             reads=[pk], writes=["rt"])
        P.op("act", lambda e: e.activation(out=inv[:], in_=rt[:], func=AF.Exp, scale=-0.5), reads=["rt"], writes=["rt"])

    def C0(t):
        xs, xkey = xt[t % 3], "xt%d" % (t % 3)
        os_, okey = og[t % 2], "og%d" % (t % 2)
        h = hb[t % 2]
        P.dma("sp", xs[:], x1T[:, 2 * t:2 * t + 2, :, :], reads=["x1T"], writes=[xkey], sem=xkey)
        for rr in range(4):
            io["oTo_dma"](P, t, rr, os_[:, rr * 4:(rr + 1) * 4, :].rearrange("p h c -> p (h c)"), (okey, rr))
        stats(xs, xkey)
        for kc in range(8):
            P.op("dve", lambda e, kc=kc, xs=xs, h=h: e.scalar_tensor_tensor(
                out=v2(h[:, kc, :]), in0=xs[:, :, kc, :], scalar=vec[:, V_G1 + kc:V_G1 + kc + 1], in1=v2(inv[:]),
                op0=ALU.mult, op1=ALU.mult), reads=[xkey, "rt", "vec"], writes=[("h", t % 2, kc)])

    def C1(t):
        os_, okey = og[t % 2], "og%d" % (t % 2)
        h, y = hb[t % 2], yb[t % 2]
        for f in range(16):
            ps, pk = banks.next()

            def mm(e, ps=ps, f=f, h=h):
                for kc in range(8):
                    r = e.matmul(ps[:], lhsT=wz[:, kc, f * 128:(f + 1) * 128], rhs=h[:, kc, :], start=(kc == 0), stop=(kc == 7))
                return r
            P.op("pe", mm, reads=[("h", t % 2, kc) for kc in range(8)] + [("wz", f // 4)], writes=[pk])
            zs = f % 2
            P.op("act", lambda e, ps=ps, zs=zs: e.activation(out=szt[zs][:], in_=ps[:], func=AF.Silu), reads=[pk], writes=[("szt", zs)])
            P.op("dve", lambda e, zs=zs, f=f, os_=os_, y=y: e.tensor_tensor(out=y[:, f, :], in0=szt[zs][:], in1=os_[:, f, :], op=ALU.mult),
                 reads=[("szt", zs), (okey, f // 4)], writes=[("y", t % 2, f)])

    def C2(t):
        xs, xkey = xt[t % 3], "xt%d" % (t % 3)
        y = yb[t % 2]
        for dc in range(8):
            ps, pk = banks.next()

            def mm(e, ps=ps, dc=dc, y=y):
                for kc in range(16):
                    r = e.matmul(ps[:], lhsT=wo[:, kc, dc * 128:(dc + 1) * 128], rhs=y[:, kc, :], start=(kc == 0), stop=(kc == 15))
                return r
            P.op("pe", mm, reads=[("y", t % 2, kc) for kc in range(16)] + [("wo", 0), ("wo", 1)], writes=[pk])
            P.op("dve", lambda e, ps=ps, dc=dc, xs=xs: e.tensor_tensor(out=xs[:, :, dc, :], in0=v2(ps[:]), in1=xs[:, :, dc, :], op=ALU.add),
                 reads=[pk, xkey], writes=[xkey])

    def C3(t):
        xs, xkey = xt[t % 3], "xt%d" % (t % 3)
        stats(xs, xkey)
        for kc in range(8):
            P.op("dve", lambda e, kc=kc, xs=xs: e.scalar_tensor_tensor(
                out=xs[:, :, kc, :], in0=xs[:, :, kc, :], scalar=vec[:, V_GF + kc:V_GF + kc + 1], in1=v2(inv[:]),
                op0=ALU.mult, op1=ALU.mult), reads=[xkey, "rt", "vec"], writes=[xkey])
        P.dma("pool", outT[:, 2 * t:2 * t + 2, :, :], xs[:], reads=[xkey],
              writes=[("outT", t)], sem="outst", final=True)

    C0(0)
    wload(False)
    C1(0)
    for t in range(NT):
        if t + 1 < NT:
            C0(t + 1)
        C2(t)
        if t + 1 < NT:
            C1(t + 1)
        C3(t)


def _dram(nc, name, shape, dt, kind):
    return nc.dram_tensor(name, list(shape), dt, kind=kind).ap()


def build_A():
    nc = bass.Bass("TRN2", target_bir_lowering=False)
    io = {
        "xT": _dram(nc, "xT", [128, 8, 8, 256], F32, "ExternalInput"),
        "xh": _dram(nc, "xh", [128, 8, HALO], F32, "ExternalInput"),
        "vec": _dram(nc, "vec", [128, NV], F32, "ExternalInput"),
        "rc": _dram(nc, "rc", [128, 64], F32, "ExternalInput"),
        "pool_w_in": _dram(nc, "pool_w_in", [128, 4, 8, 1024], F32, "ExternalInput"),
        "pool_w_group": _dram(nc, "pool_w_group", [128, 4, 4, 512], F32, "ExternalInput"),
        "pool_w_out": _dram(nc, "pool_w_out", [128, 8, 2048], F32, "ExternalInput"),
        "wl": _dram(nc, "wl", [128, 8, LATR], F32, "ExternalInput"),
        "x1T": _dram(nc, "x1T", [128, 8, 8, 256], F32, "ExternalOutput"),
        "lat": _dram(nc, "lat", [8 * 128, LATS * 256], BF16, "ExternalOutput"),
    }
    with ExitStack() as es:
        P = Prog(nc, es)
        phase_A(nc, P, io)
        P.emit()
    return nc


def build_B():
    nc = bass.Bass("TRN2", target_bir_lowering=False)
    io = {
        "latA": _dram(nc, "latA", [4 * 8 * 128, LATS * 256], BF16, "ExternalInput"),
        "posr": _dram(nc, "posr", [128, S], I32, "ExternalInput"),
        "rcol": _dram(nc, "rcol", [128, 2], F32, "ExternalInput"),
        "tri": _dram(nc, "tri", [128, 128], F32, "ExternalInput"),
        "wq": _dram(nc, "wq", [128, 3, HPC * 320], F32, "ExternalInput"),
        "wk": _dram(nc, "wk", [128, 2, HPC * 128], F32, "ExternalInput"),
        "wv": _dram(nc, "wv", [128, 2, HPC * 128], F32, "ExternalInput"),
        "oT": _dram(nc, "oT", [16 * 128, HPC * 512], BF16, "ExternalOutput"),
        "cs": nc.dram_tensor("cs", [16, 64, 2 * 512], F32).ap(),
    }
    with ExitStack() as es:
        P = Prog(nc, es)
        phase_B(nc, P, io)
        P.emit()
    return nc


def build_C():
    nc = bass.Bass("TRN2", target_bir_lowering=False)
    oTo = _dram(nc, "oTo", [4 * 4 * 128, 2048], BF16, "ExternalInput")
    io = {
        "x1T": _dram(nc, "x1T", [128, 8, 8, 256], F32, "ExternalInput"),
        "oTo": oTo,
        "oTo_dma": lambda P, t, rr, dst, key: P.dma("sp", dst, oTo[(t * 4 + rr) * 128:(t * 4 + rr + 1) * 128, :],
                                                  reads=["oTo"], writes=[key], sem=key),
        "vec": _dram(nc, "vec", [128, NV], F32, "ExternalInput"),
        "wz": _dram(nc, "wz", [128, 8, 2048], F32, "ExternalInput"),
        "mla_w_out": _dram(nc, "mla_w_out", [128, 8, 2048], F32, "ExternalInput"),
        "outT": _dram(nc, "outT", [128, 8, 8, 256], F32, "ExternalOutput"),
    }
    with ExitStack() as es:
        P = Prog(nc, es)
        phase_C(nc, P, io)
        P.emit()
    return nc


def build_fused():
    nc = bass.Bass("TRN2", target_bir_lowering=False)
    ein = lambda name, shape, dt: _dram(nc, name, shape, dt, "ExternalInput")
    x1s = nc.dram_tensor("x1s", [128, 8, 8, 256], F32).ap()
    lat_own = nc.dram_tensor("lat_own", [8 * 128, LATS * 256], BF16).ap()
    latA = nc.dram_tensor("latA", [4 * 8 * 128, LATS * 256], BF16).ap()
    o_own = nc.dram_tensor("o_own", [16 * 128, HPC * 512], BF16).ap()
    oA = nc.dram_tensor("oA", [4 * 16 * 128, HPC * 512], BF16).ap()
    vec = ein("vec", [128, NV], F32)
    groups = [[0, 1, 2, 3], [4, 5, 6, 7]]
    ioA = {
        "xT": ein("xT", [128, 8, 8, 256], F32), "xh": ein("xh", [128, 8, HALO], F32), "vec": vec,
        "rc": ein("rc", [128, 64], F32), "pool_w_in": ein("pool_w_in", [128, 4, 8, 1024], F32),
        "pool_w_group": ein("pool_w_group", [128, 4, 4, 512], F32), "pool_w_out": ein("pool_w_out", [128, 8, 2048], F32),
        "wl": ein("wl", [128, 8, LATR], F32), "x1T": x1s, "lat": lat_own,
    }
    ioB = {
        "latA": latA, "posr": ein("posr", [128, S], I32), "rcol": ein("rcol", [128, 2], F32), "tri": ein("tri", [128, 128], F32),
        "wq": ein("wq", [128, 3, HPC * 320], F32), "wk": ein("wk", [128, 2, HPC * 128], F32),
        "wv": ein("wv", [128, 2, HPC * 128], F32), "oT": o_own, "cs": nc.dram_tensor("cs", [16, 64, 2 * 512], F32).ap(),
    }

    def ag1(P, k):
        P.custom_dma("pool", lambda e: e.collective_compute(
            "AllGather", ALU.bypass, replica_groups=groups,
            ins=[lat_own[k * 256:(k + 1) * 256, :]], outs=[latA[k * 1024:(k + 1) * 1024, :]]),
            reads=[("lat", 2 * k), ("lat", 2 * k + 1)], writes=[("latA", k)], sem=("ag1", k), inc=1)

    def ag2(P, m):
        P.custom_dma("pool", lambda e: e.collective_compute(
            "AllGather", ALU.bypass, replica_groups=groups,
            ins=[o_own[m * 256:(m + 1) * 256, :]], outs=[oA[m * 1024:(m + 1) * 1024, :]]),
            reads=[("oT", 2 * m), ("oT", 2 * m + 1)], writes=[("oTo", m)], sem=("ag2", m), inc=1)

    ioA["ag1"] = ag1
    ioB["ag2"] = ag2
    wz_bf = nc.dram_tensor("wz_bf", [128, 8, 2048], BF16).ap()
    wo_bf = nc.dram_tensor("wo_bf", [128, 8, 2048], BF16).ap()

    def prefetch(P, i):
        name, dst = (("wz", wz_bf), ("mla_w_out", wo_bf))[i // 8]
        k = i % 8
        P.dma("pool", dst[:, k, :], wsrc[name][:, k, :], writes=[("pf", i)], sem=("pf", i % 2))
    ioB["prefetch"] = prefetch

    ag2_tok = {}

    def oTo_dma(P, t, rr, dst, key):
        def emit(e, sem):
            core = e.partition_id()
            for k in range(8):
                m = (k % 4) * 2 + t // 2
                row = m * 1024 + rr * 256 + (t % 2) * 128
                with e.If(core == k):
                    e.wait_ge(ag2_tok[m][0], ag2_tok[m][1])
                    e.dma_start(out=dst, in_=oA[row:row + 128, :]).then_inc(sem, 16)
        P.custom_dma("sp", emit, writes=[key], sem=key, self_inc=True)

    wsrc = {"wz": ein("wz", [128, 8, 2048], F32), "mla_w_out": ein("mla_w_out", [128, 8, 2048], F32)}
    ioC = {
        "x1T": x1s, "oTo": oA, "oTo_dma": oTo_dma, "vec": vec, "wz": wsrc["wz"],
        "mla_w_out": wsrc["mla_w_out"], "wz_bf": wz_bf, "wo_bf": wo_bf,
        "outT": _dram(nc, "outT", [128, 8, 8, 256], F32, "ExternalOutput"),
    }
    with ExitStack() as sem_es:
        with ExitStack() as es:
            PA = Prog(nc, es, sem_es)
            phase_A(nc, PA, ioA)
            PA.barrier(skip="ag1")
            PA.emit()
        with ExitStack() as es:
            PB = Prog(nc, es, sem_es)
            for k in range(4):
                PB.extern(PA.sems[("dma", ("ag1", k))], PA.dma_cnt[("ag1", k)], [("latA", k)])
            phase_B(nc, PB, ioB)
            PB.barrier(skip="ag2")
            PB.emit()
        with ExitStack() as es:
            PC = Prog(nc, es, sem_es)
            for m in range(8):
                ag2_tok[m] = (PB.sems[("dma", ("ag2", m))], PB.dma_cnt[("ag2", m)])
                PC.extern(ag2_tok[m][0], ag2_tok[m][1], [("oTo", m)])
            phase_C(nc, PC, ioC)
            PC.barrier()
            PC.emit()
    return nc


def _cols(v):
    return np.ascontiguousarray(np.asarray(v, np.float32).reshape(-1, 128).T)


def _pm(w):
    nk = w.shape[0] // 128
    return np.ascontiguousarray(w.reshape(nk, 128, w.shape[1]).transpose(1, 0, 2))


def host_inputs(inp):
    f = lambda k: np.asarray(inp[k], np.float32)
    x = f("x")
    vec = np.concatenate([_cols(f("pool_norm")[0]), _cols(f("mla_norm")[0]), _cols(f("pool_scale")[0]),
                          _cols(f("mla_q_norm")[0]), _cols(f("mla_kv_norm")[0]), _cols(f("final_norm"))], axis=1)
    assert vec.shape == (128, NV)
    perm = np.concatenate([np.concatenate([np.arange(g * 512, (g + 1) * 512), 2048 + np.arange(g * 512, (g + 1) * 512)])
                           for g in range(4)])
    w_in_p = _pm(f("pool_w_in")[0][:, perm]).reshape(128, 8, 4, 1024).transpose(0, 2, 1, 3)
    w_in_p = np.ascontiguousarray(w_in_p)
    wg_p = np.ascontiguousarray(f("pool_w_group")[0].reshape(4, 4, 128, 512).transpose(2, 0, 1, 3))
    wo0_p = _pm(f("pool_w_out")[0]).reshape(128, 8, 2048)
    wo1_p = _pm(f("mla_w_out")[0]).reshape(128, 8, 2048)
    mw = f("mla_w_in")[0]
    wl_p = _pm(np.concatenate([mw[:, :704], mw[:, 672:704], mw[:, 640:672], mw[:, 640:704]], axis=1))
    wz_p = _pm(mw[:, 704:])
    wqb, wkvb = f("mla_w_q_b")[0], f("mla_w_kv_b")[0]
    invf = (np.float32(1.0) / (np.float32(10000.0) ** (np.arange(0, 64, 2, dtype=np.float32) / np.float32(64)))).astype(np.float32)
    rcol = np.stack([np.tile(invf, 4), np.tile(np.concatenate([-np.ones(32, np.float32), np.ones(32, np.float32)]), 2)], axis=1)
    tri = (np.arange(128)[None, :] >= np.arange(128)[:, None]).astype(np.float32)
    A, Bm, C = [], [], []
    for c in range(8):
        b, r = c // 4, c % 4
        t0 = r * TOK
        xT = np.ascontiguousarray(x[b, t0:t0 + TOK].reshape(8, 256, 8, 128).transpose(3, 0, 2, 1))
        xh = np.zeros((HALO, D), np.float32)
        if t0 > 0:
            xh[:] = x[b, t0 - HALO:t0]
        xh = np.ascontiguousarray(xh.reshape(HALO, 8, 128).transpose(2, 1, 0))
        rc = np.zeros((128, 64), np.float32)
        for g, w in enumerate((2, 4, 8, 16)):
            rc[:, g * 16:(g + 1) * 16] = 1.0 / np.minimum(t0 + np.arange(16) + 1, w).astype(np.float32)
        A.append({"xT": xT, "xh": xh, "vec": vec, "rc": rc, "pool_w_in": w_in_p,
                  "pool_w_group": wg_p, "pool_w_out": wo0_p, "wl": wl_p})
        hs = [4 * r + hl for hl in range(4)]
        wq = np.concatenate([np.concatenate([wqb[:, h * 192:h * 192 + 192], wqb[:, h * 192 + 160:h * 192 + 192],
                                             wqb[:, h * 192 + 128:h * 192 + 160], wqb[:, h * 192 + 128:h * 192 + 192]], axis=1)
                             for h in hs], axis=1)
        wk = np.concatenate([wkvb[:, h * 256:h * 256 + 128] for h in hs], axis=1)
        wv = np.concatenate([wkvb[:, h * 256 + 128:h * 256 + 256] for h in hs], axis=1)
        posr = np.ascontiguousarray(np.broadcast_to(np.asarray(inp["positions"])[b].astype(np.int32)[None, :], (128, S)))
        Bm.append({"posr": posr, "rcol": rcol, "tri": tri, "wq": _pm(wq), "wk": _pm(wk), "wv": _pm(wv)})
        C.append({"vec": vec, "wz": wz_p, "mla_w_out": wo1_p})
    return A, Bm, C


def _assemble(res):
    out = np.empty((NB, S, D), np.float32)
    for c in range(8):
        b, r = c // 4, c % 4
        o = res[c]["outT"]
        out[b, r * TOK:(r + 1) * TOK, :] = o.transpose(1, 3, 2, 0).reshape(TOK, D)
    return out


_NC = {}


def _get(name, fn):
    if name not in _NC:
        _NC[name] = fn()
    return _NC[name]


FUSED = True


def kernel(**inputs):
    A, Bm, C = host_inputs(inputs)
    cores = list(range(8))
    if FUSED:
        maps = []
        for c in cores:
            m = {}
            m.update(A[c]); m.update(Bm[c]); m.update(C[c])
            maps.append(m)
        res = run_bass_kernel_spmd(_get("F", build_fused), maps, core_ids=cores).results
        return _assemble(res)
    ra = run_bass_kernel_spmd(_get("A", build_A), A, core_ids=cores).results
    for c in cores:
        b = c // 4
        Bm[c]["latA"] = np.concatenate([ra[b * 4 + r]["lat"][k * 256:(k + 1) * 256] for k in range(4) for r in range(4)], axis=0)
    rb = run_bass_kernel_spmd(_get("B", build_B), Bm, core_ids=cores).results
    for c in cores:
        b, r = c // 4, c % 4
        C[c]["x1T"] = ra[c]["x1T"]
        C[c]["oTo"] = np.ascontiguousarray(np.concatenate(
            [rb[b * 4 + rr]["oT"][(r * 4 + t) * 128:(r * 4 + t + 1) * 128, :] for t in range(4) for rr in range(4)], axis=0))
    rc = run_bass_kernel_spmd(_get("C", build_C), C, core_ids=cores).results
    return _assemble(rc)
```

```python
import numpy as np
from contextlib import ExitStack
import concourse.bass as bass
import concourse.mybir as mybir
from concourse.bass_utils import run_bass_kernel_spmd

F32 = mybir.dt.float32
BF16 = mybir.dt.bfloat16
I32 = mybir.dt.int32
AF = mybir.ActivationFunctionType
ALU = mybir.AluOpType


class _Op:
    __slots__ = ("eng", "emit", "deps", "is_dma", "semkey", "signal", "count", "waits", "final", "inc", "self_inc", "ext")

    def __init__(self):
        self.inc, self.self_inc, self.ext = 16, False, None


class Prog:
    COMPUTE = ("pe", "act", "dve", "pool")
    NSEM = 0
    NPROG = 0

    def __init__(self, nc, es, sem_es=None):
        self.nc, self.es = nc, es
        self.sem_es = sem_es if sem_es is not None else es
        Prog.NPROG += 1
        self.uid = Prog.NPROG
        self.ops = []
        self.last_writer = {}
        self.readers = {}
        self.last_dma_on_sem = {}
        self.nsb = 0

    def sbuf(self, name, shape, dtype):
        return self.es.enter_context(self.nc.sbuf_tensor("sb%d_" % self.uid + name, list(shape), dtype))

    def psum(self, name, shape, dtype):
        return self.es.enter_context(self.nc.psum_tensor("pp%d_" % self.uid + name, list(shape), dtype))

    def _record(self, op, reads, writes):
        idx = len(self.ops)
        deps = {}

        def add(i, kind):
            if i is not None:
                deps[i] = deps.get(i, 0) | kind

        for k in reads:
            add(self.last_writer.get(k), 1)
        for k in writes:
            add(self.last_writer.get(k), 2)
            for i in self.readers.get(k, ()):
                add(i, 4)
        deps.pop(idx, None)
        op.deps = deps
        self.ops.append(op)
        for k in writes:
            self.last_writer[k] = idx
            self.readers[k] = []
        for k in reads:
            lst = self.readers.setdefault(k, [])
            if not op.is_dma:
                lst[:] = [i for i in lst if self.ops[i].is_dma or self.ops[i].eng != op.eng]
            lst.append(idx)
        return idx

    def op(self, eng, emit, reads=(), writes=()):
        o = _Op()
        o.eng, o.emit, o.is_dma, o.semkey, o.signal, o.count, o.final = eng, emit, False, None, False, 0, False
        return self._record(o, list(reads), list(writes))

    def dma(self, queue, out, in_, reads=(), writes=(), sem=None, final=False, **kw):
        o = _Op()
        o.eng, o.is_dma, o.semkey, o.signal, o.count, o.final = queue, True, sem, True, 0, final
        o.emit = lambda e: e.dma_start(out=out, in_=in_, **kw)
        idx = self._record(o, list(reads), list(writes))
        prev = self.last_dma_on_sem.get(sem)
        if prev is not None:
            o.deps[prev] = o.deps.get(prev, 0) | 2
        self.last_dma_on_sem[sem] = idx
        if queue == "pool":
            self.pool_dmas = getattr(self, "pool_dmas", [])
            if len(self.pool_dmas) >= 2:
                o.deps[self.pool_dmas[-2]] = o.deps.get(self.pool_dmas[-2], 0) | 2
            self.pool_dmas.append(idx)
        return idx

    def custom_dma(self, queue, emit, reads=(), writes=(), sem=None, final=False, inc=16, self_inc=False):
        o = _Op()
        o.eng, o.is_dma, o.semkey, o.signal, o.count, o.final = queue, True, sem, True, 0, final
        o.inc, o.self_inc = inc, self_inc
        o.emit = emit
        idx = self._record(o, list(reads), list(writes))
        prev = self.last_dma_on_sem.get(sem)
        if prev is not None:
            o.deps[prev] = o.deps.get(prev, 0) | 2
        self.last_dma_on_sem[sem] = idx
        return idx

    def extern(self, sem, count, writes):
        o = _Op()
        o.eng, o.is_dma, o.semkey, o.signal, o.count, o.final = "sp", True, ("ext", len(self.ops)), True, count, False
        o.ext, o.emit = sem, None
        return self._record(o, [], list(writes))

    def barrier(self, skip=None):
        last = {}
        for i, o in enumerate(self.ops):
            if o.is_dma and skip is not None and isinstance(o.semkey, tuple) and o.semkey[0] == skip:
                continue
            last[("dma", o.semkey) if o.is_dma else ("eng", o.eng)] = i
        for eng in ("pe", "act", "dve", "pool", "sp"):
            o = _Op()
            o.eng, o.is_dma, o.semkey, o.signal, o.count, o.final = eng, False, None, False, 0, False
            o.emit = lambda e: e.nop()
            o.deps = {i: 1 for k, i in last.items() if k != ("eng", eng)}
            self.ops.append(o)

    def emit(self):
        nc, es, ops = self.nc, self.es, self.ops
        for o in ops:
            keep = []
            for i, kind in o.deps.items():
                p = ops[i]
                if not p.is_dma and not o.is_dma and p.eng == o.eng:
                    if o.eng == "pe":
                        continue
                if not p.is_dma and o.is_dma and p.eng == o.eng and not (kind & 3):
                    pass
                keep.append(i)
                p.signal = True
            o.deps = keep
        eng_cnt = {e: 0 for e in ("pe", "act", "dve", "pool", "sp")}
        dma_cnt = {}
        for o in ops:
            if o.ext is not None:
                continue
            if o.is_dma:
                dma_cnt[o.semkey] = dma_cnt.get(o.semkey, 0) + o.inc
                o.count = dma_cnt[o.semkey]
            elif o.signal:
                eng_cnt[o.eng] += 1
                o.count = eng_cnt[o.eng]
        sems = {}
        for e in self.COMPUTE:
            if eng_cnt[e]:
                Prog.NSEM += 1
                sems[("eng", e)] = self.sem_es.enter_context(nc.semaphore("sem%d" % Prog.NSEM))
        for k in dma_cnt:
            Prog.NSEM += 1
            sems[("dma", k)] = self.sem_es.enter_context(nc.semaphore("sem%d" % Prog.NSEM))
        for o in ops:
            if o.ext is not None:
                sems[("dma", o.semkey)] = o.ext
        self.sems, self.dma_cnt = sems, dma_cnt

        def tok(p):
            return (sems[("dma", p.semkey)] if p.is_dma else sems[("eng", p.eng)], p.count,
                    ("dma", p.semkey) if p.is_dma else ("eng", p.eng))

        waited = {e: {} for e in eng_cnt}
        for o in ops:
            w = {}
            for i in o.deps:
                s, c, k = tok(ops[i])
                if c > waited[o.eng].get(k, 0) and c > w.get(k, (None, 0))[1]:
                    w[k] = (s, c)
            for k, (s, c) in w.items():
                waited[o.eng][k] = c
            o.waits = list(w.values())
        finals = [(sems[("dma", k)], dma_cnt[k]) for k in dma_cnt
                  if any(o.final and o.semkey == k for o in ops if o.is_dma)]
        by_eng = {e: [o for o in ops if o.eng == e] for e in eng_cnt}

        def run(ename, e):
            for o in by_eng[ename]:
                if o.ext is not None:
                    continue
                for s, c in o.waits:
                    e.wait_ge(s, c)
                if o.self_inc:
                    o.emit(e, sems[("dma", o.semkey)])
                    continue
                inst = o.emit(e)
                if inst is None:
                    continue
                if o.is_dma:
                    inst.then_inc(sems[("dma", o.semkey)], o.inc)
                elif o.signal:
                    inst.then_inc(sems[("eng", o.eng)], 1)
            if ename == "sp":
                for s, c in finals:
                    e.wait_ge(s, c)

        with nc.Block() as block:
            @block.tensor
            def _(e):
                run("pe", e)

            @block.scalar
            def _(e):
                run("act", e)

            @block.vector
            def _(e):
                run("dve", e)

            @block.gpsimd
            def _(e):
                run("pool", e)

            @block.sync
            def _(e):
                run("sp", e)


D = 1024
S = 8192
NB = 2
TOK = 2048
HALO = 16
NH = 16
HPC = 4
EPS = 1e-6
LATS = 7
LATR = 832
SCALE = 192 ** -0.5

V_G0, V_G1, V_SC, V_GQ, V_GKV, V_GF, NV = 0, 8, 16, 32, 35, 37, 45


class Banks:
    def __init__(self, P, n=8, prefix="ps"):
        self.t = [P.psum("%s%d" % (prefix, i), [128, 512], F32) for i in range(n)]
        self.keys = [(prefix, i) for i in range(n)]
        self.i = 0

    def next(self):
        i = self.i
        self.i = (self.i + 1) % len(self.t)
        return self.t[i], self.keys[i]


def rms_inv(P, banks, ones, src_sq, nk, W, nfeat, rt, inv, keys_sq, key_rt, key_inv):
    ps, pk = banks.next()

    def mm(e):
        for k in range(nk):
            r = e.matmul(ps[:, :W], lhsT=ones[:], rhs=src_sq[:, k, :W], start=(k == 0), stop=(k == nk - 1))
        return r
    P.op("pe", mm, reads=list(keys_sq) + ["ones"], writes=[pk])
    P.op("act", lambda e: e.activation(out=rt[:, :W], in_=ps[:, :W], func=AF.Ln, bias=EPS, scale=1.0 / nfeat),
         reads=[pk], writes=[key_rt])
    P.op("act", lambda e: e.activation(out=inv[:, :W], in_=rt[:, :W], func=AF.Exp, scale=-0.5), reads=[key_rt], writes=[key_inv])


def phase_A(nc, P, io):
    TT, NT = 256, TOK // 256
    xT, x1T, lat = io["xT"], io["x1T"], io["lat"]
    banks = Banks(P)
    ones = P.sbuf("ones", [128, 128], BF16)
    P.op("pool", lambda e: e.memset(ones[:], 1.0), writes=["ones"])
    vec = P.sbuf("vec", [128, NV], F32)
    P.dma("sp", vec[:], io["vec"], writes=["vec"], sem="vec")
    rc = P.sbuf("rc", [128, 64], F32)
    P.dma("sp", rc[:], io["rc"], writes=["rc"], sem="rc")

    w_in = P.sbuf("w_in", [128, 8, 4096], BF16)
    wg = P.sbuf("wg", [128, 4, 4, 512], BF16)
    wo = P.sbuf("wo", [128, 16, 1024], BF16)
    wl = P.sbuf("wl", [128, 8, LATR], BF16)
    for g in range(4):
        P.dma("pool", w_in[:, :, g * 1024:(g + 1) * 1024], io["pool_w_in"][:, g, :, :],
              writes=[("w_in", g)], sem=("w_in", g))
        P.dma("pool", wg[:, g, :, :], io["pool_w_group"][:, g, :, :], writes=[("wg", g)], sem=("wg", g))
    for h2 in range(2):
        P.dma("pool", wo[:, h2 * 8:(h2 + 1) * 8, :].rearrange("p (a b) f -> p a (b f)", b=2),
              io["pool_w_out"][:, h2 * 4:(h2 + 1) * 4, :], writes=[("wo", h2)], sem=("wo", h2))
    P.dma("pool", wl[:], io["wl"], writes=["wl"], sem="wl")

    xt = [P.sbuf("xt%d" % i, [128, 8, TT], F32) for i in range(2)]
    sq = P.sbuf("sq", [128, 8, TT], BF16)
    rt = P.sbuf("rt", [128, TT], F32)
    inv = P.sbuf("inv", [128, TT], F32)
    hb = [P.sbuf("h%d" % i, [128, 8, TT], BF16) for i in range(3)]
    NU = 3
    u = [P.sbuf("u%d" % i, [128, HALO + TT], F32) for i in range(NU)]
    pa = P.sbuf("pa", [128, HALO + TT], F32)
    pb = P.sbuf("pb", [128, HALO + TT], F32)
    tmp16 = P.sbuf("tmp16", [128, 16], F32)
    pooled = [P.sbuf("pooled%d" % i, [128, 4, TT], BF16) for i in range(2)]
    sz = [P.sbuf("sz%d" % i, [128, 4, TT], BF16) for i in range(2)]
    yb = [P.sbuf("y%d" % i, [128, 16, TT], BF16) for i in range(2)]
    uh = [P.sbuf("uh%d" % i, [128, 16, HALO], F32) for i in range(2)]
    ql = P.sbuf("ql", [128, 5, TT], F32)
    sql = P.sbuf("sql", [128, 5, TT], BF16)
    lat_o = [P.sbuf("lat_o%d" % i, [128, LATS, TT], BF16) for i in range(2)]
    for i in range(2):
        P.op("pool", lambda e, i=i: e.memset(lat_o[i][:], 0.0), writes=[("lat_o%d" % i, j) for j in range(LATS)])
    ucount = [0]

    def norm_h(xs, xkey, W, gcol0, hi):
        h = hb[hi]
        P.op("act", lambda e: e.activation(out=sq[:, :, :W], in_=xs[:, :, :W], func=AF.Square),
             reads=[xkey], writes=["sq"])
        rms_inv(P, banks, ones, sq, 8, W, D, rt, inv, ["sq"], "rt", "inv")
        for kc in range(8):
            P.op("dve", lambda e, kc=kc: e.scalar_tensor_tensor(
                out=h[:, kc, :W], in0=xs[:, kc, :W], scalar=vec[:, gcol0 + kc:gcol0 + kc + 1],
                in1=inv[:, :W], op0=ALU.mult, op1=ALU.mult),
                reads=[xkey, "inv", "vec"], writes=[("h", hi, kc)])

    def unit8(wkeys, lhs_of, W, hi):
        ps, pk = banks.next()
        h = hb[hi]

        def mm(e):
            for kc in range(8):
                r = e.matmul(ps[:, :W], lhsT=lhs_of(kc), rhs=h[:, kc, :W], start=(kc == 0), stop=(kc == 7))
            return r
        P.op("pe", mm, reads=[("h", hi, kc) for kc in range(8)] + list(wkeys), writes=[pk])
        return ps, pk

    xh = xt[1]
    P.dma("sp", xh[:, :, :HALO], io["xh"], writes=["xt1"], sem="xt1")
    norm_h(xh, "xt1", HALO, V_G0, 2)

    def halo(g):
        for f in range(4 * g, 4 * g + 4):
            ps, pk = unit8([("w_in", g)], lambda kc, f=f, g=g: w_in[:, kc, g * 1024 + (f % 4) * 128:g * 1024 + (f % 4 + 1) * 128], HALO, 2)
            P.op("act", lambda e, ps=ps, f=f: e.activation(out=uh[0][:, f, :], in_=ps[:, :HALO], func=AF.Copy),
                 reads=[pk], writes=[("uh", 0, f)])

    def F0(t):
        xs, xkey = xt[t % 2], "xt%d" % (t % 2)
        P.dma("sp", xs[:], xT[:, t, :, :], writes=[xkey], sem=xkey)
        norm_h(xs, xkey, TT, V_G0, t % 2)

    def F1(t, g):
        uhc, uhn = uh[t % 2], uh[(t + 1) % 2]
        slot = (t * 4 + g) % 2
        w = 2 << g
        hi = t % 2
        for fc in range(4):
            f = g * 4 + fc
            us = u[ucount[0] % NU]
            ukey = "u%d" % (ucount[0] % NU)
            ucount[0] += 1
            ps, pk = unit8([("w_in", g)], lambda kc, fc=fc, g=g: w_in[:, kc, g * 1024 + fc * 128:g * 1024 + (fc + 1) * 128], TT, hi)
            P.op("act", lambda e, ps=ps, us=us: e.activation(out=us[:, HALO:], in_=ps[:, :TT], func=AF.Copy),
                 reads=[pk], writes=[ukey])
            P.op("act", lambda e, ps=ps, f=f, uhn=uhn: e.activation(out=uhn[:, f, :], in_=ps[:, TT - HALO:TT], func=AF.Copy),
                 reads=[pk], writes=[("uh", (t + 1) % 2, f)])
            P.op("act", lambda e, us=us, f=f, uhc=uhc: e.activation(out=us[:, :HALO], in_=uhc[:, f, :], func=AF.Copy),
                 reads=[("uh", t % 2, f)], writes=[ukey])
            ps, pk = unit8([("w_in", g)], lambda kc, fc=fc, g=g: w_in[:, kc, g * 1024 + 512 + fc * 128:g * 1024 + 512 + (fc + 1) * 128], TT, hi)
            P.op("act", lambda e, ps=ps, slot=slot, fc=fc: e.activation(out=sz[slot][:, fc, :], in_=ps[:, :TT], func=AF.Silu),
                 reads=[pk], writes=[("sz", slot, fc)])
            E = HALO + TT
            src, skey = us, ukey
            bufs = [(pa, "pa"), (pb, "pb")]
            step, lo, bi = 1, 1, 0
            while step < w:
                dst, dkey = bufs[bi]
                P.op("dve", lambda e, dst=dst, src=src, step=step, lo=lo: e.tensor_tensor(
                    out=dst[:, lo:E], in0=src[:, lo:E], in1=src[:, lo - step:E - step], op=ALU.add),
                    reads=[skey], writes=[dkey])
                src, skey = dst, dkey
                step *= 2
                lo = 2 * step - 1
                bi ^= 1
            P.op("dve", lambda e, src=src, us=us, slot=slot, fc=fc, w=w: e.scalar_tensor_tensor(
                out=pooled[slot][:, fc, :], in0=src[:, HALO:E], scalar=1.0 / w, in1=us[:, HALO:E],
                op0=ALU.mult, op1=ALU.subtract),
                reads=[skey, ukey], writes=[("pooled", slot, fc)])
            if t == 0:
                P.op("dve", lambda e, src=src, g=g: e.tensor_tensor(
                    out=tmp16[:], in0=src[:, HALO:HALO + 16], in1=rc[:, g * 16:(g + 1) * 16], op=ALU.mult),
                    reads=[skey, "rc"], writes=["tmp16"])
                P.op("dve", lambda e, us=us, slot=slot, fc=fc: e.tensor_tensor(
                    out=pooled[slot][:, fc, 0:16], in0=tmp16[:], in1=us[:, HALO:HALO + 16], op=ALU.subtract),
                    reads=["tmp16", ukey], writes=[("pooled", slot, fc)])

    def F2(t, g):
        slot = (t * 4 + g) % 2
        y, yk = yb[t % 2], t % 2
        for fo in range(4):
            f = g * 4 + fo
            ps, pk = banks.next()

            def mm(e, ps=ps, g=g, fo=fo, slot=slot):
                for kc in range(4):
                    r = e.matmul(ps[:, :TT], lhsT=wg[:, g, kc, fo * 128:(fo + 1) * 128],
                                 rhs=pooled[slot][:, kc, :], start=(kc == 0), stop=(kc == 3))
                return r
            P.op("pe", mm, reads=[("pooled", slot, kc) for kc in range(4)] + [("wg", g)], writes=[pk])
            P.op("dve", lambda e, ps=ps, f=f, fo=fo, slot=slot, y=y: e.scalar_tensor_tensor(
                out=y[:, f, :], in0=ps[:, :TT], scalar=vec[:, V_SC + f:V_SC + f + 1], in1=sz[slot][:, fo, :],
                op0=ALU.mult, op1=ALU.mult),
                reads=[pk, ("sz", slot, fo), "vec"], writes=[("y", yk, f)])

    def F3(t):
        xs, xkey = xt[t % 2], "xt%d" % (t % 2)
        y, yk = yb[t % 2], t % 2
        for dc in range(8):
            ps, pk = banks.next()

            def mm(e, ps=ps, dc=dc, y=y):
                for kc in range(16):
                    r = e.matmul(ps[:, :TT], lhsT=wo[:, kc, dc * 128:(dc + 1) * 128], rhs=y[:, kc, :],
                                 start=(kc == 0), stop=(kc == 15))
                return r
            P.op("pe", mm, reads=[("y", yk, kc) for kc in range(16)] + [("wo", 0), ("wo", 1)], writes=[pk])
            P.op("dve", lambda e, ps=ps, dc=dc, xs=xs: e.tensor_tensor(
                out=xs[:, dc, :], in0=ps[:, :TT], in1=xs[:, dc, :], op=ALU.add),
                reads=[pk, xkey], writes=[xkey])

    def F4(t):
        xs, xkey = xt[t % 2], "xt%d" % (t % 2)
        P.dma("pool", x1T[:, t, :, :], xs[:], reads=[xkey], writes=[("x1T", t)],
              sem=("x1st", t % 2), final=True)
        norm_h(xs, xkey, TT, V_G1, 2)

    def F5(t):
        lo_t, lkey = lat_o[t % 2], "lat_o%d" % (t % 2)
        h = hb[2]
        for j in range(5):
            ps, pk = unit8(["wl"], lambda kc, j=j: wl[:, kc, j * 128:(j + 1) * 128], TT, 2)
            P.op("act", lambda e, ps=ps, j=j: e.activation(out=ql[:, j, :], in_=ps[:, :TT], func=AF.Copy),
                 reads=[pk], writes=[("ql", j)])
        for j in (5, 6):
            ps, pk = unit8(["wl"], lambda kc, j=j: wl[:, kc, 640 + (j - 5) * 64:768 + (j - 5) * 64], TT, 2)
            P.op("act", lambda e, ps=ps, lo_t=lo_t, j=j: e.activation(out=lo_t[0:64, j, :], in_=ps[0:64, :TT], func=AF.Copy),
                 reads=[pk], writes=[(lkey, j)])
        for (j0, nj, nfeat, gc) in ((0, 3, 384, V_GQ), (3, 2, 256, V_GKV)):
            P.op("act", lambda e, j0=j0, nj=nj: e.activation(out=sql[:, j0:j0 + nj, :], in_=ql[:, j0:j0 + nj, :], func=AF.Square),
                 reads=[("ql", j) for j in range(j0, j0 + nj)], writes=[("sql", j0)])
            ps, pk = banks.next()

            def mm(e, ps=ps, j0=j0, nj=nj):
                for k in range(nj):
                    r = e.matmul(ps[:, :TT], lhsT=ones[:], rhs=sql[:, j0 + k, :], start=(k == 0), stop=(k == nj - 1))
                return r
            P.op("pe", mm, reads=[("sql", j0), "ones"], writes=[pk])
            P.op("act", lambda e, ps=ps, nfeat=nfeat: e.activation(out=rt[:, :TT], in_=ps[:, :TT], func=AF.Ln, bias=EPS, scale=1.0 / nfeat),
                 reads=[pk], writes=["rt"])
            P.op("act", lambda e: e.activation(out=inv[:, :TT], in_=rt[:, :TT], func=AF.Exp, scale=-0.5), reads=["rt"], writes=["inv"])
            for k in range(nj):
                j = j0 + k
                P.op("dve", lambda e, j=j, k=k, gc=gc, lo_t=lo_t: e.scalar_tensor_tensor(
                    out=lo_t[:, j, :], in0=ql[:, j, :], scalar=vec[:, gc + k:gc + k + 1], in1=inv[:, :TT],
                    op0=ALU.mult, op1=ALU.mult),
                    reads=[("ql", j), "inv", "vec"], writes=[(lkey, j)])
        P.dma("pool", lat[t * 128:(t + 1) * 128, :], lo_t[:].rearrange("p j c -> p (j c)"), reads=[(lkey, j) for j in range(LATS)],
              writes=[("lat", t)], sem=("latst", t % 2), final=True)
        if t % 2 == 1 and io.get("ag1") is not None:
            io["ag1"](P, t // 2)

    F0(0)
    for t in range(NT):
        prev = t - 1
        if t == 0:
            halo(0)
        F1(t, 0)
        if prev >= 0:
            F2(prev, 3)
        if t == 0:
            halo(1)
        F1(t, 1)
        if prev >= 0:
            F3(prev)
        F2(t, 0)
        if t == 0:
            halo(2)
        F1(t, 2)
        if prev >= 0:
            F4(prev)
        F2(t, 1)
        if t == 0:
            halo(3)
        F1(t, 3)
        if t + 1 < NT:
            F0(t + 1)
        if prev >= 0:
            F5(prev)
        F2(t, 2)
    F2(NT - 1, 3)
    F3(NT - 1)
    F4(NT - 1)
    F5(NT - 1)


PI = float(np.pi)
C1 = 6.28125
C2 = float(2.0 * np.pi - 6.28125)


def phase_B(nc, P, io):
    QT = 512
    NQ = S // QT
    latA, oT = io["latA"], io["oT"]
    ones = P.sbuf("ones", [128, 128], BF16)
    P.op("pool", lambda e: e.memset(ones[:], 1.0), writes=["ones"])
    tri = P.sbuf("tri", [128, 128], BF16)
    P.dma("pool", tri[:], io["tri"], writes=["tri"], sem="tri")
    rcol = P.sbuf("rcol", [128, 2], F32)
    P.dma("sp", rcol[:], io["rcol"], writes=["rcol"], sem="rcol")
    wq = P.sbuf("wq", [128, 3, HPC * 320], BF16)
    wk = P.sbuf("wk", [128, 2, HPC * 128], BF16)
    wv = P.sbuf("wv", [128, 2, HPC * 128], BF16)
    P.dma("pool", wk[:], io["wk"], writes=["wk"], sem="wk")
    P.dma("pool", wv[:], io["wv"], writes=["wv"], sem="wv")
    P.dma("pool", wq[:], io["wq"], writes=["wq"], sem="wq")

    KT = P.sbuf("KT", [128, HPC, S], BF16)
    V = P.sbuf("V", [128, S // 128, HPC * 128], BF16)
    KR = P.sbuf("KR", [128, S], BF16)
    pst = [P.psum("ps%d" % i, [128, 512], F32) for i in range(8)]
    pskey = [("ps", i) for i in range(8)]
    latA_v = latA.rearrange("(k r t p) (j c) -> p k r t j c", k=4, r=4, t=2, p=128, j=LATS)

    SB = [0, 1, 2, 7]
    OB = [3, 4]
    DB = [5, 6]
    sbi = [0]

    def sbank():
        b = SB[sbi[0] % 4]
        sbi[0] += 1
        return pst[b], pskey[b]

    posi = P.sbuf("posi", [128, QT], I32)
    ang = P.sbuf("ang", [128, QT], F32)
    kf = P.sbuf("kf", [128, QT], F32)
    rr = P.sbuf("rr", [128, QT], F32)
    CS = [P.sbuf("CS%d" % i, [128, QT], F32) for i in range(2)]
    Ct = [c[0:64, :] for c in CS]
    S0 = P.sbuf("S0", [64, QT], F32)
    kk = [P.sbuf("kk%d" % i, [64, 2, 2, QT // 2], BF16) for i in range(2)]

    def v2(ap):
        return ap.rearrange("p (t c) -> p t c", t=2)
    kvn = [P.sbuf("kvn%d" % i, [128, 2, 2, QT // 2], BF16) for i in range(2)]
    qn = [P.sbuf("qn%d" % i, [128, 2, 3, QT // 2], BF16) for i in range(2)]
    t1f = P.sbuf("t1", [128, QT], F32)
    bi = [0]
    ev = [0]

    def evac_copy(dst, src, rkeys, wkeys):
        ev[0] += 1
        if ev[0] % 2:
            P.op("act", lambda e: e.activation(out=dst, in_=src, func=AF.Copy), reads=rkeys, writes=wkeys)
        else:
            P.op("dve", lambda e: e.tensor_copy(out=dst, in_=src), reads=rkeys, writes=wkeys)

    def prep_steps(tt):
        r_ = tt // 4
        sl = tt % 2
        g0 = tt * QT
        ck = ("latA", tt % 4)
        steps = []

        def dve(fn, reads, writes):
            steps.append(lambda: P.op("dve", fn, reads=reads, writes=writes))

        def loads():
            P.dma("sp", posi[:], io["posr"][:, g0:g0 + QT], writes=["posi"], sem="posi")
            P.dma("sp", kk[sl][:], latA_v[0:64, tt % 4, r_, :, 5:7, :], reads=[ck], writes=[("kk", sl)], sem=("kk", sl))
            P.dma("sp", kvn[sl][:], latA_v[:, tt % 4, r_, :, 3:5, :], reads=[ck], writes=[("kvn", sl)], sem=("kvn", sl))
            P.dma("sp", qn[sl][:], latA_v[:, tt % 4, r_, :, 0:3, :], reads=[ck], writes=[("qn", sl)], sem=("qn", sl))
        steps.append(loads)
        for hl in range(HPC):
            def kstep(hl=hl):
                ps, pk = sbank()

                def mm(e, ps=ps, hl=hl, sl=sl):
                    for kc in range(2):
                        r = e.matmul(ps[:], lhsT=wk[:, kc, hl * 128:(hl + 1) * 128], rhs=kvn[sl][:, :, kc, :],
                                     start=(kc == 0), stop=(kc == 1))
                    return r
                P.op("pe", mm, reads=[("kvn", sl), "wk"], writes=[pk])
                evac_copy(KT[:, hl, g0:g0 + QT], ps[:], [pk], [("KT", hl, tt)])
            steps.append(kstep)
        for sub in range(4):
            def vstep(sub=sub):
                ps, pk = sbank()

                def mm(e, ps=ps, sub=sub, sl=sl):
                    for kc in range(2):
                        r = e.matmul(ps[:], lhsT=kvn[sl][:, sub // 2, kc, (sub % 2) * 128:(sub % 2 + 1) * 128], rhs=wv[:, kc, :],
                                     start=(kc == 0), stop=(kc == 1))
                    return r
                P.op("pe", mm, reads=[("kvn", sl), "wv"], writes=[pk])
                evac_copy(V[:, tt * 4 + sub, :], ps[:], [pk], [("V", tt * 4 + sub)])
            steps.append(vstep)
        dve(lambda e: e.tensor_copy(out=ang[:], in_=posi[:]), ["posi"], ["ang"])
        dve(lambda e: e.tensor_scalar(out=ang[:], in0=ang[:], scalar1=rcol[:, 0:1], scalar2=None, op0=ALU.mult), ["ang", "rcol"], ["ang"])
        dve(lambda e: e.tensor_scalar(out=kf[:], in0=ang[:], scalar1=1.0 / (2 * PI), scalar2=0.5, op0=ALU.mult, op1=ALU.add), ["ang"], ["kf"])
        dve(lambda e: e.tensor_copy(out=posi[:], in_=kf[:]), ["kf"], ["posi"])
        dve(lambda e: e.tensor_copy(out=kf[:], in_=posi[:]), ["posi"], ["kf"])
        dve(lambda e: e.scalar_tensor_tensor(out=rr[:], in0=kf[:], scalar=-C1, in1=ang[:], op0=ALU.mult, op1=ALU.add), ["kf", "ang"], ["rr"])
        dve(lambda e: e.scalar_tensor_tensor(out=rr[:], in0=kf[:], scalar=-C2, in1=rr[:], op0=ALU.mult, op1=ALU.add), ["kf", "rr"], ["rr"])

        dve(lambda e: e.tensor_scalar(out=kf[:], in0=rr[:], scalar1=-PI, scalar2=2 * PI, op0=ALU.is_lt, op1=ALU.mult), ["rr"], ["kf"])
        dve(lambda e: e.tensor_tensor(out=rr[:], in0=rr[:], in1=kf[:], op=ALU.add), ["rr", "kf"], ["rr"])
        dve(lambda e: e.tensor_scalar(out=ang[:], in0=rr[:], scalar1=PI / 2, scalar2=None, op0=ALU.add), ["rr"], ["ang"])
        dve(lambda e: e.tensor_scalar(out=kf[:], in0=ang[:], scalar1=PI, scalar2=-2 * PI, op0=ALU.is_gt, op1=ALU.mult), ["ang"], ["kf"])
        dve(lambda e: e.tensor_tensor(out=ang[:], in0=ang[:], in1=kf[:], op=ALU.add), ["ang", "kf"], ["ang"])

        def sins():
            P.op("act", lambda e: e.activation(out=S0[:], in_=rr[0:64, :], func=AF.Sin), reads=["rr"], writes=["S0"])
            P.op("act", lambda e: e.activation(out=CS[sl][64:128, :], in_=rr[64:128, :], func=AF.Sin), reads=["rr"], writes=[("St", sl)])
            P.op("act", lambda e: e.activation(out=CS[sl][0:64, :], in_=ang[0:64, :], func=AF.Sin), reads=["ang"], writes=[("Ct", sl)])
        nA = len(steps)
        steps.append(sins)
        dve(lambda e: e.tensor_scalar(out=S0[:], in0=S0[:], scalar1=rcol[0:64, 1:2], scalar2=None, op0=ALU.mult), ["S0", "rcol"], ["S0"])
        dve(lambda e: e.tensor_scalar(out=CS[sl][64:128, :], in0=CS[sl][64:128, :], scalar1=rcol[64:128, 1:2], scalar2=None, op0=ALU.mult),
            [("St", sl), "rcol"], [("St", sl)])
        dve(lambda e: e.tensor_tensor(out=v2(kf[0:64, :]), in0=kk[sl][:, :, 0, :], in1=v2(Ct[sl]), op=ALU.mult), [("kk", sl), ("Ct", sl)], ["kf"])
        dve(lambda e: e.tensor_tensor(out=v2(rr[0:64, :]), in0=kk[sl][:, :, 1, :], in1=v2(S0[:]), op=ALU.mult), [("kk", sl), "S0"], ["rr"])
        dve(lambda e: e.tensor_tensor(out=KR[0:64, g0:g0 + QT], in0=kf[0:64, :], in1=rr[0:64, :], op=ALU.add), ["kf", "rr"], [("KR", tt)])
        steps.append(lambda: P.dma("sp", KR[64:128, g0:g0 + QT], KR[0:64, g0:g0 + QT], reads=[("KR", tt)], writes=[("KR2", tt)], sem=("krd", sl)))
        return steps[:nA], steps[nA:nA + 1], steps[nA + 1:]

    pendA, pendB, pendC = [], [], []

    def drain(lst, n):
        for _ in range(min(n, len(lst))):
            lst.pop(0)()

    Qn = [P.sbuf("Qn%d" % i, [128, QT], BF16) for i in range(2)]
    Qr = [P.sbuf("Qr%d" % i, [128, QT], BF16) for i in range(2)]
    NP = 5
    pT = [P.sbuf("pT%d" % i, [128, QT], BF16) for i in range(NP)]
    rden = t1f
    ot1 = P.sbuf("ot", [128, HPC, QT], BF16)
    ot = [ot1, ot1]
    psm = [P.sbuf("psm%d" % i, [128, QT], BF16) for i in range(4)]
    psi = [0]
    pti = [0]
    WQH = 320

    def qproj(it):
        qi, hl = it // HPC, it % HPC
        qs, s2 = qi % 2, it % 2
        ps, pk = sbank()

        def mm(e, ps=ps, hl=hl, qs=qs):
            for kc in range(3):
                r = e.matmul(ps[:], lhsT=wq[:, kc, hl * WQH:hl * WQH + 128], rhs=qn[qs][:, :, kc, :],
                             start=(kc == 0), stop=(kc == 2))
            return r
        P.op("pe", mm, reads=[("qn", qs), "wq"], writes=[pk])
        P.op("act", lambda e, ps=ps, s2=s2: e.activation(out=Qn[s2][:], in_=ps[:], func=AF.Copy),
             reads=[pk], writes=[("Qn", s2)])
        ps, pk = sbank()
        off = hl * WQH + 128

        def mm(e, ps=ps, off=off, qs=qs):
            for kc in range(3):
                r = e.matmul(ps[:], lhsT=wq[:, kc, off:off + 128], rhs=qn[qs][:, :, kc, :],
                             start=(kc == 0), stop=(kc == 2))
            return r
        P.op("pe", mm, reads=[("qn", qs), "wq"], writes=[pk])
        P.op("dve", lambda e, ps=ps, qs=qs, s2=s2: e.tensor_tensor(out=Qr[s2][:], in0=ps[:], in1=CS[qs][:], op=ALU.mult),
             reads=[pk, ("Ct", qs), ("St", qs)], writes=[("Qr", s2)])

    def attn(it):
        qi, hl = it // HPC, it % HPC
        qs, s2 = qi % 2, it % 2
        nk = 4 * (qi + 1)
        ob, db = OB[s2], DB[s2]
        LA = 3
        stiles = {}
        den_at = {}

        def emit_S(kj):
            m = kj - 4 * qi
            lo = 128 * m if m > 0 else 0
            ps, pk = sbank()

            def mm(e, ps=ps, kj=kj, lo=lo, hl=hl, s2=s2):
                e.matmul(ps[:, lo:QT], lhsT=KT[:, hl, kj * 128:(kj + 1) * 128], rhs=Qn[s2][:, lo:QT],
                         start=True, stop=False)
                return e.matmul(ps[:, lo:QT], lhsT=KR[:, kj * 128:(kj + 1) * 128], rhs=Qr[s2][:, lo:QT],
                                start=False, stop=True)
            P.op("pe", mm, reads=[("KT", hl, kj // 4), ("KR", kj // 4), ("KR2", kj // 4), ("Qn", s2), ("Qr", s2)], writes=[pk])
            pi_ = pti[0] % NP
            pti[0] += 1
            P.op("act", lambda e, ps=ps, lo=lo, pi_=pi_: e.activation(out=pT[pi_][:, lo:QT], in_=ps[:, lo:QT], func=AF.Exp, scale=SCALE),
                 reads=[pk], writes=[("pT", pi_)])
            if m >= 0:
                P.op("dve", lambda e, lo=lo, pi_=pi_: e.tensor_tensor(out=pT[pi_][:, lo:lo + 128], in0=pT[pi_][:, lo:lo + 128], in1=tri[:], op=ALU.mult),
                     reads=[("pT", pi_), "tri"], writes=[("pT", pi_)])
            sm = None
            if m < 0 and kj % 2 == 1:
                nfull = 4 * qi
                sp_ = (kj // 2) % 4
                pj = stiles[kj - 1][0]
                P.op("dve", lambda e, sp_=sp_, pj=pj, pi_=pi_: e.tensor_tensor(out=psm[sp_][:], in0=pT[pj][:], in1=pT[pi_][:], op=ALU.add),
                     reads=[("pT", pj), ("pT", pi_)], writes=[("psm", sp_)])
                if kj % 4 == 3:
                    P.op("dve", lambda e, sp_=sp_: e.tensor_tensor(out=psm[sp_][:], in0=psm[sp_ - 1][:], in1=psm[sp_][:], op=ALU.add),
                         reads=[("psm", sp_ - 1), ("psm", sp_)], writes=[("psm", sp_)])
                    if kj % 8 == 7:
                        P.op("dve", lambda e: e.tensor_tensor(out=psm[3][:], in0=psm[1][:], in1=psm[3][:], op=ALU.add),
                             reads=[("psm", 1), ("psm", 3)], writes=[("psm", 3)])
                        den_at[min(kj + 2, nfull - 1)] = (3, kj == 7)
                    elif kj == nfull - 1:
                        den_at[nfull - 1] = (1, kj == 3)
            stiles[kj] = (pi_, lo, sm)

        def emit_PV(kj):
            pi_, lo, sm = stiles.pop(kj)
            m = kj - 4 * qi
            sm, first = den_at.pop(kj, (None, False))

            def mm(e, kj=kj, lo=lo, pi_=pi_, hl=hl, ob=ob, db=db, nk=nk, sm=sm, m=m, first=first):
                r = e.matmul(pst[ob][:, lo:QT], lhsT=V[:, kj, hl * 128:(hl + 1) * 128], rhs=pT[pi_][:, lo:QT],
                             start=(kj == 0), stop=(kj == nk - 1))
                if sm is not None:
                    r = e.matmul(pst[db][:], lhsT=ones[:], rhs=psm[sm][:], start=first, stop=False)
                if m >= 0:
                    r = e.matmul(pst[db][:, lo:QT], lhsT=ones[:], rhs=pT[pi_][:, lo:QT],
                                 start=(kj == 0), stop=(kj == nk - 1))
                return r
            rd = [("pT", pi_), ("V", kj), "ones"] + ([("psm", sm)] if sm is not None else [])
            P.op("pe", mm, reads=rd, writes=[pskey[ob], pskey[db]])

        if hl == 1:
            drain(pendA, len(pendA))
        if hl == 2:
            drain(pendB, len(pendB))
        if hl == 3:
            drain(pendC, len(pendC))
        perA = -(-len(pendA) // nk) if hl == 0 else 0
        perC = -(-len(pendC) // nk) if hl == 2 else 0
        for kj in range(nk + LA):
            if kj < nk:
                emit_S(kj)
                drain(pendA, perA)
                drain(pendC, perC)
            if kj - LA >= 0:
                emit_PV(kj - LA)
            if kj == 1 and it + 1 < NQ * HPC:
                qproj(it + 1)
        P.op("act", lambda e, db=db: e.activation(out=rden[:], in_=pst[db][:], func=AF.Ln), reads=[pskey[db]], writes=["t1"])
        P.op("act", lambda e: e.activation(out=rden[:], in_=rden[:], func=AF.Exp, scale=-1.0), reads=["t1"], writes=["t1"])
        P.op("dve", lambda e, ob=ob, qs=qs, hl=hl: e.tensor_tensor(out=ot[qs][:, hl, :], in0=pst[ob][:], in1=rden[:], op=ALU.mult),
             reads=[pskey[ob], "t1"], writes=[("ot", hl)])

    for lst in prep_steps(0):
        for st in lst:
            st()
    qproj(0)
    for qi in range(NQ):
        if qi + 1 < NQ:
            a_, b_, c_ = prep_steps(qi + 1)
            pendA.extend(a_), pendB.extend(b_), pendC.extend(c_)
        for hl in range(HPC):
            attn(qi * HPC + hl)
        qs = qi % 2
        P.dma("pool", oT[qi * 128:(qi + 1) * 128, :], ot[qs][:].rearrange("p h c -> p (h c)"),
              reads=[("ot", hl) for hl in range(HPC)], writes=[("oT", qi)], sem="ost", final=True)
        if qi % 2 == 1 and io.get("ag2") is not None:
            io["ag2"](P, qi // 2)
        if io.get("prefetch") is not None:
            io["prefetch"](P, qi)


def phase_C(nc, P, io):
    TT, NT = 512, TOK // 512
    x1T, oTo, outT = io["x1T"], io["oTo"], io["outT"]
    banks = Banks(P)
    ones = P.sbuf("ones", [128, 128], BF16)
    P.op("pool", lambda e: e.memset(ones[:], 1.0), writes=["ones"])
    vec = P.sbuf("vec", [128, NV], F32)
    P.dma("sp", vec[:], io["vec"], writes=["vec"], sem="vec")
    wz = P.sbuf("wz", [128, 8, 2048], BF16)
    wo = P.sbuf("wo", [128, 16, 1024], BF16)
    wq_, wsrc, wosrc = ("act", io["wz_bf"], io["wo_bf"]) if io.get("wz_bf") is not None else ("pool", io["wz"], io["mla_w_out"])

    def wload(first):
        for q4 in ([0] if first else [1, 2, 3]):
            P.dma(wq_, wz[:, :, q4 * 512:(q4 + 1) * 512], wsrc[:, :, q4 * 512:(q4 + 1) * 512],
                  reads=([] if first else ["xt0"]), writes=[("wz", q4)], sem=("wz", q4))
        if not first:
            for h2 in range(2):
                P.dma(wq_, wo[:, h2 * 8:(h2 + 1) * 8, :].rearrange("p (a b) f -> p a (b f)", b=2),
                      wosrc[:, h2 * 4:(h2 + 1) * 4, :], reads=["xt0"], writes=[("wo", h2)], sem=("wo", h2))
    wload(True)
    xt = [P.sbuf("xt%d" % i, [128, 2, 8, 256], F32) for i in range(3)]
    og = [P.sbuf("og%d" % i, [128, 16, TT], BF16) for i in range(2)]
    sq = P.sbuf("sq", [128, 2, 8, 256], BF16)
    rt = P.sbuf("rt", [128, TT], F32)
    inv = rt
    hb = [P.sbuf("h%d" % i, [128, 8, TT], BF16) for i in range(2)]
    szt = [P.sbuf("szt%d" % i, [128, TT], F32) for i in range(2)]
    yb = [P.sbuf("y%d" % i, [128, 16, TT], BF16) for i in range(2)]

    def v2(ap):
        return ap.rearrange("p (a c) -> p a c", a=2)

    def stats(xs, xkey):
        P.op("act", lambda e, xs=xs: e.activation(out=sq[:], in_=xs[:], func=AF.Square), reads=[xkey], writes=["sq"])
        ps, pk = banks.next()

        def mm(e, ps=ps):
            for k in range(8):
                r = e.matmul(ps[:], lhsT=ones[:], rhs=sq[:, :, k, :], start=(k == 0), stop=(k == 7))
            return r
        P.op("pe", mm, reads=["sq", "ones"], writes=[pk])
        P.op("act", lambda e, ps=ps: e.activation(out=rt[:], in_=ps[:], func=AF.Ln, bias=EPS, scale=1.0 / D),
             reads=[pk], writes=["rt"])
        P.op("act", lambda e: e.activation(out=inv[:], in_=rt[:], func=AF.Exp, scale=-0.5), reads=["rt"], writes=["rt"])

    def C0(t):
        xs, xkey = xt[t % 3], "xt%d" % (t % 3)
        os_, okey = og[t % 2], "og%d" % (t % 2)
        h = hb[t % 2]
        P.dma("sp", xs[:], x1T[:, 2 * t:2 * t + 2, :, :], reads=["x1T"], writes=[xkey], sem=xkey)
        for rr in range(4):
            io["oTo_dma"](P, t, rr, os_[:, rr * 4:(rr + 1) * 4, :].rearrange("p h c -> p (h c)"), (okey, rr))
        stats(xs, xkey)
        for kc in range(8):
            P.op("dve", lambda e, kc=kc, xs=xs, h=h: e.scalar_tensor_tensor(
                out=v2(h[:, kc, :]), in0=xs[:, :, kc, :], scalar=vec[:, V_G1 + kc:V_G1 + kc + 1], in1=v2(inv[:]),
                op0=ALU.mult, op1=ALU.mult), reads=[xkey, "rt", "vec"], writes=[("h", t % 2, kc)])

    def C1(t):
        os_, okey = og[t % 2], "og%d" % (t % 2)
        h, y = hb[t % 2], yb[t % 2]
        for f in range(16):
            ps, pk = banks.next()

            def mm(e, ps=ps, f=f, h=h):
                for kc in range(8):
                    r = e.matmul(ps[:], lhsT=wz[:, kc, f * 128:(f + 1) * 128], rhs=h[:, kc, :], start=(kc == 0), stop=(kc == 7))
                return r
            P.op("pe", mm, reads=[("h", t % 2, kc) for kc in range(8)] + [("wz", f // 4)], writes=[pk])
            zs = f % 2
            P.op("act", lambda e, ps=ps, zs=zs: e.activation(out=szt[zs][:], in_=ps[:], func=AF.Silu), reads=[pk], writes=[("szt", zs)])
            P.op("dve", lambda e, zs=zs, f=f, os_=os_, y=y: e.tensor_tensor(out=y[:, f, :], in0=szt[zs][:], in1=os_[:, f, :], op=ALU.mult),
                 reads=[("szt", zs), (okey, f // 4)], writes=[("y", t % 2, f)])

    def C2(t):
        xs, xkey = xt[t % 3], "xt%d" % (t % 3)
        y = yb[t % 2]
        for dc in range(8):
            ps, pk = banks.next()

            def mm(e, ps=ps, dc=dc, y=y):
                for kc in range(16):
                    r = e.matmul(ps[:], lhsT=wo[:, kc, dc * 128:(dc + 1) * 128], rhs=y[:, kc, :], start=(kc == 0), stop=(kc == 15))
                return r
            P.op("pe", mm, reads=[("y", t % 2, kc) for kc in range(16)] + [("wo", 0), ("wo", 1)], writes=[pk])
            P.op("dve", lambda e, ps=ps, dc=dc, xs=xs: e.tensor_tensor(out=xs[:, :, dc, :], in0=v2(ps[:]), in1=xs[:, :, dc, :], op=ALU.add),
                 reads=[pk, xkey], writes=[xkey])

    def C3(t):
        xs, xkey = xt[t % 3], "xt%d" % (t % 3)
        stats(xs, xkey)
        for kc in range(8):
            P.op("dve", lambda e, kc=kc, xs=xs: e.scalar_tensor_tensor(
                out=xs[:, :, kc, :], in0=xs[:, :, kc, :], scalar=vec[:, V_GF + kc:V_GF + kc + 1], in1=v2(inv[:]),
                op0=ALU.mult, op1=ALU.mult), reads=[xkey, "rt", "vec"], writes=[xkey])
        P.dma("pool", outT[:, 2 * t:2 * t + 2, :, :], xs[:], reads=[xkey],
              writes=[("outT", t)], sem="outst", final=True)

    C0(0)
    wload(False)
    C1(0)
    for t in range(NT):
        if t + 1 < NT:
            C0(t + 1)
        C2(t)
        if t + 1 < NT:
            C1(t + 1)
        C3(t)


def _dram(nc, name, shape, dt, kind):
    return nc.dram_tensor(name, list(shape), dt, kind=kind).ap()


def build_A():
    nc = bass.Bass("TRN2", target_bir_lowering=False)
    io = {
        "xT": _dram(nc, "xT", [128, 8, 8, 256], F32, "ExternalInput"),
        "xh": _dram(nc, "xh", [128, 8, HALO], F32, "ExternalInput"),
        "vec": _dram(nc, "vec", [128, NV], F32, "ExternalInput"),
        "rc": _dram(nc, "rc", [128, 64], F32, "ExternalInput"),
        "pool_w_in": _dram(nc, "pool_w_in", [128, 4, 8, 1024], F32, "ExternalInput"),
        "pool_w_group": _dram(nc, "pool_w_group", [128, 4, 4, 512], F32, "ExternalInput"),
        "pool_w_out": _dram(nc, "pool_w_out", [128, 8, 2048], F32, "ExternalInput"),
        "wl": _dram(nc, "wl", [128, 8, LATR], F32, "ExternalInput"),
        "x1T": _dram(nc, "x1T", [128, 8, 8, 256], F32, "ExternalOutput"),
        "lat": _dram(nc, "lat", [8 * 128, LATS * 256], BF16, "ExternalOutput"),
    }
    with ExitStack() as es:
        P = Prog(nc, es)
        phase_A(nc, P, io)
        P.emit()
    return nc


def build_B():
    nc = bass.Bass("TRN2", target_bir_lowering=False)
    io = {
        "latA": _dram(nc, "latA", [4 * 8 * 128, LATS * 256], BF16, "ExternalInput"),
        "posr": _dram(nc, "posr", [128, S], I32, "ExternalInput"),
        "rcol": _dram(nc, "rcol", [128, 2], F32, "ExternalInput"),
        "tri": _dram(nc, "tri", [128, 128], F32, "ExternalInput"),
        "wq": _dram(nc, "wq", [128, 3, HPC * 320], F32, "ExternalInput"),
        "wk": _dram(nc, "wk", [128, 2, HPC * 128], F32, "ExternalInput"),
        "wv": _dram(nc, "wv", [128, 2, HPC * 128], F32, "ExternalInput"),
        "oT": _dram(nc, "oT", [16 * 128, HPC * 512], BF16, "ExternalOutput"),
        "cs": nc.dram_tensor("cs", [16, 64, 2 * 512], F32).ap(),
    }
    with ExitStack() as es:
        P = Prog(nc, es)
        phase_B(nc, P, io)
        P.emit()
    return nc


def build_C():
    nc = bass.Bass("TRN2", target_bir_lowering=False)
    oTo = _dram(nc, "oTo", [4 * 4 * 128, 2048], BF16, "ExternalInput")
    io = {
        "x1T": _dram(nc, "x1T", [128, 8, 8, 256], F32, "ExternalInput"),
        "oTo": oTo,
        "oTo_dma": lambda P, t, rr, dst, key: P.dma("sp", dst, oTo[(t * 4 + rr) * 128:(t * 4 + rr + 1) * 128, :],
                                                  reads=["oTo"], writes=[key], sem=key),
        "vec": _dram(nc, "vec", [128, NV], F32, "ExternalInput"),
        "wz": _dram(nc, "wz", [128, 8, 2048], F32, "ExternalInput"),
        "mla_w_out": _dram(nc, "mla_w_out", [128, 8, 2048], F32, "ExternalInput"),
        "outT": _dram(nc, "outT", [128, 8, 8, 256], F32, "ExternalOutput"),
    }
    with ExitStack() as es:
        P = Prog(nc, es)
        phase_C(nc, P, io)
        P.emit()
    return nc


def build_fused():
    nc = bass.Bass("TRN2", target_bir_lowering=False)
    ein = lambda name, shape, dt: _dram(nc, name, shape, dt, "ExternalInput")
    x1s = nc.dram_tensor("x1s", [128, 8, 8, 256], F32).ap()
    lat_own = nc.dram_tensor("lat_own", [8 * 128, LATS * 256], BF16).ap()
    latA = nc.dram_tensor("latA", [4 * 8 * 128, LATS * 256], BF16).ap()
    o_own = nc.dram_tensor("o_own", [16 * 128, HPC * 512], BF16).ap()
    oA = nc.dram_tensor("oA", [4 * 16 * 128, HPC * 512], BF16).ap()
    vec = ein("vec", [128, NV], F32)
    groups = [[0, 1, 2, 3], [4, 5, 6, 7]]
    ioA = {
        "xT": ein("xT", [128, 8, 8, 256], F32), "xh": ein("xh", [128, 8, HALO], F32), "vec": vec,
        "rc": ein("rc", [128, 64], F32), "pool_w_in": ein("pool_w_in", [128, 4, 8, 1024], F32),
        "pool_w_group": ein("pool_w_group", [128, 4, 4, 512], F32), "pool_w_out": ein("pool_w_out", [128, 8, 2048], F32),
        "wl": ein("wl", [128, 8, LATR], F32), "x1T": x1s, "lat": lat_own,
    }
    ioB = {
        "latA": latA, "posr": ein("posr", [128, S], I32), "rcol": ein("rcol", [128, 2], F32), "tri": ein("tri", [128, 128], F32),
        "wq": ein("wq", [128, 3, HPC * 320], F32), "wk": ein("wk", [128, 2, HPC * 128], F32),
        "wv": ein("wv", [128, 2, HPC * 128], F32), "oT": o_own, "cs": nc.dram_tensor("cs", [16, 64, 2 * 512], F32).ap(),
    }

    def ag1(P, k):
        P.custom_dma("pool", lambda e: e.collective_compute(
            "AllGather", ALU.bypass, replica_groups=groups,
            ins=[lat_own[k * 256:(k + 1) * 256, :]], outs=[latA[k * 1024:(k + 1) * 1024, :]]),
            reads=[("lat", 2 * k), ("lat", 2 * k + 1)], writes=[("latA", k)], sem=("ag1", k), inc=1)

    def ag2(P, m):
        P.custom_dma("pool", lambda e: e.collective_compute(
            "AllGather", ALU.bypass, replica_groups=groups,
            ins=[o_own[m * 256:(m + 1) * 256, :]], outs=[oA[m * 1024:(m + 1) * 1024, :]]),
            reads=[("oT", 2 * m), ("oT", 2 * m + 1)], writes=[("oTo", m)], sem=("ag2", m), inc=1)

    ioA["ag1"] = ag1
    ioB["ag2"] = ag2
    wz_bf = nc.dram_tensor("wz_bf", [128, 8, 2048], BF16).ap()
    wo_bf = nc.dram_tensor("wo_bf", [128, 8, 2048], BF16).ap()

    def prefetch(P, i):
        name, dst = (("wz", wz_bf), ("mla_w_out", wo_bf))[i // 8]
        k = i % 8
        P.dma("pool", dst[:, k, :], wsrc[name][:, k, :], writes=[("pf", i)], sem=("pf", i % 2))
    ioB["prefetch"] = prefetch

    ag2_tok = {}

    def oTo_dma(P, t, rr, dst, key):
        def emit(e, sem):
            core = e.partition_id()
            for k in range(8):
                m = (k % 4) * 2 + t // 2
                row = m * 1024 + rr * 256 + (t % 2) * 128
                with e.If(core == k):
                    e.wait_ge(ag2_tok[m][0], ag2_tok[m][1])
                    e.dma_start(out=dst, in_=oA[row:row + 128, :]).then_inc(sem, 16)
        P.custom_dma("sp", emit, writes=[key], sem=key, self_inc=True)

    wsrc = {"wz": ein("wz", [128, 8, 2048], F32), "mla_w_out": ein("mla_w_out", [128, 8, 2048], F32)}
    ioC = {
        "x1T": x1s, "oTo": oA, "oTo_dma": oTo_dma, "vec": vec, "wz": wsrc["wz"],
        "mla_w_out": wsrc["mla_w_out"], "wz_bf": wz_bf, "wo_bf": wo_bf,
        "outT": _dram(nc, "outT", [128, 8, 8, 256], F32, "ExternalOutput"),
    }
    with ExitStack() as sem_es:
        with ExitStack() as es:
            PA = Prog(nc, es, sem_es)
            phase_A(nc, PA, ioA)
            PA.barrier(skip="ag1")
            PA.emit()
        with ExitStack() as es:
            PB = Prog(nc, es, sem_es)
            for k in range(4):
                PB.extern(PA.sems[("dma", ("ag1", k))], PA.dma_cnt[("ag1", k)], [("latA", k)])
            phase_B(nc, PB, ioB)
            PB.barrier(skip="ag2")
            PB.emit()
        with ExitStack() as es:
            PC = Prog(nc, es, sem_es)
            for m in range(8):
                ag2_tok[m] = (PB.sems[("dma", ("ag2", m))], PB.dma_cnt[("ag2", m)])
                PC.extern(ag2_tok[m][0], ag2_tok[m][1], [("oTo", m)])
            phase_C(nc, PC, ioC)
            PC.barrier()
            PC.emit()
    return nc


def _cols(v):
    return np.ascontiguousarray(np.asarray(v, np.float32).reshape(-1, 128).T)


def _pm(w):
    nk = w.shape[0] // 128
    return np.ascontiguousarray(w.reshape(nk, 128, w.shape[1]).transpose(1, 0, 2))


def host_inputs(inp):
    f = lambda k: np.asarray(inp[k], np.float32)
    x = f("x")
    vec = np.concatenate([_cols(f("pool_norm")[0]), _cols(f("mla_norm")[0]), _cols(f("pool_scale")[0]),
                          _cols(f("mla_q_norm")[0]), _cols(f("mla_kv_norm")[0]), _cols(f("final_norm"))], axis=1)
    assert vec.shape == (128, NV)
    perm = np.concatenate([np.concatenate([np.arange(g * 512, (g + 1) * 512), 2048 + np.arange(g * 512, (g + 1) * 512)])
                           for g in range(4)])
    w_in_p = _pm(f("pool_w_in")[0][:, perm]).reshape(128, 8, 4, 1024).transpose(0, 2, 1, 3)
    w_in_p = np.ascontiguousarray(w_in_p)
    wg_p = np.ascontiguousarray(f("pool_w_group")[0].reshape(4, 4, 128, 512).transpose(2, 0, 1, 3))
    wo0_p = _pm(f("pool_w_out")[0]).reshape(128, 8, 2048)
    wo1_p = _pm(f("mla_w_out")[0]).reshape(128, 8, 2048)
    mw = f("mla_w_in")[0]
    wl_p = _pm(np.concatenate([mw[:, :704], mw[:, 672:704], mw[:, 640:672], mw[:, 640:704]], axis=1))
    wz_p = _pm(mw[:, 704:])
    wqb, wkvb = f("mla_w_q_b")[0], f("mla_w_kv_b")[0]
    invf = (np.float32(1.0) / (np.float32(10000.0) ** (np.arange(0, 64, 2, dtype=np.float32) / np.float32(64)))).astype(np.float32)
    rcol = np.stack([np.tile(invf, 4), np.tile(np.concatenate([-np.ones(32, np.float32), np.ones(32, np.float32)]), 2)], axis=1)
    tri = (np.arange(128)[None, :] >= np.arange(128)[:, None]).astype(np.float32)
    A, Bm, C = [], [], []
    for c in range(8):
        b, r = c // 4, c % 4
        t0 = r * TOK
        xT = np.ascontiguousarray(x[b, t0:t0 + TOK].reshape(8, 256, 8, 128).transpose(3, 0, 2, 1))
        xh = np.zeros((HALO, D), np.float32)
        if t0 > 0:
            xh[:] = x[b, t0 - HALO:t0]
        xh = np.ascontiguousarray(xh.reshape(HALO, 8, 128).transpose(2, 1, 0))
        rc = np.zeros((128, 64), np.float32)
        for g, w in enumerate((2, 4, 8, 16)):
            rc[:, g * 16:(g + 1) * 16] = 1.0 / np.minimum(t0 + np.arange(16) + 1, w).astype(np.float32)
        A.append({"xT": xT, "xh": xh, "vec": vec, "rc": rc, "pool_w_in": w_in_p,
                  "pool_w_group": wg_p, "pool_w_out": wo0_p, "wl": wl_p})
        hs = [4 * r + hl for hl in range(4)]
        wq = np.concatenate([np.concatenate([wqb[:, h * 192:h * 192 + 192], wqb[:, h * 192 + 160:h * 192 + 192],
                                             wqb[:, h * 192 + 128:h * 192 + 160], wqb[:, h * 192 + 128:h * 192 + 192]], axis=1)
                             for h in hs], axis=1)
        wk = np.concatenate([wkvb[:, h * 256:h * 256 + 128] for h in hs], axis=1)
        wv = np.concatenate([wkvb[:, h * 256 + 128:h * 256 + 256] for h in hs], axis=1)
        posr = np.ascontiguousarray(np.broadcast_to(np.asarray(inp["positions"])[b].astype(np.int32)[None, :], (128, S)))
        Bm.append({"posr": posr, "rcol": rcol, "tri": tri, "wq": _pm(wq), "wk": _pm(wk), "wv": _pm(wv)})
        C.append({"vec": vec, "wz": wz_p, "mla_w_out": wo1_p})
    return A, Bm, C


def _assemble(res):
    out = np.empty((NB, S, D), np.float32)
    for c in range(8):
        b, r = c // 4, c % 4
        o = res[c]["outT"]
        out[b, r * TOK:(r + 1) * TOK, :] = o.transpose(1, 3, 2, 0).reshape(TOK, D)
    return out


_NC = {}


def _get(name, fn):
    if name not in _NC:
        _NC[name] = fn()
    return _NC[name]


FUSED = True


def kernel(**inputs):
    A, Bm, C = host_inputs(inputs)
    cores = list(range(8))
    if FUSED:
        maps = []
        for c in cores:
            m = {}
            m.update(A[c]); m.update(Bm[c]); m.update(C[c])
            maps.append(m)
        res = run_bass_kernel_spmd(_get("F", build_fused), maps, core_ids=cores).results
        return _assemble(res)
    ra = run_bass_kernel_spmd(_get("A", build_A), A, core_ids=cores).results
    for c in cores:
        b = c // 4
        Bm[c]["latA"] = np.concatenate([ra[b * 4 + r]["lat"][k * 256:(k + 1) * 256] for k in range(4) for r in range(4)], axis=0)
    rb = run_bass_kernel_spmd(_get("B", build_B), Bm, core_ids=cores).results
    for c in cores:
        b, r = c // 4, c % 4
        C[c]["x1T"] = ra[c]["x1T"]
        C[c]["oTo"] = np.ascontiguousarray(np.concatenate(
            [rb[b * 4 + rr]["oT"][(r * 4 + t) * 128:(r * 4 + t + 1) * 128, :] for t in range(4) for rr in range(4)], axis=0))
    rc = run_bass_kernel_spmd(_get("C", build_C), C, core_ids=cores).results
    return _assemble(rc)
```

```python
import numpy as np
from contextlib import ExitStack
import concourse.bass as bass
import concourse.mybir as mybir
from concourse.bass_utils import run_bass_kernel_spmd

F32 = mybir.dt.float32
BF16 = mybir.dt.bfloat16
I32 = mybir.dt.int32
AF = mybir.ActivationFunctionType
ALU = mybir.AluOpType


class _Op:
    __slots__ = ("eng", "emit", "deps", "is_dma", "semkey", "signal", "count", "waits", "final", "inc", "self_inc", "ext")

    def __init__(self):
        self.inc, self.self_inc, self.ext = 16, False, None


class Prog:
    COMPUTE = ("pe", "act", "dve", "pool")
    NSEM = 0
    NPROG = 0

    def __init__(self, nc, es, sem_es=None):
        self.nc, self.es = nc, es
        self.sem_es = sem_es if sem_es is not None else es
        Prog.NPROG += 1
        self.uid = Prog.NPROG
        self.ops = []
        self.last_writer = {}
        self.readers = {}
        self.last_dma_on_sem = {}
        self.nsb = 0

    def sbuf(self, name, shape, dtype):
        return self.es.enter_context(self.nc.sbuf_tensor("sb%d_" % self.uid + name, list(shape), dtype))

    def psum(self, name, shape, dtype):
        return self.es.enter_context(self.nc.psum_tensor("pp%d_" % self.uid + name, list(shape), dtype))

    def _record(self, op, reads, writes):
        idx = len(self.ops)
        deps = {}

        def add(i, kind):
            if i is not None:
                deps[i] = deps.get(i, 0) | kind

        for k in reads:
            add(self.last_writer.get(k), 1)
        for k in writes:
            add(self.last_writer.get(k), 2)
            for i in self.readers.get(k, ()):
                add(i, 4)
        deps.pop(idx, None)
        op.deps = deps
        self.ops.append(op)
        for k in writes:
            self.last_writer[k] = idx
            self.readers[k] = []
        for k in reads:
            lst = self.readers.setdefault(k, [])
            if not op.is_dma:
                lst[:] = [i for i in lst if self.ops[i].is_dma or self.ops[i].eng != op.eng]
            lst.append(idx)
        return idx

    def op(self, eng, emit, reads=(), writes=()):
        o = _Op()
        o.eng, o.emit, o.is_dma, o.semkey, o.signal, o.count, o.final = eng, emit, False, None, False, 0, False
        return self._record(o, list(reads), list(writes))

    def dma(self, queue, out, in_, reads=(), writes=(), sem=None, final=False, **kw):
        o = _Op()
        o.eng, o.is_dma, o.semkey, o.signal, o.count, o.final = queue, True, sem, True, 0, final
        o.emit = lambda e: e.dma_start(out=out, in_=in_, **kw)
        idx = self._record(o, list(reads), list(writes))
        prev = self.last_dma_on_sem.get(sem)
        if prev is not None:
            o.deps[prev] = o.deps.get(prev, 0) | 2
        self.last_dma_on_sem[sem] = idx
        if queue == "pool":
            self.pool_dmas = getattr(self, "pool_dmas", [])
            if len(self.pool_dmas) >= 2:
                o.deps[self.pool_dmas[-2]] = o.deps.get(self.pool_dmas[-2], 0) | 2
            self.pool_dmas.append(idx)
        return idx

    def custom_dma(self, queue, emit, reads=(), writes=(), sem=None, final=False, inc=16, self_inc=False):
        o = _Op()
        o.eng, o.is_dma, o.semkey, o.signal, o.count, o.final = queue, True, sem, True, 0, final
        o.inc, o.self_inc = inc, self_inc
        o.emit = emit
        idx = self._record(o, list(reads), list(writes))
        prev = self.last_dma_on_sem.get(sem)
        if prev is not None:
            o.deps[prev] = o.deps.get(prev, 0) | 2
        self.last_dma_on_sem[sem] = idx
        return idx

    def extern(self, sem, count, writes):
        o = _Op()
        o.eng, o.is_dma, o.semkey, o.signal, o.count, o.final = "sp", True, ("ext", len(self.ops)), True, count, False
        o.ext, o.emit = sem, None
        return self._record(o, [], list(writes))

    def barrier(self, skip=None):
        last = {}
        for i, o in enumerate(self.ops):
            if o.is_dma and skip is not None and isinstance(o.semkey, tuple) and o.semkey[0] == skip:
                continue
            last[("dma", o.semkey) if o.is_dma else ("eng", o.eng)] = i
        for eng in ("pe", "act", "dve", "pool", "sp"):
            o = _Op()
            o.eng, o.is_dma, o.semkey, o.signal, o.count, o.final = eng, False, None, False, 0, False
            o.emit = lambda e: e.nop()
            o.deps = {i: 1 for k, i in last.items() if k != ("eng", eng)}
            self.ops.append(o)

    def emit(self):
        nc, es, ops = self.nc, self.es, self.ops
        for o in ops:
            keep = []
            for i, kind in o.deps.items():
                p = ops[i]
                if not p.is_dma and not o.is_dma and p.eng == o.eng:
                    if o.eng == "pe":
                        continue
                if not p.is_dma and o.is_dma and p.eng == o.eng and not (kind & 3):
                    pass
                keep.append(i)
                p.signal = True
            o.deps = keep
        eng_cnt = {e: 0 for e in ("pe", "act", "dve", "pool", "sp")}
        dma_cnt = {}
        for o in ops:
            if o.ext is not None:
                continue
            if o.is_dma:
                dma_cnt[o.semkey] = dma_cnt.get(o.semkey, 0) + o.inc
                o.count = dma_cnt[o.semkey]
            elif o.signal:
                eng_cnt[o.eng] += 1
                o.count = eng_cnt[o.eng]
        sems = {}
        for e in self.COMPUTE:
            if eng_cnt[e]:
                Prog.NSEM += 1
                sems[("eng", e)] = self.sem_es.enter_context(nc.semaphore("sem%d" % Prog.NSEM))
        for k in dma_cnt:
            Prog.NSEM += 1
            sems[("dma", k)] = self.sem_es.enter_context(nc.semaphore("sem%d" % Prog.NSEM))
        for o in ops:
            if o.ext is not None:
                sems[("dma", o.semkey)] = o.ext
        self.sems, self.dma_cnt = sems, dma_cnt

        def tok(p):
            return (sems[("dma", p.semkey)] if p.is_dma else sems[("eng", p.eng)], p.count,
                    ("dma", p.semkey) if p.is_dma else ("eng", p.eng))

        waited = {e: {} for e in eng_cnt}
        for o in ops:
            w = {}
            for i in o.deps:
                s, c, k = tok(ops[i])
                if c > waited[o.eng].get(k, 0) and c > w.get(k, (None, 0))[1]:
                    w[k] = (s, c)
            for k, (s, c) in w.items():
                waited[o.eng][k] = c
            o.waits = list(w.values())
        finals = [(sems[("dma", k)], dma_cnt[k]) for k in dma_cnt
                  if any(o.final and o.semkey == k for o in ops if o.is_dma)]
        by_eng = {e: [o for o in ops if o.eng == e] for e in eng_cnt}

        def run(ename, e):
            for o in by_eng[ename]:
                if o.ext is not None:
                    continue
                for s, c in o.waits:
                    e.wait_ge(s, c)
                if o.self_inc:
                    o.emit(e, sems[("dma", o.semkey)])
                    continue
                inst = o.emit(e)
                if inst is None:
                    continue
                if o.is_dma:
                    inst.then_inc(sems[("dma", o.semkey)], o.inc)
                elif o.signal:
                    inst.then_inc(sems[("eng", o.eng)], 1)
            if ename == "sp":
                for s, c in finals:
                    e.wait_ge(s, c)

        with nc.Block() as block:
            @block.tensor
            def _(e):
                run("pe", e)

            @block.scalar
            def _(e):
                run("act", e)

            @block.vector
            def _(e):
                run("dve", e)

            @block.gpsimd
            def _(e):
                run("pool", e)

            @block.sync
            def _(e):
                run("sp", e)


D = 1024
S = 8192
NB = 2
TOK = 2048
HALO = 16
NH = 16
HPC = 4
EPS = 1e-6
LATS = 7
LATR = 832
SCALE = 192 ** -0.5

V_G0, V_G1, V_SC, V_GQ, V_GKV, V_GF, NV = 0, 8, 16, 32, 35, 37, 45


class Banks:
    def __init__(self, P, n=8, prefix="ps"):
        self.t = [P.psum("%s%d" % (prefix, i), [128, 512], F32) for i in range(n)]
        self.keys = [(prefix, i) for i in range(n)]
        self.i = 0

    def next(self):
        i = self.i
        self.i = (self.i + 1) % len(self.t)
        return self.t[i], self.keys[i]


def rms_inv(P, banks, ones, src_sq, nk, W, nfeat, rt, inv, keys_sq, key_rt, key_inv):
    ps, pk = banks.next()

    def mm(e):
        for k in range(nk):
            r = e.matmul(ps[:, :W], lhsT=ones[:], rhs=src_sq[:, k, :W], start=(k == 0), stop=(k == nk - 1))
        return r
    P.op("pe", mm, reads=list(keys_sq) + ["ones"], writes=[pk])
    P.op("act", lambda e: e.activation(out=rt[:, :W], in_=ps[:, :W], func=AF.Ln, bias=EPS, scale=1.0 / nfeat),
         reads=[pk], writes=[key_rt])
    P.op("act", lambda e: e.activation(out=inv[:, :W], in_=rt[:, :W], func=AF.Exp, scale=-0.5), reads=[key_rt], writes=[key_inv])


def phase_A(nc, P, io):
    TT, NT = 256, TOK // 256
    xT, x1T, lat = io["xT"], io["x1T"], io["lat"]
    banks = Banks(P)
    ones = P.sbuf("ones", [128, 128], BF16)
    P.op("pool", lambda e: e.memset(ones[:], 1.0), writes=["ones"])
    vec = P.sbuf("vec", [128, NV], F32)
    P.dma("sp", vec[:], io["vec"], writes=["vec"], sem="vec")
    rc = P.sbuf("rc", [128, 64], F32)
    P.dma("sp", rc[:], io["rc"], writes=["rc"], sem="rc")

    w_in = P.sbuf("w_in", [128, 8, 4096], BF16)
    wg = P.sbuf("wg", [128, 4, 4, 512], BF16)
    wo = P.sbuf("wo", [128, 16, 1024], BF16)
    wl = P.sbuf("wl", [128, 8, LATR], BF16)
    for g in range(4):
        P.dma("pool", w_in[:, :, g * 1024:(g + 1) * 1024], io["pool_w_in"][:, g, :, :],
              writes=[("w_in", g)], sem=("w_in", g))
        P.dma("pool", wg[:, g, :, :], io["pool_w_group"][:, g, :, :], writes=[("wg", g)], sem=("wg", g))
    for h2 in range(2):
        P.dma("pool", wo[:, h2 * 8:(h2 + 1) * 8, :].rearrange("p (a b) f -> p a (b f)", b=2),
              io["pool_w_out"][:, h2 * 4:(h2 + 1) * 4, :], writes=[("wo", h2)], sem=("wo", h2))
    P.dma("pool", wl[:], io["wl"], writes=["wl"], sem="wl")

    xt = [P.sbuf("xt%d" % i, [128, 8, TT], F32) for i in range(2)]
    sq = P.sbuf("sq", [128, 8, TT], BF16)
    rt = P.sbuf("rt", [128, TT], F32)
    inv = P.sbuf("inv", [128, TT], F32)
    hb = [P.sbuf("h%d" % i, [128, 8, TT], BF16) for i in range(3)]
    NU = 3
    u = [P.sbuf("u%d" % i, [128, HALO + TT], F32) for i in range(NU)]
    pa = P.sbuf("pa", [128, HALO + TT], F32)
    pb = P.sbuf("pb", [128, HALO + TT], F32)
    tmp16 = P.sbuf("tmp16", [128, 16], F32)
    pooled = [P.sbuf("pooled%d" % i, [128, 4, TT], BF16) for i in range(2)]
    sz = [P.sbuf("sz%d" % i, [128, 4, TT], BF16) for i in range(2)]
    yb = [P.sbuf("y%d" % i, [128, 16, TT], BF16) for i in range(2)]
    uh = [P.sbuf("uh%d" % i, [128, 16, HALO], F32) for i in range(2)]
    ql = P.sbuf("ql", [128, 5, TT], F32)
    sql = P.sbuf("sql", [128, 5, TT], BF16)
    lat_o = [P.sbuf("lat_o%d" % i, [128, LATS, TT], BF16) for i in range(2)]
    for i in range(2):
        P.op("pool", lambda e, i=i: e.memset(lat_o[i][:], 0.0), writes=[("lat_o%d" % i, j) for j in range(LATS)])
    ucount = [0]

    def norm_h(xs, xkey, W, gcol0, hi):
        h = hb[hi]
        P.op("act", lambda e: e.activation(out=sq[:, :, :W], in_=xs[:, :, :W], func=AF.Square),
             reads=[xkey], writes=["sq"])
        rms_inv(P, banks, ones, sq, 8, W, D, rt, inv, ["sq"], "rt", "inv")
        for kc in range(8):
            P.op("dve", lambda e, kc=kc: e.scalar_tensor_tensor(
                out=h[:, kc, :W], in0=xs[:, kc, :W], scalar=vec[:, gcol0 + kc:gcol0 + kc + 1],
                in1=inv[:, :W], op0=ALU.mult, op1=ALU.mult),
                reads=[xkey, "inv", "vec"], writes=[("h", hi, kc)])

    def unit8(wkeys, lhs_of, W, hi):
        ps, pk = banks.next()
        h = hb[hi]

        def mm(e):
            for kc in range(8):
                r = e.matmul(ps[:, :W], lhsT=lhs_of(kc), rhs=h[:, kc, :W], start=(kc == 0), stop=(kc == 7))
            return r
        P.op("pe", mm, reads=[("h", hi, kc) for kc in range(8)] + list(wkeys), writes=[pk])
        return ps, pk

    xh = xt[1]
    P.dma("sp", xh[:, :, :HALO], io["xh"], writes=["xt1"], sem="xt1")
    norm_h(xh, "xt1", HALO, V_G0, 2)

    def halo(g):
        for f in range(4 * g, 4 * g + 4):
            ps, pk = unit8([("w_in", g)], lambda kc, f=f, g=g: w_in[:, kc, g * 1024 + (f % 4) * 128:g * 1024 + (f % 4 + 1) * 128], HALO, 2)
            P.op("act", lambda e, ps=ps, f=f: e.activation(out=uh[0][:, f, :], in_=ps[:, :HALO], func=AF.Copy),
                 reads=[pk], writes=[("uh", 0, f)])

    def F0(t):
        xs, xkey = xt[t % 2], "xt%d" % (t % 2)
        P.dma("sp", xs[:], xT[:, t, :, :], writes=[xkey], sem=xkey)
        norm_h(xs, xkey, TT, V_G0, t % 2)

    def F1(t, g):
        uhc, uhn = uh[t % 2], uh[(t + 1) % 2]
        slot = (t * 4 + g) % 2
        w = 2 << g
        hi = t % 2
        for fc in range(4):
            f = g * 4 + fc
            us = u[ucount[0] % NU]
            ukey = "u%d" % (ucount[0] % NU)
            ucount[0] += 1
            ps, pk = unit8([("w_in", g)], lambda kc, fc=fc, g=g: w_in[:, kc, g * 1024 + fc * 128:g * 1024 + (fc + 1) * 128], TT, hi)
            P.op("act", lambda e, ps=ps, us=us: e.activation(out=us[:, HALO:], in_=ps[:, :TT], func=AF.Copy),
                 reads=[pk], writes=[ukey])
            P.op("act", lambda e, ps=ps, f=f, uhn=uhn: e.activation(out=uhn[:, f, :], in_=ps[:, TT - HALO:TT], func=AF.Copy),
                 reads=[pk], writes=[("uh", (t + 1) % 2, f)])
            P.op("act", lambda e, us=us, f=f, uhc=uhc: e.activation(out=us[:, :HALO], in_=uhc[:, f, :], func=AF.Copy),
                 reads=[("uh", t % 2, f)], writes=[ukey])
            ps, pk = unit8([("w_in", g)], lambda kc, fc=fc, g=g: w_in[:, kc, g * 1024 + 512 + fc * 128:g * 1024 + 512 + (fc + 1) * 128], TT, hi)
            P.op("act", lambda e, ps=ps, slot=slot, fc=fc: e.activation(out=sz[slot][:, fc, :], in_=ps[:, :TT], func=AF.Silu),
                 reads=[pk], writes=[("sz", slot, fc)])
            E = HALO + TT
            src, skey = us, ukey
            bufs = [(pa, "pa"), (pb, "pb")]
            step, lo, bi = 1, 1, 0
            while step < w:
                dst, dkey = bufs[bi]
                P.op("dve", lambda e, dst=dst, src=src, step=step, lo=lo: e.tensor_tensor(
                    out=dst[:, lo:E], in0=src[:, lo:E], in1=src[:, lo - step:E - step], op=ALU.add),
                    reads=[skey], writes=[dkey])
                src, skey = dst, dkey
                step *= 2
                lo = 2 * step - 1
                bi ^= 1
            P.op("dve", lambda e, src=src, us=us, slot=slot, fc=fc, w=w: e.scalar_tensor_tensor(
                out=pooled[slot][:, fc, :], in0=src[:, HALO:E], scalar=1.0 / w, in1=us[:, HALO:E],
                op0=ALU.mult, op1=ALU.subtract),
                reads=[skey, ukey], writes=[("pooled", slot, fc)])
            if t == 0:
                P.op("dve", lambda e, src=src, g=g: e.tensor_tensor(
                    out=tmp16[:], in0=src[:, HALO:HALO + 16], in1=rc[:, g * 16:(g + 1) * 16], op=ALU.mult),
                    reads=[skey, "rc"], writes=["tmp16"])
                P.op("dve", lambda e, us=us, slot=slot, fc=fc: e.tensor_tensor(
                    out=pooled[slot][:, fc, 0:16], in0=tmp16[:], in1=us[:, HALO:HALO + 16], op=ALU.subtract),
                    reads=["tmp16", ukey], writes=[("pooled", slot, fc)])

    def F2(t, g):
        slot = (t * 4 + g) % 2
        y, yk = yb[t % 2], t % 2
        for fo in range(4):
            f = g * 4 + fo
            ps, pk = banks.next()

            def mm(e, ps=ps, g=g, fo=fo, slot=slot):
                for kc in range(4):
                    r = e.matmul(ps[:, :TT], lhsT=wg[:, g, kc, fo * 128:(fo + 1) * 128],
                                 rhs=pooled[slot][:, kc, :], start=(kc == 0), stop=(kc == 3))
                return r
            P.op("pe", mm, reads=[("pooled", slot, kc) for kc in range(4)] + [("wg", g)], writes=[pk])
            P.op("dve", lambda e, ps=ps, f=f, fo=fo, slot=slot, y=y: e.scalar_tensor_tensor(
                out=y[:, f, :], in0=ps[:, :TT], scalar=vec[:, V_SC + f:V_SC + f + 1], in1=sz[slot][:, fo, :],
                op0=ALU.mult, op1=ALU.mult),
                reads=[pk, ("sz", slot, fo), "vec"], writes=[("y", yk, f)])

    def F3(t):
        xs, xkey = xt[t % 2], "xt%d" % (t % 2)
        y, yk = yb[t % 2], t % 2
        for dc in range(8):
            ps, pk = banks.next()

            def mm(e, ps=ps, dc=dc, y=y):
                for kc in range(16):
                    r = e.matmul(ps[:, :TT], lhsT=wo[:, kc, dc * 128:(dc + 1) * 128], rhs=y[:, kc, :],
                                 start=(kc == 0), stop=(kc == 15))
                return r
            P.op("pe", mm, reads=[("y", yk, kc) for kc in range(16)] + [("wo", 0), ("wo", 1)], writes=[pk])
            P.op("dve", lambda e, ps=ps, dc=dc, xs=xs: e.tensor_tensor(
                out=xs[:, dc, :], in0=ps[:, :TT], in1=xs[:, dc, :], op=ALU.add),
                reads=[pk, xkey], writes=[xkey])

    def F4(t):
        xs, xkey = xt[t % 2], "xt%d" % (t % 2)
        P.dma("pool", x1T[:, t, :, :], xs[:], reads=[xkey], writes=[("x1T", t)],
              sem=("x1st", t % 2), final=True)
        norm_h(xs, xkey, TT, V_G1, 2)

    def F5(t):
        lo_t, lkey = lat_o[t % 2], "lat_o%d" % (t % 2)
        h = hb[2]
        for j in range(5):
            ps, pk = unit8(["wl"], lambda kc, j=j: wl[:, kc, j * 128:(j + 1) * 128], TT, 2)
            P.op("act", lambda e, ps=ps, j=j: e.activation(out=ql[:, j, :], in_=ps[:, :TT], func=AF.Copy),
                 reads=[pk], writes=[("ql", j)])
        for j in (5, 6):
            ps, pk = unit8(["wl"], lambda kc, j=j: wl[:, kc, 640 + (j - 5) * 64:768 + (j - 5) * 64], TT, 2)
            P.op("act", lambda e, ps=ps, lo_t=lo_t, j=j: e.activation(out=lo_t[0:64, j, :], in_=ps[0:64, :TT], func=AF.Copy),
                 reads=[pk], writes=[(lkey, j)])
        for (j0, nj, nfeat, gc) in ((0, 3, 384, V_GQ), (3, 2, 256, V_GKV)):
            P.op("act", lambda e, j0=j0, nj=nj: e.activation(out=sql[:, j0:j0 + nj, :], in_=ql[:, j0:j0 + nj, :], func=AF.Square),
                 reads=[("ql", j) for j in range(j0, j0 + nj)], writes=[("sql", j0)])
            ps, pk = banks.next()

            def mm(e, ps=ps, j0=j0, nj=nj):
                for k in range(nj):
                    r = e.matmul(ps[:, :TT], lhsT=ones[:], rhs=sql[:, j0 + k, :], start=(k == 0), stop=(k == nj - 1))
                return r
            P.op("pe", mm, reads=[("sql", j0), "ones"], writes=[pk])
            P.op("act", lambda e, ps=ps, nfeat=nfeat: e.activation(out=rt[:, :TT], in_=ps[:, :TT], func=AF.Ln, bias=EPS, scale=1.0 / nfeat),
                 reads=[pk], writes=["rt"])
            P.op("act", lambda e: e.activation(out=inv[:, :TT], in_=rt[:, :TT], func=AF.Exp, scale=-0.5), reads=["rt"], writes=["inv"])
            for k in range(nj):
                j = j0 + k
                P.op("dve", lambda e, j=j, k=k, gc=gc, lo_t=lo_t: e.scalar_tensor_tensor(
                    out=lo_t[:, j, :], in0=ql[:, j, :], scalar=vec[:, gc + k:gc + k + 1], in1=inv[:, :TT],
                    op0=ALU.mult, op1=ALU.mult),
                    reads=[("ql", j), "inv", "vec"], writes=[(lkey, j)])
        P.dma("pool", lat[t * 128:(t + 1) * 128, :], lo_t[:].rearrange("p j c -> p (j c)"), reads=[(lkey, j) for j in range(LATS)],
              writes=[("lat", t)], sem=("latst", t % 2), final=True)
        if t % 2 == 1 and io.get("ag1") is not None:
            io["ag1"](P, t // 2)

    F0(0)
    for t in range(NT):
        prev = t - 1
        if t == 0:
            halo(0)
        F1(t, 0)
        if prev >= 0:
            F2(prev, 3)
        if t == 0:
            halo(1)
        F1(t, 1)
        if prev >= 0:
            F3(prev)
        F2(t, 0)
        if t == 0:
            halo(2)
        F1(t, 2)
        if prev >= 0:
            F4(prev)
        F2(t, 1)
        if t == 0:
            halo(3)
        F1(t, 3)
        if t + 1 < NT:
            F0(t + 1)
        if prev >= 0:
            F5(prev)
        F2(t, 2)
    F2(NT - 1, 3)
    F3(NT - 1)
    F4(NT - 1)
    F5(NT - 1)


PI = float(np.pi)
C1 = 6.28125
C2 = float(2.0 * np.pi - 6.28125)


def phase_B(nc, P, io):
    QT = 512
    NQ = S // QT
    latA, oT = io["latA"], io["oT"]
    ones = P.sbuf("ones", [128, 128], BF16)
    P.op("pool", lambda e: e.memset(ones[:], 1.0), writes=["ones"])
    tri = P.sbuf("tri", [128, 128], BF16)
    P.dma("pool", tri[:], io["tri"], writes=["tri"], sem="tri")
    rcol = P.sbuf("rcol", [128, 2], F32)
    P.dma("sp", rcol[:], io["rcol"], writes=["rcol"], sem="rcol")
    wq = P.sbuf("wq", [128, 3, HPC * 320], BF16)
    wk = P.sbuf("wk", [128, 2, HPC * 128], BF16)
    wv = P.sbuf("wv", [128, 2, HPC * 128], BF16)
    P.dma("pool", wk[:], io["wk"], writes=["wk"], sem="wk")
    P.dma("pool", wv[:], io["wv"], writes=["wv"], sem="wv")
    P.dma("pool", wq[:], io["wq"], writes=["wq"], sem="wq")

    KT = P.sbuf("KT", [128, HPC, S], BF16)
    V = P.sbuf("V", [128, S // 128, HPC * 128], BF16)
    KR = P.sbuf("KR", [128, S], BF16)
    pst = [P.psum("ps%d" % i, [128, 512], F32) for i in range(8)]
    pskey = [("ps", i) for i in range(8)]
    latA_v = latA.rearrange("(k r t p) (j c) -> p k r t j c", k=4, r=4, t=2, p=128, j=LATS)

    SB = [0, 1, 2, 7]
    OB = [3, 4]
    DB = [5, 6]
    sbi = [0]

    def sbank():
        b = SB[sbi[0] % 4]
        sbi[0] += 1
        return pst[b], pskey[b]

    posi = P.sbuf("posi", [128, QT], I32)
    ang = P.sbuf("ang", [128, QT], F32)
    kf = P.sbuf("kf", [128, QT], F32)
    rr = P.sbuf("rr", [128, QT], F32)
    CS = [P.sbuf("CS%d" % i, [128, QT], F32) for i in range(2)]
    Ct = [c[0:64, :] for c in CS]
    S0 = P.sbuf("S0", [64, QT], F32)
    kk = [P.sbuf("kk%d" % i, [64, 2, 2, QT // 2], BF16) for i in range(2)]

    def v2(ap):
        return ap.rearrange("p (t c) -> p t c", t=2)
    kvn = [P.sbuf("kvn%d" % i, [128, 2, 2, QT // 2], BF16) for i in range(2)]
    qn = [P.sbuf("qn%d" % i, [128, 2, 3, QT // 2], BF16) for i in range(2)]
    t1f = P.sbuf("t1", [128, QT], F32)
    bi = [0]
    ev = [0]

    def evac_copy(dst, src, rkeys, wkeys):
        ev[0] += 1
        if ev[0] % 2:
            P.op("act", lambda e: e.activation(out=dst, in_=src, func=AF.Copy), reads=rkeys, writes=wkeys)
        else:
            P.op("dve", lambda e: e.tensor_copy(out=dst, in_=src), reads=rkeys, writes=wkeys)

    def prep_steps(tt):
        r_ = tt // 4
        sl = tt % 2
        g0 = tt * QT
        ck = ("latA", tt % 4)
        steps = []

        def dve(fn, reads, writes):
            steps.append(lambda: P.op("dve", fn, reads=reads, writes=writes))

        def loads():
            P.dma("sp", posi[:], io["posr"][:, g0:g0 + QT], writes=["posi"], sem="posi")
            P.dma("sp", kk[sl][:], latA_v[0:64, tt % 4, r_, :, 5:7, :], reads=[ck], writes=[("kk", sl)], sem=("kk", sl))
            P.dma("sp", kvn[sl][:], latA_v[:, tt % 4, r_, :, 3:5, :], reads=[ck], writes=[("kvn", sl)], sem=("kvn", sl))
            P.dma("sp", qn[sl][:], latA_v[:, tt % 4, r_, :, 0:3, :], reads=[ck], writes=[("qn", sl)], sem=("qn", sl))
        steps.append(loads)
        for hl in range(HPC):
            def kstep(hl=hl):
                ps, pk = sbank()

                def mm(e, ps=ps, hl=hl, sl=sl):
                    for kc in range(2):
                        r = e.matmul(ps[:], lhsT=wk[:, kc, hl * 128:(hl + 1) * 128], rhs=kvn[sl][:, :, kc, :],
                                     start=(kc == 0), stop=(kc == 1))
                    return r
                P.op("pe", mm, reads=[("kvn", sl), "wk"], writes=[pk])
                evac_copy(KT[:, hl, g0:g0 + QT], ps[:], [pk], [("KT", hl, tt)])
            steps.append(kstep)
        for sub in range(4):
            def vstep(sub=sub):
                ps, pk = sbank()

                def mm(e, ps=ps, sub=sub, sl=sl):
                    for kc in range(2):
                        r = e.matmul(ps[:], lhsT=kvn[sl][:, sub // 2, kc, (sub % 2) * 128:(sub % 2 + 1) * 128], rhs=wv[:, kc, :],
                                     start=(kc == 0), stop=(kc == 1))
                    return r
                P.op("pe", mm, reads=[("kvn", sl), "wv"], writes=[pk])
                evac_copy(V[:, tt * 4 + sub, :], ps[:], [pk], [("V", tt * 4 + sub)])
            steps.append(vstep)
        dve(lambda e: e.tensor_scalar(out=ang[:], in0=posi[:], scalar1=rcol[:, 0:1], scalar2=None, op0=ALU.mult), ["posi", "rcol"], ["ang"])
        dve(lambda e: e.tensor_scalar(out=posi[:], in0=ang[:], scalar1=1.0 / (2 * PI), scalar2=0.5, op0=ALU.mult, op1=ALU.add), ["ang"], ["posi"])
        dve(lambda e: e.scalar_tensor_tensor(out=rr[:], in0=posi[:], scalar=-C1, in1=ang[:], op0=ALU.mult, op1=ALU.add), ["posi", "ang"], ["rr"])
        dve(lambda e: e.scalar_tensor_tensor(out=rr[:], in0=posi[:], scalar=-C2, in1=rr[:], op0=ALU.mult, op1=ALU.add), ["posi", "rr"], ["rr"])
        dve(lambda e: e.tensor_scalar(out=kf[:], in0=rr[:], scalar1=-PI, scalar2=2 * PI, op0=ALU.is_lt, op1=ALU.mult), ["rr"], ["kf"])
        dve(lambda e: e.tensor_tensor(out=rr[:], in0=rr[:], in1=kf[:], op=ALU.add), ["rr", "kf"], ["rr"])
        dve(lambda e: e.tensor_scalar(out=ang[:], in0=rr[:], scalar1=PI / 2, scalar2=None, op0=ALU.add), ["rr"], ["ang"])
        dve(lambda e: e.tensor_scalar(out=kf[:], in0=ang[:], scalar1=PI, scalar2=-2 * PI, op0=ALU.is_gt, op1=ALU.mult), ["ang"], ["kf"])
        dve(lambda e: e.tensor_tensor(out=ang[:], in0=ang[:], in1=kf[:], op=ALU.add), ["ang", "kf"], ["ang"])

        def sins():
            P.op("act", lambda e: e.activation(out=S0[:], in_=rr[0:64, :], func=AF.Sin, scale=rcol[0:64, 1:2]),
                 reads=["rr", "rcol"], writes=["S0"])
            P.op("act", lambda e: e.activation(out=CS[sl][64:128, :], in_=rr[64:128, :], func=AF.Sin, scale=rcol[64:128, 1:2]),
                 reads=["rr", "rcol"], writes=[("St", sl)])
            P.op("act", lambda e: e.activation(out=CS[sl][0:64, :], in_=ang[0:64, :], func=AF.Sin), reads=["ang"], writes=[("Ct", sl)])
        nA = len(steps)
        steps.append(sins)
        dve(lambda e: e.tensor_tensor(out=v2(kf[0:64, :]), in0=kk[sl][:, :, 0, :], in1=v2(Ct[sl]), op=ALU.mult), [("kk", sl), ("Ct", sl)], ["kf"])
        dve(lambda e: e.tensor_tensor(out=v2(rr[0:64, :]), in0=kk[sl][:, :, 1, :], in1=v2(S0[:]), op=ALU.mult), [("kk", sl), "S0"], ["rr"])
        dve(lambda e: e.tensor_tensor(out=KR[0:64, g0:g0 + QT], in0=kf[0:64, :], in1=rr[0:64, :], op=ALU.add), ["kf", "rr"], [("KR", tt)])
        steps.append(lambda: P.dma("sp", KR[64:128, g0:g0 + QT], KR[0:64, g0:g0 + QT], reads=[("KR", tt)], writes=[("KR2", tt)], sem=("krd", sl)))
        return steps[:nA], steps[nA:nA + 1], steps[nA + 1:]

    pendA, pendB, pendC = [], [], []

    def drain(lst, n):
        for _ in range(min(n, len(lst))):
            lst.pop(0)()

    Qn = [P.sbuf("Qn%d" % i, [128, QT], BF16) for i in range(2)]
    Qr = [P.sbuf("Qr%d" % i, [128, QT], BF16) for i in range(2)]
    NP = 6
    pT = [P.sbuf("pT%d" % i, [128, QT], BF16) for i in range(NP)]
    rden = t1f
    ot1 = P.sbuf("ot", [128, HPC, QT], BF16)
    ot = [ot1, ot1]
    psm = [P.sbuf("psm%d" % i, [128, QT], BF16) for i in range(4)]
    psi = [0]
    pti = [0]
    WQH = 320

    def qproj(it):
        qi, hl = it // HPC, it % HPC
        qs, s2 = qi % 2, it % 2
        ps, pk = sbank()

        def mm(e, ps=ps, hl=hl, qs=qs):
            for kc in range(3):
                r = e.matmul(ps[:], lhsT=wq[:, kc, hl * WQH:hl * WQH + 128], rhs=qn[qs][:, :, kc, :],
                             start=(kc == 0), stop=(kc == 2))
            return r
        P.op("pe", mm, reads=[("qn", qs), "wq"], writes=[pk])
        P.op("act", lambda e, ps=ps, s2=s2: e.activation(out=Qn[s2][:], in_=ps[:], func=AF.Copy),
             reads=[pk], writes=[("Qn", s2)])
        ps, pk = sbank()
        off = hl * WQH + 128

        def mm(e, ps=ps, off=off, qs=qs):
            for kc in range(3):
                r = e.matmul(ps[:], lhsT=wq[:, kc, off:off + 128], rhs=qn[qs][:, :, kc, :],
                             start=(kc == 0), stop=(kc == 2))
            return r
        P.op("pe", mm, reads=[("qn", qs), "wq"], writes=[pk])
        P.op("dve", lambda e, ps=ps, qs=qs, s2=s2: e.tensor_tensor(out=Qr[s2][:], in0=ps[:], in1=CS[qs][:], op=ALU.mult),
             reads=[pk, ("Ct", qs), ("St", qs)], writes=[("Qr", s2)])

    def attn(it):
        qi, hl = it // HPC, it % HPC
        qs, s2 = qi % 2, it % 2
        nk = 4 * (qi + 1)
        ob, db = OB[s2], DB[s2]
        LA = 4
        stiles = {}
        den_at = {}

        def emit_S(kj):
            m = kj - 4 * qi
            lo = 128 * m if m > 0 else 0
            ps, pk = sbank()

            def mm(e, ps=ps, kj=kj, lo=lo, hl=hl, s2=s2):
                e.matmul(ps[:, lo:QT], lhsT=KT[:, hl, kj * 128:(kj + 1) * 128], rhs=Qn[s2][:, lo:QT],
                         start=True, stop=False)
                return e.matmul(ps[:, lo:QT], lhsT=KR[:, kj * 128:(kj + 1) * 128], rhs=Qr[s2][:, lo:QT],
                                start=False, stop=True)
            P.op("pe", mm, reads=[("KT", hl, kj // 4), ("KR", kj // 4), ("KR2", kj // 4), ("Qn", s2), ("Qr", s2)], writes=[pk])
            pi_ = pti[0] % NP
            pti[0] += 1
            P.op("act", lambda e, ps=ps, lo=lo, pi_=pi_: e.activation(out=pT[pi_][:, lo:QT], in_=ps[:, lo:QT], func=AF.Exp, scale=SCALE),
                 reads=[pk], writes=[("pT", pi_)])
            if m >= 0:
                P.op("dve", lambda e, lo=lo, pi_=pi_: e.tensor_tensor(out=pT[pi_][:, lo:lo + 128], in0=pT[pi_][:, lo:lo + 128], in1=tri[:], op=ALU.mult),
                     reads=[("pT", pi_), "tri"], writes=[("pT", pi_)])
            sm = None
            if m < 0 and kj % 2 == 1:
                nfull = 4 * qi
                sp_ = (kj // 2) % 4
                pj = stiles[kj - 1][0]
                P.op("dve", lambda e, sp_=sp_, pj=pj, pi_=pi_: e.tensor_tensor(out=psm[sp_][:], in0=pT[pj][:], in1=pT[pi_][:], op=ALU.add),
                     reads=[("pT", pj), ("pT", pi_)], writes=[("psm", sp_)])
                if kj % 4 == 3:
                    P.op("dve", lambda e, sp_=sp_: e.tensor_tensor(out=psm[sp_][:], in0=psm[sp_ - 1][:], in1=psm[sp_][:], op=ALU.add),
                         reads=[("psm", sp_ - 1), ("psm", sp_)], writes=[("psm", sp_)])
                    if kj % 8 == 7:
                        P.op("dve", lambda e: e.tensor_tensor(out=psm[3][:], in0=psm[1][:], in1=psm[3][:], op=ALU.add),
                             reads=[("psm", 1), ("psm", 3)], writes=[("psm", 3)])
                        den_at[min(kj + 2, nfull - 1)] = (3, kj == 7)
                    elif kj == nfull - 1:
                        den_at[nfull - 1] = (1, kj == 3)
            stiles[kj] = (pi_, lo, sm)

        def emit_PV(kj):
            pi_, lo, sm = stiles.pop(kj)
            m = kj - 4 * qi
            sm, first = den_at.pop(kj, (None, False))

            def mm(e, kj=kj, lo=lo, pi_=pi_, hl=hl, ob=ob, db=db, nk=nk, sm=sm, m=m, first=first):
                r = e.matmul(pst[ob][:, lo:QT], lhsT=V[:, kj, hl * 128:(hl + 1) * 128], rhs=pT[pi_][:, lo:QT],
                             start=(kj == 0), stop=(kj == nk - 1))
                if sm is not None:
                    r = e.matmul(pst[db][:], lhsT=ones[:], rhs=psm[sm][:], start=first, stop=False)
                if m >= 0:
                    r = e.matmul(pst[db][:, lo:QT], lhsT=ones[:], rhs=pT[pi_][:, lo:QT],
                                 start=(kj == 0), stop=(kj == nk - 1))
                return r
            rd = [("pT", pi_), ("V", kj), "ones"] + ([("psm", sm)] if sm is not None else [])
            P.op("pe", mm, reads=rd, writes=[pskey[ob], pskey[db]])

        if hl == 1:
            drain(pendA, len(pendA))
        if hl == 2:
            drain(pendB, len(pendB))
        if hl == 3:
            drain(pendC, len(pendC))
        perA = -(-len(pendA) // nk) if hl == 0 else 0
        perC = -(-len(pendC) // nk) if hl == 2 else 0
        for kj in range(nk + LA):
            if kj < nk:
                emit_S(kj)
                drain(pendA, perA)
                drain(pendC, perC)
            if kj - LA >= 0:
                emit_PV(kj - LA)
            if kj == 1 and it + 1 < NQ * HPC:
                qproj(it + 1)
        P.op("act", lambda e, db=db: e.activation(out=rden[:], in_=pst[db][:], func=AF.Ln), reads=[pskey[db]], writes=["t1"])
        P.op("act", lambda e: e.activation(out=rden[:], in_=rden[:], func=AF.Exp, scale=-1.0), reads=["t1"], writes=["t1"])
        P.op("dve", lambda e, ob=ob, qs=qs, hl=hl: e.tensor_tensor(out=ot[qs][:, hl, :], in0=pst[ob][:], in1=rden[:], op=ALU.mult),
             reads=[pskey[ob], "t1"], writes=[("ot", hl)])

    for lst in prep_steps(0):
        for st in lst:
            st()
    qproj(0)
    for qi in range(NQ):
        if qi + 1 < NQ:
            a_, b_, c_ = prep_steps(qi + 1)
            pendA.extend(a_), pendB.extend(b_), pendC.extend(c_)
        for hl in range(HPC):
            attn(qi * HPC + hl)
        qs = qi % 2
        P.dma("pool", oT[qi * 128:(qi + 1) * 128, :], ot[qs][:].rearrange("p h c -> p (h c)"),
              reads=[("ot", hl) for hl in range(HPC)], writes=[("oT", qi)], sem="ost", final=True)
        if qi % 2 == 1 and io.get("ag2") is not None:
            io["ag2"](P, qi // 2)
        if io.get("prefetch") is not None:
            io["prefetch"](P, qi)


def phase_C(nc, P, io):
    TT, NT = 512, TOK // 512
    x1T, oTo, outT = io["x1T"], io["oTo"], io["outT"]
    banks = Banks(P)
    ones = P.sbuf("ones", [128, 128], BF16)
    P.op("pool", lambda e: e.memset(ones[:], 1.0), writes=["ones"])
    vec = P.sbuf("vec", [128, NV], F32)
    P.dma("sp", vec[:], io["vec"], writes=["vec"], sem="vec")
    wz = P.sbuf("wz", [128, 8, 2048], BF16)
    wo = P.sbuf("wo", [128, 16, 1024], BF16)
    wq_, wsrc, wosrc = ("act", io["wz_bf"], io["wo_bf"]) if io.get("wz_bf") is not None else ("pool", io["wz"], io["mla_w_out"])

    def wload(first):
        for q4 in ([0] if first else [1, 2, 3]):
            P.dma(wq_, wz[:, :, q4 * 512:(q4 + 1) * 512], wsrc[:, :, q4 * 512:(q4 + 1) * 512],
                  reads=([] if first else ["xt0"]), writes=[("wz", q4)], sem=("wz", q4))
        if not first:
            for h2 in range(2):
                P.dma(wq_, wo[:, h2 * 8:(h2 + 1) * 8, :].rearrange("p (a b) f -> p a (b f)", b=2),
                      wosrc[:, h2 * 4:(h2 + 1) * 4, :], reads=["xt0"], writes=[("wo", h2)], sem=("wo", h2))
    wload(True)
    xt = [P.sbuf("xt%d" % i, [128, 2, 8, 256], F32) for i in range(3)]
    og = [P.sbuf("og%d" % i, [128, 16, TT], BF16) for i in range(2)]
    sq = P.sbuf("sq", [128, 2, 8, 256], BF16)
    rt = P.sbuf("rt", [128, TT], F32)
    inv = rt
    hb = [P.sbuf("h%d" % i, [128, 8, TT], BF16) for i in range(2)]
    szt = [P.sbuf("szt%d" % i, [128, TT], F32) for i in range(2)]
    yb = [P.sbuf("y%d" % i, [128, 16, TT], BF16) for i in range(2)]

    def v2(ap):
        return ap.rearrange("p (a c) -> p a c", a=2)

    def stats(xs, xkey):
        P.op("act", lambda e, xs=xs: e.activation(out=sq[:], in_=xs[:], func=AF.Square), reads=[xkey], writes=["sq"])
        ps, pk = banks.next()

        def mm(e, ps=ps):
            for k in range(8):
                r = e.matmul(ps[:], lhsT=ones[:], rhs=sq[:, :, k, :], start=(k == 0), stop=(k == 7))
            return r
        P.op("pe", mm, reads=["sq", "ones"], writes=[pk])
        P.op("act", lambda e, ps=ps: e.activation(out=rt[:], in_=ps[:], func=AF.Ln, bias=EPS, scale=1.0 / D),
             reads=[pk], writes=["rt"])
        P.op("act", lambda e: e.activation(out=inv[:], in_=rt[:], func=AF.Exp, scale=-0.5), reads=["rt"], writes=["rt"])

    def C0(t):
        xs, xkey = xt[t % 3], "xt%d" % (t % 3)
        os_, okey = og[t % 2], "og%d" % (t % 2)
        h = hb[t % 2]
        P.dma("sp", xs[:], x1T[:, 2 * t:2 * t + 2, :, :], reads=["x1T"], writes=[xkey], sem=xkey)
        for rr in range(4):
            io["oTo_dma"](P, t, rr, os_[:, rr * 4:(rr + 1) * 4, :].rearrange("p h c -> p (h c)"), (okey, rr))
        stats(xs, xkey)
        for kc in range(8):
            P.op("dve", lambda e, kc=kc, xs=xs, h=h: e.scalar_tensor_tensor(
                out=v2(h[:, kc, :]), in0=xs[:, :, kc, :], scalar=vec[:, V_G1 + kc:V_G1 + kc + 1], in1=v2(inv[:]),
                op0=ALU.mult, op1=ALU.mult), reads=[xkey, "rt", "vec"], writes=[("h", t % 2, kc)])

    def C1(t):
        os_, okey = og[t % 2], "og%d" % (t % 2)
        h, y = hb[t % 2], yb[t % 2]
        for f in range(16):
            ps, pk = banks.next()

            def mm(e, ps=ps, f=f, h=h):
                for kc in range(8):
                    r = e.matmul(ps[:], lhsT=wz[:, kc, f * 128:(f + 1) * 128], rhs=h[:, kc, :], start=(kc == 0), stop=(kc == 7))
                return r
            P.op("pe", mm, reads=[("h", t % 2, kc) for kc in range(8)] + [("wz", f // 4)], writes=[pk])
            zs = f % 2
            P.op("act", lambda e, ps=ps, zs=zs: e.activation(out=szt[zs][:], in_=ps[:], func=AF.Silu), reads=[pk], writes=[("szt", zs)])
            P.op("dve", lambda e, zs=zs, f=f, os_=os_, y=y: e.tensor_tensor(out=y[:, f, :], in0=szt[zs][:], in1=os_[:, f, :], op=ALU.mult),
                 reads=[("szt", zs), (okey, f // 4)], writes=[("y", t % 2, f)])

    def C2(t):
        xs, xkey = xt[t % 3], "xt%d" % (t % 3)
        y = yb[t % 2]
        for dc in range(8):
            ps, pk = banks.next()

            def mm(e, ps=ps, dc=dc, y=y):
                for kc in range(16):
                    r = e.matmul(ps[:], lhsT=wo[:, kc, dc * 128:(dc + 1) * 128], rhs=y[:, kc, :], start=(kc == 0), stop=(kc == 15))
                return r
            P.op("pe", mm, reads=[("y", t % 2, kc) for kc in range(16)] + [("wo", 0), ("wo", 1)], writes=[pk])
            P.op("dve", lambda e, ps=ps, dc=dc, xs=xs: e.tensor_tensor(out=xs[:, :, dc, :], in0=v2(ps[:]), in1=xs[:, :, dc, :], op=ALU.add),
                 reads=[pk, xkey], writes=[xkey])

    def C3(t):
        xs, xkey = xt[t % 3], "xt%d" % (t % 3)
        stats(xs, xkey)
        for kc in range(8):
            P.op("dve", lambda e, kc=kc, xs=xs: e.scalar_tensor_tensor(
                out=xs[:, :, kc, :], in0=xs[:, :, kc, :], scalar=vec[:, V_GF + kc:V_GF + kc + 1], in1=v2(inv[:]),
                op0=ALU.mult, op1=ALU.mult), reads=[xkey, "rt", "vec"], writes=[xkey])
        P.dma("pool", outT[:, 2 * t:2 * t + 2, :, :], xs[:], reads=[xkey],
              writes=[("outT", t)], sem="outst", final=True)

    C0(0)
    wload(False)
    C1(0)
    for t in range(NT):
        if t + 1 < NT:
            C0(t + 1)
        C2(t)
        if t + 1 < NT:
            C1(t + 1)
        C3(t)


def _dram(nc, name, shape, dt, kind):
    return nc.dram_tensor(name, list(shape), dt, kind=kind).ap()


def build_A():
    nc = bass.Bass("TRN2", target_bir_lowering=False)
    io = {
        "xT": _dram(nc, "xT", [128, 8, 8, 256], F32, "ExternalInput"),
        "xh": _dram(nc, "xh", [128, 8, HALO], F32, "ExternalInput"),
        "vec": _dram(nc, "vec", [128, NV], F32, "ExternalInput"),
        "rc": _dram(nc, "rc", [128, 64], F32, "ExternalInput"),
        "pool_w_in": _dram(nc, "pool_w_in", [128, 4, 8, 1024], F32, "ExternalInput"),
        "pool_w_group": _dram(nc, "pool_w_group", [128, 4, 4, 512], F32, "ExternalInput"),
        "pool_w_out": _dram(nc, "pool_w_out", [128, 8, 2048], F32, "ExternalInput"),
        "wl": _dram(nc, "wl", [128, 8, LATR], F32, "ExternalInput"),
        "x1T": _dram(nc, "x1T", [128, 8, 8, 256], F32, "ExternalOutput"),
        "lat": _dram(nc, "lat", [8 * 128, LATS * 256], BF16, "ExternalOutput"),
    }
    with ExitStack() as es:
        P = Prog(nc, es)
        phase_A(nc, P, io)
        P.emit()
    return nc


def build_B():
    nc = bass.Bass("TRN2", target_bir_lowering=False)
    io = {
        "latA": _dram(nc, "latA", [4 * 8 * 128, LATS * 256], BF16, "ExternalInput"),
        "posr": _dram(nc, "posr", [128, S], I32, "ExternalInput"),
        "rcol": _dram(nc, "rcol", [128, 2], F32, "ExternalInput"),
        "tri": _dram(nc, "tri", [128, 128], F32, "ExternalInput"),
        "wq": _dram(nc, "wq", [128, 3, HPC * 320], F32, "ExternalInput"),
        "wk": _dram(nc, "wk", [128, 2, HPC * 128], F32, "ExternalInput"),
        "wv": _dram(nc, "wv", [128, 2, HPC * 128], F32, "ExternalInput"),
        "oT": _dram(nc, "oT", [16 * 128, HPC * 512], BF16, "ExternalOutput"),
        "cs": nc.dram_tensor("cs", [16, 64, 2 * 512], F32).ap(),
    }
    with ExitStack() as es:
        P = Prog(nc, es)
        phase_B(nc, P, io)
        P.emit()
    return nc


def build_C():
    nc = bass.Bass("TRN2", target_bir_lowering=False)
    oTo = _dram(nc, "oTo", [4 * 4 * 128, 2048], BF16, "ExternalInput")
    io = {
        "x1T": _dram(nc, "x1T", [128, 8, 8, 256], F32, "ExternalInput"),
        "oTo": oTo,
        "oTo_dma": lambda P, t, rr, dst, key: P.dma("sp", dst, oTo[(t * 4 + rr) * 128:(t * 4 + rr + 1) * 128, :],
                                                  reads=["oTo"], writes=[key], sem=key),
        "vec": _dram(nc, "vec", [128, NV], F32, "ExternalInput"),
        "wz": _dram(nc, "wz", [128, 8, 2048], F32, "ExternalInput"),
        "mla_w_out": _dram(nc, "mla_w_out", [128, 8, 2048], F32, "ExternalInput"),
        "outT": _dram(nc, "outT", [128, 8, 8, 256], F32, "ExternalOutput"),
    }
    with ExitStack() as es:
        P = Prog(nc, es)
        phase_C(nc, P, io)
        P.emit()
    return nc


def build_fused():
    nc = bass.Bass("TRN2", target_bir_lowering=False)
    ein = lambda name, shape, dt: _dram(nc, name, shape, dt, "ExternalInput")
    x1s = nc.dram_tensor("x1s", [128, 8, 8, 256], F32).ap()
    lat_own = nc.dram_tensor("lat_own", [8 * 128, LATS * 256], BF16).ap()
    latA = nc.dram_tensor("latA", [4 * 8 * 128, LATS * 256], BF16).ap()
    o_own = nc.dram_tensor("o_own", [16 * 128, HPC * 512], BF16).ap()
    oA = nc.dram_tensor("oA", [4 * 16 * 128, HPC * 512], BF16).ap()
    vec = ein("vec", [128, NV], F32)
    groups = [[0, 1, 2, 3], [4, 5, 6, 7]]
    ioA = {
        "xT": ein("xT", [128, 8, 8, 256], F32), "xh": ein("xh", [128, 8, HALO], F32), "vec": vec,
        "rc": ein("rc", [128, 64], F32), "pool_w_in": ein("pool_w_in", [128, 4, 8, 1024], F32),
        "pool_w_group": ein("pool_w_group", [128, 4, 4, 512], F32), "pool_w_out": ein("pool_w_out", [128, 8, 2048], F32),
        "wl": ein("wl", [128, 8, LATR], F32), "x1T": x1s, "lat": lat_own,
    }
    ioB = {
        "latA": latA, "posr": ein("posr", [128, S], I32), "rcol": ein("rcol", [128, 2], F32), "tri": ein("tri", [128, 128], F32),
        "wq": ein("wq", [128, 3, HPC * 320], F32), "wk": ein("wk", [128, 2, HPC * 128], F32),
        "wv": ein("wv", [128, 2, HPC * 128], F32), "oT": o_own, "cs": nc.dram_tensor("cs", [16, 64, 2 * 512], F32).ap(),
    }

    def ag1(P, k):
        P.custom_dma("pool", lambda e: e.collective_compute(
            "AllGather", ALU.bypass, replica_groups=groups,
            ins=[lat_own[k * 256:(k + 1) * 256, :]], outs=[latA[k * 1024:(k + 1) * 1024, :]]),
            reads=[("lat", 2 * k), ("lat", 2 * k + 1)], writes=[("latA", k)], sem=("ag1", k), inc=1)

    def ag2(P, m):
        P.custom_dma("pool", lambda e: e.collective_compute(
            "AllGather", ALU.bypass, replica_groups=groups,
            ins=[o_own[m * 256:(m + 1) * 256, :]], outs=[oA[m * 1024:(m + 1) * 1024, :]]),
            reads=[("oT", 2 * m), ("oT", 2 * m + 1)], writes=[("oTo", m)], sem=("ag2", m), inc=1)

    ioA["ag1"] = ag1
    ioB["ag2"] = ag2
    wz_bf = nc.dram_tensor("wz_bf", [128, 8, 2048], BF16).ap()
    wo_bf = nc.dram_tensor("wo_bf", [128, 8, 2048], BF16).ap()

    def prefetch(P, i):
        name, dst = (("wz", wz_bf), ("mla_w_out", wo_bf))[i // 8]
        k = i % 8
        P.dma("pool", dst[:, k, :], wsrc[name][:, k, :], writes=[("pf", i)], sem=("pf", i % 2))
    ioB["prefetch"] = prefetch

    ag2_tok = {}

    def oTo_dma(P, t, rr, dst, key):
        def emit(e, sem):
            core = e.partition_id()
            for k in range(8):
                m = (k % 4) * 2 + t // 2
                row = m * 1024 + rr * 256 + (t % 2) * 128
                with e.If(core == k):
                    e.wait_ge(ag2_tok[m][0], ag2_tok[m][1])
                    e.dma_start(out=dst, in_=oA[row:row + 128, :]).then_inc(sem, 16)
        P.custom_dma("sp", emit, writes=[key], sem=key, self_inc=True)

    wsrc = {"wz": ein("wz", [128, 8, 2048], F32), "mla_w_out": ein("mla_w_out", [128, 8, 2048], F32)}
    ioC = {
        "x1T": x1s, "oTo": oA, "oTo_dma": oTo_dma, "vec": vec, "wz": wsrc["wz"],
        "mla_w_out": wsrc["mla_w_out"], "wz_bf": wz_bf, "wo_bf": wo_bf,
        "outT": _dram(nc, "outT", [128, 8, 8, 256], F32, "ExternalOutput"),
    }
    with ExitStack() as sem_es:
        with ExitStack() as es:
            PA = Prog(nc, es, sem_es)
            phase_A(nc, PA, ioA)
            PA.barrier(skip="ag1")
            PA.emit()
        with ExitStack() as es:
            PB = Prog(nc, es, sem_es)
            for k in range(4):
                PB.extern(PA.sems[("dma", ("ag1", k))], PA.dma_cnt[("ag1", k)], [("latA", k)])
            phase_B(nc, PB, ioB)
            PB.barrier(skip="ag2")
            PB.emit()
        with ExitStack() as es:
            PC = Prog(nc, es, sem_es)
            for m in range(8):
                ag2_tok[m] = (PB.sems[("dma", ("ag2", m))], PB.dma_cnt[("ag2", m)])
                PC.extern(ag2_tok[m][0], ag2_tok[m][1], [("oTo", m)])
            phase_C(nc, PC, ioC)
            PC.barrier()
            PC.emit()
    return nc


def _cols(v):
    return np.ascontiguousarray(np.asarray(v, np.float32).reshape(-1, 128).T)


def _pm(w):
    nk = w.shape[0] // 128
    return np.ascontiguousarray(w.reshape(nk, 128, w.shape[1]).transpose(1, 0, 2))


def host_inputs(inp):
    f = lambda k: np.asarray(inp[k], np.float32)
    x = f("x")
    vec = np.concatenate([_cols(f("pool_norm")[0]), _cols(f("mla_norm")[0]), _cols(f("pool_scale")[0]),
                          _cols(f("mla_q_norm")[0]), _cols(f("mla_kv_norm")[0]), _cols(f("final_norm"))], axis=1)
    assert vec.shape == (128, NV)
    perm = np.concatenate([np.concatenate([np.arange(g * 512, (g + 1) * 512), 2048 + np.arange(g * 512, (g + 1) * 512)])
                           for g in range(4)])
    w_in_p = _pm(f("pool_w_in")[0][:, perm]).reshape(128, 8, 4, 1024).transpose(0, 2, 1, 3)
    w_in_p = np.ascontiguousarray(w_in_p)
    wg_p = np.ascontiguousarray(f("pool_w_group")[0].reshape(4, 4, 128, 512).transpose(2, 0, 1, 3))
    wo0_p = _pm(f("pool_w_out")[0]).reshape(128, 8, 2048)
    wo1_p = _pm(f("mla_w_out")[0]).reshape(128, 8, 2048)
    mw = f("mla_w_in")[0]
    wl_p = _pm(np.concatenate([mw[:, :704], mw[:, 672:704], mw[:, 640:672], mw[:, 640:704]], axis=1))
    wz_p = _pm(mw[:, 704:])
    wqb, wkvb = f("mla_w_q_b")[0], f("mla_w_kv_b")[0]
    invf = (np.float32(1.0) / (np.float32(10000.0) ** (np.arange(0, 64, 2, dtype=np.float32) / np.float32(64)))).astype(np.float32)
    rcol = np.stack([np.tile(invf, 4), np.tile(np.concatenate([-np.ones(32, np.float32), np.ones(32, np.float32)]), 2)], axis=1)
    tri = (np.arange(128)[None, :] >= np.arange(128)[:, None]).astype(np.float32)
    A, Bm, C = [], [], []
    for c in range(8):
        b, r = c // 4, c % 4
        t0 = r * TOK
        xT = np.ascontiguousarray(x[b, t0:t0 + TOK].reshape(8, 256, 8, 128).transpose(3, 0, 2, 1))
        xh = np.zeros((HALO, D), np.float32)
        if t0 > 0:
            xh[:] = x[b, t0 - HALO:t0]
        xh = np.ascontiguousarray(xh.reshape(HALO, 8, 128).transpose(2, 1, 0))
        rc = np.zeros((128, 64), np.float32)
        for g, w in enumerate((2, 4, 8, 16)):
            rc[:, g * 16:(g + 1) * 16] = 1.0 / np.minimum(t0 + np.arange(16) + 1, w).astype(np.float32)
        A.append({"xT": xT, "xh": xh, "vec": vec, "rc": rc, "pool_w_in": w_in_p,
                  "pool_w_group": wg_p, "pool_w_out": wo0_p, "wl": wl_p})
        hs = [4 * r + hl for hl in range(4)]
        wq = np.concatenate([np.concatenate([wqb[:, h * 192:h * 192 + 192], wqb[:, h * 192 + 160:h * 192 + 192],
                                             wqb[:, h * 192 + 128:h * 192 + 160], wqb[:, h * 192 + 128:h * 192 + 192]], axis=1)
                             for h in hs], axis=1)
        wk = np.concatenate([wkvb[:, h * 256:h * 256 + 128] for h in hs], axis=1)
        wv = np.concatenate([wkvb[:, h * 256 + 128:h * 256 + 256] for h in hs], axis=1)
        posr = np.ascontiguousarray(np.broadcast_to(np.asarray(inp["positions"])[b].astype(np.int32)[None, :], (128, S)))
        Bm.append({"posr": posr, "rcol": rcol, "tri": tri, "wq": _pm(wq), "wk": _pm(wk), "wv": _pm(wv)})
        C.append({"vec": vec, "wz": wz_p, "mla_w_out": wo1_p})
    return A, Bm, C


def _assemble(res):
    out = np.empty((NB, S, D), np.float32)
    for c in range(8):
        b, r = c // 4, c % 4
        o = res[c]["outT"]
        out[b, r * TOK:(r + 1) * TOK, :] = o.transpose(1, 3, 2, 0).reshape(TOK, D)
    return out


_NC = {}


def _get(name, fn):
    if name not in _NC:
        _NC[name] = fn()
    return _NC[name]


FUSED = True


def kernel(**inputs):
    A, Bm, C = host_inputs(inputs)
    cores = list(range(8))
    if FUSED:
        maps = []
        for c in cores:
            m = {}
            m.update(A[c]); m.update(Bm[c]); m.update(C[c])
            maps.append(m)
        res = run_bass_kernel_spmd(_get("F", build_fused), maps, core_ids=cores).results
        return _assemble(res)
    ra = run_bass_kernel_spmd(_get("A", build_A), A, core_ids=cores).results
    for c in cores:
        b = c // 4
        Bm[c]["latA"] = np.concatenate([ra[b * 4 + r]["lat"][k * 256:(k + 1) * 256] for k in range(4) for r in range(4)], axis=0)
    rb = run_bass_kernel_spmd(_get("B", build_B), Bm, core_ids=cores).results
    for c in cores:
        b, r = c // 4, c % 4
        C[c]["x1T"] = ra[c]["x1T"]
        C[c]["oTo"] = np.ascontiguousarray(np.concatenate(
            [rb[b * 4 + rr]["oT"][(r * 4 + t) * 128:(r * 4 + t + 1) * 128, :] for t in range(4) for rr in range(4)], axis=0))
    rc = run_bass_kernel_spmd(_get("C", build_C), C, core_ids=cores).results
    return _assemble(rc)
```
